# Optimizing a Trainium2 kernel written in Bass

```python
import math
import jax, jax.numpy as jnp
from jax import lax
import numpy as np

D_MODEL = 1024
BATCH = 16
SEQ = 2048
DEPTH = 2

MEM_LEN = 256
D_MIX = D_MODEL
GM_WIDTH = D_MIX // 2
CHUNK = 128
GM_HEAD_DIM = 128
GM_HEADS = GM_WIDTH // GM_HEAD_DIM
SSM_WIDTH = D_MIX // 4
SSM_GROUP = 16
SSM_GROUPS = SSM_WIDTH // SSM_GROUP
SSM_STATE = 64
XA_WIDTH = D_MIX - GM_WIDTH - SSM_WIDTH
XA_HEADS = 4
XA_HEAD_DIM = XA_WIDTH // XA_HEADS
IN_WIDTH = 3 * GM_WIDTH + 2 * SSM_WIDTH + 2 * XA_WIDTH
SPLITS = (GM_WIDTH, 2 * GM_WIDTH, 3 * GM_WIDTH,
          3 * GM_WIDTH + SSM_WIDTH, 3 * GM_WIDTH + 2 * SSM_WIDTH,
          3 * GM_WIDTH + 2 * SSM_WIDTH + XA_WIDTH)
DN_ALPHA = (2 * DEPTH) ** 0.25
DN_BETA = (8 * DEPTH) ** -0.25
LN_EPS = 1e-5
DT_MIN = 0.001
DT_MAX = 0.1

kernel_name = "hybrid_gmlp_s5_memxattn_deepnorm"


def layer_norm(x, g, b):
    xf = x.astype(jnp.float32)
    mu = jnp.mean(xf, axis=-1, keepdims=True)
    xc = xf - mu
    var = jnp.mean(xc * xc, axis=-1, keepdims=True)
    return (xc * lax.rsqrt(var + LN_EPS) * g.astype(jnp.float32) + b.astype(jnp.float32)).astype(x.dtype)


def spatial_gating(u, v, w_s, b_s, ln_g, ln_b, causal):
    Bsz, L, _ = v.shape
    nc = L // CHUNK
    vh = layer_norm(v.reshape(Bsz, L, GM_HEADS, GM_HEAD_DIM), ln_g, ln_b)
    w = jnp.where(causal[None], w_s, 0)
    vc = vh.reshape(Bsz, nc, CHUNK, GM_HEADS, GM_HEAD_DIM)
    mixed = jnp.einsum('hts,bcshd->bcthd', w, vc) + b_s.T[None, None, :, :, None]
    return u * mixed.reshape(Bsz, L, GM_WIDTH)


def _diag_combine(c1, c2):
    a1, b1 = c1
    a2, b2 = c2
    return a1 * a2, a2 * b1 + b2


def s5_branch(xs, lam_re, lam_im, log_step, b_re, b_im, c_re, c_im, d_skip, glu_w, glu_b):
    f32 = jnp.float32
    Bsz, L, W = xs.shape
    xg = xs.reshape(Bsz, L, SSM_GROUPS, SSM_GROUP).astype(f32)
    lam = lax.complex(lam_re.astype(f32), lam_im.astype(f32))
    step = jnp.exp(log_step.astype(f32))[:, None]
    lam_bar = jnp.exp(lam * step)
    b_mat = lax.complex(b_re.astype(f32), b_im.astype(f32))
    b_bar = ((lam_bar - 1.0) / lam)[:, :, None] * b_mat
    c_mat = lax.complex(c_re.astype(f32), c_im.astype(f32))
    bu = jnp.einsum('gpc,blgc->blgp', b_bar, xg.astype(jnp.complex64))
    decay = jnp.broadcast_to(lam_bar, bu.shape)
    _, h = lax.associative_scan(_diag_combine, (decay, bu), axis=1)
    y = jnp.einsum('gcp,blgp->blgc', c_mat, h).real \
        + d_skip.astype(f32).reshape(SSM_GROUPS, SSM_GROUP) * xg
    y = jax.nn.gelu(y.reshape(Bsz, L, W)).astype(xs.dtype)
    return y * jax.nn.sigmoid(y @ glu_w + glu_b)


def memory_cross_attention(q, mem, w_k, w_v):
    Bsz, L, _ = q.shape
    M = mem.shape[1]
    qh = q.reshape(Bsz, L, XA_HEADS, XA_HEAD_DIM)
    kh = (mem @ w_k).reshape(Bsz, M, XA_HEADS, XA_HEAD_DIM)
    vh = (mem @ w_v).reshape(Bsz, M, XA_HEADS, XA_HEAD_DIM)
    s = jnp.einsum('blhd,bmhd->bhlm', qh, kh, preferred_element_type=jnp.float32)
    p = jax.nn.softmax(s * (XA_HEAD_DIM ** -0.5), axis=-1).astype(vh.dtype)
    o = jnp.einsum('bhlm,bmhd->blhd', p, vh)
    return o.reshape(Bsz, L, XA_WIDTH)


def setup_inputs(seed: int = 0) -> dict:
    key = jax.random.key(seed)
    ks = jax.random.split(key, 24)
    f32 = jnp.float32
    nrm = lambda k, shape, s: jax.random.normal(k, shape, f32) * s
    x = jax.random.normal(ks[0], (BATCH, SEQ, D_MODEL), f32)
    mem = jax.random.normal(ks[1], (BATCH, MEM_LEN, D_MODEL), f32)
    w_in = nrm(ks[2], (DEPTH, D_MODEL, IN_WIDTH), D_MODEL ** -0.5)
    gm_w_s = nrm(ks[3], (DEPTH, GM_HEADS, CHUNK, CHUNK), CHUNK ** -0.5)
    gm_b_s = 1.0 + nrm(ks[4], (DEPTH, GM_HEADS, CHUNK), 0.01)
    gm_ln_g = 1.0 + nrm(ks[5], (DEPTH, GM_HEADS, GM_HEAD_DIM), 0.01)
    gm_ln_b = nrm(ks[6], (DEPTH, GM_HEADS, GM_HEAD_DIM), 0.01)
    n = jnp.arange(SSM_STATE, dtype=f32)
    ssm_lam_re = -0.5 + nrm(ks[7], (DEPTH, SSM_GROUPS, SSM_STATE), 0.01)
    ssm_lam_im = math.pi * n + nrm(ks[8], (DEPTH, SSM_GROUPS, SSM_STATE), 0.01)
    ssm_log_step = jax.random.uniform(ks[9], (DEPTH, SSM_GROUPS), f32,
                                      math.log(DT_MIN), math.log(DT_MAX))
    bs = (2.0 * SSM_GROUP) ** -0.5
    ssm_b_re = nrm(ks[10], (DEPTH, SSM_GROUPS, SSM_STATE, SSM_GROUP), bs)
    ssm_b_im = nrm(ks[11], (DEPTH, SSM_GROUPS, SSM_STATE, SSM_GROUP), bs)
    cs = (2.0 * SSM_STATE) ** -0.5
    ssm_c_re = nrm(ks[12], (DEPTH, SSM_GROUPS, SSM_GROUP, SSM_STATE), cs)
    ssm_c_im = nrm(ks[13], (DEPTH, SSM_GROUPS, SSM_GROUP, SSM_STATE), cs)
    ssm_d = nrm(ks[14], (DEPTH, SSM_WIDTH), 1.0)
    glu_w = nrm(ks[15], (DEPTH, SSM_WIDTH, SSM_WIDTH), SSM_WIDTH ** -0.5)
    glu_b = nrm(ks[16], (DEPTH, SSM_WIDTH), 0.01)
    xa_w_k = nrm(ks[17], (DEPTH, D_MODEL, XA_WIDTH), D_MODEL ** -0.5)
    xa_w_v = nrm(ks[18], (DEPTH, D_MODEL, XA_WIDTH), D_MODEL ** -0.5)
    w_out = nrm(ks[19], (DEPTH, D_MIX, D_MODEL), DN_BETA * D_MIX ** -0.5)
    ln_g = 1.0 + nrm(ks[20], (DEPTH, D_MODEL), 0.01)
    ln_b = nrm(ks[21], (DEPTH, D_MODEL), 0.01)
    return {"x": x, "mem": mem, "w_in": w_in, "gm_w_s": gm_w_s, "gm_b_s": gm_b_s,
            "gm_ln_g": gm_ln_g, "gm_ln_b": gm_ln_b, "ssm_lam_re": ssm_lam_re,
            "ssm_lam_im": ssm_lam_im, "ssm_log_step": ssm_log_step,
            "ssm_b_re": ssm_b_re, "ssm_b_im": ssm_b_im, "ssm_c_re": ssm_c_re,
            "ssm_c_im": ssm_c_im, "ssm_d": ssm_d, "glu_w": glu_w, "glu_b": glu_b,
            "xa_w_k": xa_w_k, "xa_w_v": xa_w_v, "w_out": w_out,
            "ln_g": ln_g, "ln_b": ln_b}


def reference(x, mem, w_in, gm_w_s, gm_b_s, gm_ln_g, gm_ln_b, ssm_lam_re, ssm_lam_im,
              ssm_log_step, ssm_b_re, ssm_b_im, ssm_c_re, ssm_c_im, ssm_d, glu_w, glu_b,
              xa_w_k, xa_w_v, w_out, ln_g, ln_b):
    causal = jnp.tril(jnp.ones((CHUNK, CHUNK), dtype=bool))
    for layer in range(DEPTH):
        z = x @ w_in[layer]
        u_a, v_a, g_a, x_b, g_b, q_x, g_x = jnp.split(z, SPLITS, axis=-1)
        y_a = spatial_gating(jax.nn.gelu(u_a), jax.nn.gelu(v_a), gm_w_s[layer], gm_b_s[layer],
                             gm_ln_g[layer], gm_ln_b[layer], causal)
        y_b = s5_branch(x_b, ssm_lam_re[layer], ssm_lam_im[layer], ssm_log_step[layer],
                        ssm_b_re[layer], ssm_b_im[layer], ssm_c_re[layer], ssm_c_im[layer],
                        ssm_d[layer], glu_w[layer], glu_b[layer])
        y_x = memory_cross_attention(q_x, mem, xa_w_k[layer], xa_w_v[layer])
        y = jnp.concatenate([y_a * jax.nn.silu(g_a),
                             y_b * jax.nn.silu(g_b),
                             y_x * jax.nn.silu(g_x)], axis=-1)
        x = layer_norm(DN_ALPHA * x + y @ w_out[layer], ln_g[layer], ln_b[layer])
    return x
```

```python
import math
import contextlib
import numpy as np
import concourse.bass as bass
import concourse.mybir as mybir
from concourse.bass_utils import run_bass_kernel_spmd

F32 = mybir.dt.float32
BF16 = mybir.dt.bfloat16
I32 = mybir.dt.int32
AF = mybir.ActivationFunctionType
ALU = mybir.AluOpType

N_CORES = 8
NB = 2
SEQ = 2048
D = 1024
MEM = 256
DEPTH = 2
SEG = 1024
NSEG = SEQ // SEG
BLK = 512
NBLK = SEG // BLK
J = BLK // 8
ALPHA = (2 * DEPTH) ** 0.25
EPS = 1e-5
TWO_PI = 2.0 * math.pi


class Plan:
    ENGS = ("pe", "act", "dve", "pool", "sp")

    def __init__(self, nc):
        self.nc = nc
        self.ops = {e: [] for e in self.ENGS}
        self.count = {e: 0 for e in self.ENGS}
        self.last_write = {}
        self.readers = {}
        self.known = {e: {} for e in self.ENGS}
        self.dma_sems = {}
        self.alias = {}

    def set_alias(self, a, others):
        for o in others:
            self.alias.setdefault(a, set()).add(o)
            self.alias.setdefault(o, set()).add(a)

    def _expand(self, keys):
        out = []
        for k in keys:
            out.append(k)
            for a in self.alias.get(k, ()):
                out.append(a)
        return out

    def _add(self, waits, name, val):
        if waits.get(name, 0) < val:
            waits[name] = val

    def _add_dep(self, waits, dep):
        if dep is None:
            return
        if dep[0] == "eng":
            self._add(waits, "c_" + dep[1], dep[2])
        else:
            self._add(waits, dep[1], dep[2])

    def _deps(self, eng, reads, writes):
        waits = {}
        for k in self._expand(reads):
            self._add_dep(waits, self.last_write.get(k))
        for k in self._expand(writes):
            self._add_dep(waits, self.last_write.get(k))
            for e, idx in self.readers.get(k, {}).items():
                if e.startswith("dma:"):
                    self._add(waits, e[4:], idx)
                else:
                    self._add(waits, "c_" + e, idx)
        if eng == "pe":
            waits.pop("c_pe", None)
        out = {}
        kn = self.known[eng]
        for name, val in waits.items():
            if kn.get(name, 0) >= val:
                continue
            kn[name] = val
            out[name] = val
        return out

    def op(self, eng, fn, reads=(), writes=(), tag=None):
        waits = self._deps(eng, reads, writes)
        if tag is not None:
            self.tags = getattr(self, "tags", {})
            self.tags.setdefault(tag, []).append((eng, dict(waits), dict(self.known[eng]), {k: self.last_write.get(k) for k in self._expand(reads)}))
        self.count[eng] += 1
        idx = self.count[eng]
        self.ops[eng].append((waits, fn, [("c_" + eng, 1)]))
        for k in reads:
            self.readers.setdefault(k, {})[eng] = idx
        for k in writes:
            self.last_write[k] = ("eng", eng, idx)
            self.readers[k] = {}
        return idx

    def dma(self, fn, reads=(), writes=(), sem=None, q="sp"):
        if len(writes) > 0:
            sem = "d_" + str(writes[0])
        else:
            sem = "d_o_" + str(reads[0])
        waits = self._deps(q, reads, writes)
        self.dma_sems[sem] = self.dma_sems.get(sem, 0) + 16
        val = self.dma_sems[sem]
        self.ops[q].append((waits, fn, [(sem, 16)]))
        for k in writes:
            self.last_write[k] = ("dma", sem, val)
            self.readers[k] = {}
        for k in reads:
            self.readers.setdefault(k, {})["dma:" + sem] = val
        return val

    def emit(self):
        nc = self.nc
        names = set(["c_" + e for e in self.ENGS]) | set(self.dma_sems.keys())
        with contextlib.ExitStack() as st:
            sems = {n: st.enter_context(nc.semaphore(n)) for n in sorted(names)}
            block = st.enter_context(nc.Block())

            def replay(ename):
                def body(eng):
                    for waits, fn, incs in self.ops[ename]:
                        for n, v in waits.items():
                            eng.wait_ge(sems[n], v)
                        ins = fn(eng)
                        for n, v in incs:
                            ins.then_inc(sems[n], v)
                    if ename == "sp":
                        for n, v in self.dma_sems.items():
                            eng.wait_ge(sems[n], v)
                return body

            block.tensor(replay("pe"))
            block.scalar(replay("act"))
            block.vector(replay("dve"))
            block.gpsimd(replay("pool"))
            block.sync(replay("sp"))


def host_consts():
    c = {}
    c["c_ident"] = np.eye(128, dtype=np.float32)
    zu = np.zeros((128, 8, 240), np.float32)
    for gl in range(8):
        for cc in range(16):
            zu[16 * gl + cc, gl, 112 + cc] = 1.0
    c["c_zu"] = zu.reshape(128, 8 * 240)
    sc = np.arange(128) // 16
    cm = (sc[None, :] >= sc[:, None]).astype(np.float32)
    c["c_cm"] = cm
    tri = (np.arange(128)[None, :] <= np.arange(128)[:, None]).astype(np.float32)
    c["c_tri"] = tri
    im = np.zeros((128, 2, 128), np.float32)
    for k in range(128):
        im[k, k // 64, k] = 1.0
    c["c_imask"] = im.reshape(128, 256)
    kvals = np.concatenate([np.arange(7, -1, -1), np.arange(-7, 1), np.arange(1, 9)]).astype(np.float32)
    kv = np.tile(kvals[None, None, :], (128, 8, 1))
    c["c_kv3"] = kv.reshape(128, 192)
    c["c_jv"] = np.tile(np.arange(1, J + 1, dtype=np.float32)[None, :], (128, 1))
    hm = np.zeros((128, 2, 128), np.float32)
    hm[:, 0, 0:64] = 2.0
    hm[:, 1, 64:128] = 2.0
    c["c_hmask"] = hm.reshape(128, 256)
    return c


CONST_SHAPES = {"c_ident": [128, 128], "c_zu": [128, 1920], "c_cm": [128, 128], "c_tri": [128, 128],
                "c_imask": [128, 256], "c_kv3": [128, 192], "c_jv": [128, J], "c_hmask": [128, 256]}

W_SHAPES = {
    "w_in": [DEPTH, D, 2560], "gm_w_s": [DEPTH, 4, 128, 128], "gm_b_s": [DEPTH, 4, 128],
    "gm_ln_g": [DEPTH, 4, 128], "gm_ln_b": [DEPTH, 4, 128], "ssm_lam_re": [DEPTH, 16, 64],
    "ssm_lam_im": [DEPTH, 16, 64], "ssm_log_step": [DEPTH, 16], "ssm_b_re": [DEPTH, 16, 64, 16],
    "ssm_b_im": [DEPTH, 16, 64, 16], "ssm_c_re": [DEPTH, 16, 16, 64], "ssm_c_im": [DEPTH, 16, 16, 64],
    "ssm_d": [DEPTH, 256], "glu_w": [DEPTH, 256, 256], "glu_b": [DEPTH, 256],
    "xa_w_k": [DEPTH, D, 256], "xa_w_v": [DEPTH, D, 256], "w_out": [DEPTH, D, D],
    "ln_g": [DEPTH, D], "ln_b": [DEPTH, D],
}


SBUF_FREE = [None]


def build_program(dbg=False, n_segs_limit=None):
    nc = bass.Bass("TRN2", target_bir_lowering=False)
    dr = {}
    dr["x"] = nc.dram_tensor("x", [NB, SEQ, D], F32, kind="ExternalInput").ap()
    dr["mem"] = nc.dram_tensor("mem", [NB, MEM, D], F32, kind="ExternalInput").ap()
    for k, shp in W_SHAPES.items():
        dr[k] = nc.dram_tensor(k, shp, F32, kind="ExternalInput").ap()
    for k, shp in CONST_SHAPES.items():
        dr[k] = nc.dram_tensor(k, shp, F32, kind="ExternalInput").ap()
    out = nc.dram_tensor("out", [NB, SEQ, D], F32, kind="ExternalOutput").ap()
    sck = dict(kind="ExternalOutput") if dbg else {}
    sc_ws = [nc.dram_tensor("sc_ws%d" % l, [128, 16 * 2 * 128], BF16, **sck).ap() for l in range(DEPTH)]
    sc_ml = [nc.dram_tensor("sc_ml%d" % l, [128, 16 * 128], BF16, **sck).ap() for l in range(DEPTH)]
    sc_ez = [nc.dram_tensor("sc_ez%d" % l, [128, 16 * 2 * 128], BF16, **sck).ap() for l in range(DEPTH)]
    sc_tb = [nc.dram_tensor("sc_tb%d" % l, [128, 3 * 8 * J + 8], F32, **sck).ap() for l in range(DEPTH)]
    if dbg:
        dbg_y = nc.dram_tensor("dbg_y", [128, 8 * BLK], F32, kind="ExternalOutput").ap()
        dbg_x = nc.dram_tensor("dbg_x", [128, 4 * D], F32, kind="ExternalOutput").ap()
    taps = {}

    def tap(P, name, ap, n, dt, reads):
        if not dbg:
            return
        t = nc.dram_tensor(name, [128, n], dt, kind="ExternalOutput").ap()
        P.dma(lambda e: e.dma_start(out=t, in_=ap), reads=reads, sem="d_dbg")

    P = Plan(nc)
    with contextlib.ExitStack() as st:
        def sb(name, shape, dt):
            return st.enter_context(nc.sbuf_tensor(name, shape, dt))

        def ps(name, shape, dt):
            return st.enter_context(nc.psum_tensor(name, shape, dt))

        psf = [ps("psf%d" % i, [128, 512], F32) for i in range(8)]
        rot = {"f": 0}

        def nf():
            i = rot["f"]; rot["f"] = (i + 1) % 8
            return psf[i], "psf%d" % i

        def nb_():
            i = rot["f"]; rot["f"] = (i + 1) % 8
            return psf[i][:, :].bitcast(BF16), "psf%d" % i

        identf = sb("identf", [128, 128], F32)
        identb = sb("identb", [128, 128], BF16)
        zu = sb("zu", [128, 8, 240], BF16)
        cm = sb("cm", [128, 128], F32)
        imask = sb("imask", [128, 2, 128], BF16)
        hmask = sb("hmask", [128, 2, 128], BF16)
        onesb = sb("onesb", [128, 128], BF16)
        onesf = sb("onesf", [1, 128], F32)
        epst = sb("epst", [128, 1], F32)
        win = sb("win", [128, 8, 2560], BF16)
        wout = sb("wout", [128, 8, 1024], BF16)
        wk = sb("wk", [128, 8, 256], BF16)
        wv = sb("wv", [128, 8, 256], BF16)
        wsz = sb("wsz", [128, 16, 2, 128], BF16)
        mloc = sb("mloc", [128, 16, 128], BF16)
        ez = sb("ez", [128, 16, 2, 128], BF16)
        tabs = sb("tabs", [128, 3 * 8 * J + 8], F32)
        cosT = tabs[:, 0:8 * J]
        sinT = tabs[:, 8 * J:16 * J]
        r8 = tabs[:, 16 * J:16 * J + 8]
        decT = tabs[:, 16 * J + 8:24 * J + 8]
        lnG = sb("lnG", [128, D], F32)
        lnB = sb("lnB", [128, D], F32)
        wTg = [sb("wTg%d" % l, [128, 4, 128], BF16) for l in range(DEPTH)]
        bias2 = [sb("bias2_%d" % l, [128, 4, 128], F32) for l in range(DEPTH)]
        lng = [sb("lng%d" % l, [128, 4], F32) for l in range(DEPTH)]
        gluw = [sb("gluw%d" % l, [128, 2, 256], BF16) for l in range(DEPTH)]
        glub = [sb("glub%d" % l, [128, 2], F32) for l in range(DEPTH)]
        kTm = [sb("kTm%d" % l, [128, 4, 256], BF16) for l in range(DEPTH)]
        vm = [sb("vm%d" % l, [128, 2, 4, 128], BF16) for l in range(DEPTH)]
        cH = [[sb("cH%d_%d" % (l, r), [128, 8], F32) for r in range(2)] for l in range(DEPTH)]
        x_tok = sb("x_tok", [128, 8, D], F32)
        x_bf = sb("x_bf", [128, 4, D], BF16)
        xbT = sb("xbT", [128, 2, BLK], BF16)
        gbs = sb("gbs", [128, 2, BLK], BF16)
        yT = sb("yT", [128, 8, BLK], BF16)
        arX = sb("arX", [128, 6144], BF16)
        xT = arX[:, 0:4096].rearrange("p (k n) -> p k n", k=8)
        vn = arX[:, 4096:6144].rearrange("p (t n) -> p t n", t=4)
        hdi_t = sb("hdi_t", [128, 8 * J], F32)
        P.set_alias("s5t", ["x_bf0", "x_bf1", "x_bf2", "x_bf3"])
        mem_bf = x_bf[:, 0:2, :]
        memT = x_bf[:, 2:4, :].rearrange("p a n -> p (a n)").rearrange("p (k n) -> p k n", k=8)
        P.set_alias("memb", ["x_bf0", "x_bf1", "s5t"]); P.set_alias("memT", ["x_bf2", "x_bf3", "s5t"])
        arU = sb("arU", [128, 1024], BF16)
        ug = arU[:, 0:512]
        gs = arU[:, 512:1024]
        ugs = [(arU[:, 0:512], arU[:, 512:1024]), (arU[:, 0:512], arU[:, 512:1024])]
        Xg_t = sb("Xg_t", [128, 1024], BF16)
        Xg = Xg_t[:, :].rearrange("p (g j) -> p g j", g=16)
        arQ = sb("arQ", [128, 1024], BF16)
        qT = arQ[:, 0:512]
        gxs = arQ[:, 512:1024]
        Yg = arQ[:, :].rearrange("p (g j) -> p g j", g=16)
        P.set_alias("Yg", ["qT", "gxs"])
        arR = sb("arR", [128, 1056], BF16)
        recip = arR[:, 0:1024].bitcast(F32)
        Hb = arR[:, 0:2 * 8 * (J + 1)].rearrange("p (r g j) -> p r g j", r=2, g=8)
        P.set_alias("Hb", ["recip"])
        arV = sb("arV", [128, 1024], BF16)
        tmpA = arV[:, 0:512]
        ybT = arV[:, :].rearrange("p (c n) -> p c n", c=2)
        P.set_alias("ybT", ["tmpA"])
        sgt = sb("sgt", [128, BLK], BF16)
        arP = sb("arP", [128, 2048], BF16)
        pT = arP[:, :].rearrange("p (m h n) -> p m h n", m=2, h=2)
        acc0 = arP[:, :].bitcast(F32)
        acc1_t = sb("acc1_t", [128, D], F32)
        accs = [(acc0, "acc"), (acc1_t[:, :], "acc1")]
        P.set_alias("acc", ["pT"])
        st6 = sb("st6", [128, 16, 6], F32)
        mv = sb("mv", [128, 16, 2], F32)
        rstd = sb("rstd", [128, 16], F32)
        car = sb("car", [128, 2, 8], F32)
        lnst = sb("lnst", [128, 2, 16], F32)
        stc = sb("stc", [128, 2, 2, 6], F32)
        mvc = sb("mvc", [128, 2, 2], F32)
        rsc = sb("rsc", [128, 2, 2], F32)
        nw_v = sb("nw_v", [128, 18], F32)
        nw_t = sb("nw_t", [128, 18], F32)
        nw_i = sb("nw_i", [128, 18], I32)

        P.set_alias("setup", ["x_tok%d" % i for i in range(8)] + ["x_bf%d" % i for i in range(4)] + ["yT%d" % i for i in range(8)]
                    + ["x_bf", "xT", "vn", "s5t", "memb", "memT", "wout", "sin0", "sin1", "setupc", "RmZ"])
        sp_sem_ct = [0]
        SBUF_FREE[0] = nc.sbuf_bytes_remaining

        def sem_name(base):
            return "d_" + base

        P.dma(lambda e: e.dma_start(out=identf[:], in_=dr["c_ident"]), writes=["identf"], sem="d_c0")
        P.dma(lambda e: e.dma_start(out=cm[:], in_=dr["c_cm"]), writes=["cm"], sem="d_c1")
        P.dma(lambda e: e.dma_start(out=identb[:], in_=dr["c_ident"]), writes=["identb"], sem="d_c2", q="pool")
        P.dma(lambda e: e.dma_start(out=zu[:].rearrange("p g n -> p (g n)"), in_=dr["c_zu"]), writes=["zu"], sem="d_c3", q="pool")
        P.dma(lambda e: e.dma_start(out=imask[:].rearrange("p g n -> p (g n)"), in_=dr["c_imask"]), writes=["imask"], sem="d_c4", q="pool")
        P.dma(lambda e: e.dma_start(out=hmask[:].rearrange("p g n -> p (g n)"), in_=dr["c_hmask"]), writes=["hmask"], sem="d_c5", q="pool")
        P.op("dve", lambda e: e.memset(onesb[:], 1.0), writes=["onesb"])
        P.op("dve", lambda e: e.memset(onesf[:], 1.0), writes=["onesf"])
        P.op("dve", lambda e: e.memset(epst[:], EPS), writes=["epst"])
        for l in range(DEPTH):
            P.op("dve", lambda e, l=l: e.memset(kTm[l][:], 0.0), writes=["kTm%d" % l])
            P.op("pool", lambda e, l=l: e.memset(vm[l][:], 0.0), writes=["vm%d" % l])

        WIN_GROUPS = [("v", 512, 512), ("xb", 1536, 256), ("gb", 1792, 256)]
        for h in range(4):
            WIN_GROUPS += [("u%d" % h, h * 128, 128), ("g%d" % h, 1024 + h * 128, 128)]
        for c in range(2):
            WIN_GROUPS += [("q%d" % c, 2048 + c * 128, 128), ("gx%d" % c, 2304 + c * 128, 128)]
        WIN_G = {n: (c0, nc_) for n, c0, nc_ in WIN_GROUPS}

        def load_win_group(l, name):
            c0, ncol = WIN_G[name]
            P.dma(lambda e, l=l, c0=c0, ncol=ncol: e.dma_start(out=win[:, :, c0:c0 + ncol],
                                                               in_=dr["w_in"][l, :, c0:c0 + ncol].rearrange("(k p) n -> p k n", p=128)),
                  writes=["win_" + name], q="pool")

        def load_wkv(l):
            P.dma(lambda e, l=l: e.dma_start(out=wk[:], in_=dr["xa_w_k"][l].rearrange("(k p) n -> p k n", p=128)), writes=["wk"], q="pool")
            P.dma(lambda e, l=l: e.dma_start(out=wv[:], in_=dr["xa_w_v"][l].rearrange("(k p) n -> p k n", p=128)), writes=["wv"], q="pool")

        for name, _, _ in WIN_GROUPS:
            load_win_group(0, name)
        load_wkv(0)

        xt_flat = x_tok[:].rearrange("p t d -> p (t d)")
        wo_flat = wout[:].rearrange("p k n -> p (k n)").bitcast(F32)
        off = [0]
        off2 = [0]

        def tmp32(n):
            a = xt_flat[:, off[0]:off[0] + n]
            off[0] += n
            assert off[0] <= 8192, off[0]
            return a

        def tmpw(n):
            a = wo_flat[:, off2[0]:off2[0] + n]
            off2[0] += n
            assert off2[0] <= 4096
            return a

        NK = 24
        kv3 = tmp32(8 * NK)
        jv = tmp32(J)
        tri = tmp32(128)
        P.dma(lambda e: e.dma_start(out=kv3, in_=dr["c_kv3"]), writes=["setupc"])
        P.dma(lambda e: e.dma_start(out=jv, in_=dr["c_jv"]), writes=["setupc"])
        P.dma(lambda e: e.dma_start(out=tri, in_=dr["c_tri"]), writes=["setupc"])

        INP = []
        QS = ["sp", "act"]
        qi = [0]

        def qdma(fn, SK):
            P.dma(fn, writes=[SK], q=QS[qi[0] % 2])
            qi[0] += 1

        INP = [None] * DEPTH
        for l in reversed(range(DEPTH)):
            I_ = {}
            SK = "sin%d" % l
            I_["wraw"] = tmp32(512); I_["bs_bc"] = tmp32(512)
            I_["PA"] = tmp32(128); I_["LT"] = tmp32(128); I_["L16"] = tmp32(16)
            I_["Bre"] = tmp32(128); I_["Bim"] = tmp32(128); I_["CTr"] = tmp32(256); I_["CTi"] = tmp32(256)
            INP[l] = I_
            qdma(lambda e, l=l, a=I_["wraw"]: e.dma_start(out=a.rearrange("p (h s) -> p h s", h=4), in_=dr["gm_w_s"][l].rearrange("h t s -> t h s")), SK)
            qdma(lambda e, l=l, a=I_["bs_bc"]: e.dma_start(out=a, in_=dr["gm_b_s"][l].rearrange("(o h) d -> o (h d)", o=1).to_broadcast([128, 512])), SK)
            qdma(lambda e, l=l, a=I_["PA"]: e.dma_start(out=a[0:4, :], in_=dr["gm_ln_b"][l]), SK)
            qdma(lambda e, l=l, a=I_["PA"]: e.dma_start(out=a[4:8, :], in_=dr["gm_ln_g"][l]), SK)
            qdma(lambda e, l=l, a=I_["PA"]: e.dma_start(out=a[8:10, :], in_=dr["glu_b"][l].rearrange("(c p) -> c p", p=128)), SK)
            P.dma(lambda e, l=l: e.dma_start(out=gluw[l][:], in_=dr["glu_w"][l].rearrange("(k p) n -> p k n", p=128)), writes=["gluw%d" % l], q="pool")
            for o in range(2):
                qdma(lambda e, l=l, o=o, a=I_["LT"]: e.dma_start(out=a[0:16, o * 64:(o + 1) * 64], in_=dr["ssm_lam_re"][l]), SK)
                qdma(lambda e, l=l, o=o, a=I_["LT"]: e.dma_start(out=a[16:32, o * 64:(o + 1) * 64], in_=dr["ssm_lam_im"][l]), SK)
            qdma(lambda e, l=l, a=I_["LT"]: e.dma_start(out=a[32:48, :].rearrange("g (s c) -> g s c", s=8),
                                                      in_=dr["ssm_d"][l].rearrange("(g o c) -> g o c", o=1, c=16).to_broadcast([16, 8, 16])), SK)
            qdma(lambda e, l=l, a=I_["L16"]: e.dma_start(out=a, in_=dr["ssm_log_step"][l].rearrange("(o g) -> o g", o=1).to_broadcast([128, 16])), SK)
            for gl in range(2):
                rs_ = slice(gl * 64, (gl + 1) * 64)
                qdma(lambda e, l=l, gl=gl, rs_=rs_, a=I_["Bre"]: e.dma_start(out=a[rs_, :].rearrange("p (g c) -> p g c", g=8),
                                                                       in_=dr["ssm_b_re"][l].rearrange("(gp gl) p c -> gl p gp c", gl=2)[gl]), SK)
                qdma(lambda e, l=l, gl=gl, rs_=rs_, a=I_["Bim"]: e.dma_start(out=a[rs_, :].rearrange("p (g c) -> p g c", g=8),
                                                                       in_=dr["ssm_b_im"][l].rearrange("(gp gl) p c -> gl p gp c", gl=2)[gl]), SK)
            for nm, key in (("ssm_c_re", "CTr"), ("ssm_c_im", "CTi")):
                for t in range(2):
                    for o in range(2):
                        qdma(lambda e, l=l, t=t, o=o, nm=nm, a=I_[key]: e.dma_start(
                            out=a[:, t * 128 + o * 64: t * 128 + (o + 1) * 64], in_=dr[nm][l].rearrange("(t gi) c p -> t (gi c) p", t=2)[t]), SK)
        base_off = off[0]

        def setup_layer(l):
            off[0] = base_off
            off2[0] = 0
            K = "setup"
            SK = "sin%d" % l
            RK = [K, SK, "setupc"]
            I_ = INP[l]
            Bre, Bim = I_["Bre"], I_["Bim"]
            lre = tmp32(8); lim = tmp32(8); lst = tmp32(8); dcol = tmp32(16); lnb_col = tmp32(4)
            pf, pfk = nf()
            P.op("pe", lambda e, pf=pf: e.transpose(pf[:, 0:10], I_["PA"][0:10, :], identf[0:10, 0:10]), reads=RK + ["identf"], writes=[pfk])
            P.op("dve", lambda e, pf=pf: e.tensor_copy(out=lnb_col, in_=pf[:, 0:4]), reads=[pfk], writes=[K])
            P.op("dve", lambda e, pf=pf: e.tensor_scalar_mul(out=lng[l][:], in0=pf[:, 4:8], scalar1=0.5), reads=[pfk], writes=["lng%d" % l])
            P.op("dve", lambda e, pf=pf: e.tensor_scalar_mul(out=glub[l][:], in0=pf[:, 8:10], scalar1=0.5), reads=[pfk], writes=["glub%d" % l])
            pf2, pfk2 = nf()
            P.op("pe", lambda e, pf2=pf2: e.transpose(pf2[:, 0:48], I_["LT"][0:48, :], identf[0:48, 0:48]), reads=RK + ["identf"], writes=[pfk2])
            for gl in range(2):
                rs_ = slice(gl * 64, (gl + 1) * 64)
                P.op("dve", lambda e, pf2=pf2, gl=gl, rs_=rs_: e.tensor_copy(out=lre[rs_, :], in_=pf2[rs_, gl:16:2]), reads=[pfk2], writes=[K])
                P.op("dve", lambda e, pf2=pf2, gl=gl, rs_=rs_: e.tensor_copy(out=lim[rs_, :], in_=pf2[rs_, 16 + gl:32:2]), reads=[pfk2], writes=[K])
                P.op("dve", lambda e, gl=gl, rs_=rs_: e.tensor_copy(out=lst[rs_, :], in_=I_["L16"][rs_, gl:16:2]), reads=RK, writes=[K])
            P.op("dve", lambda e, pf2=pf2: e.tensor_copy(out=dcol, in_=pf2[:, 32:48]), reads=[pfk2], writes=[K])
            wraw = I_["wraw"]
            wmb = x_bf[:, 0, 0:512]
            P.op("dve", lambda e, wraw=wraw, wmb=wmb: e.tensor_tensor(
                out=wmb.rearrange("p (h s) -> p h s", h=4), in0=wraw.rearrange("p (h s) -> p h s", h=4),
                in1=tri.unsqueeze(1).to_broadcast([128, 4, 128]), op=ALU.mult), reads=RK, writes=["x_bf"])
            pb, pbk = nb_()
            for h in range(4):
                P.op("pe", lambda e, h=h, pb=pb, wmb=wmb: e.transpose(pb[:, h * 128:(h + 1) * 128], wmb[:, h * 128:(h + 1) * 128], identb[:]),
                     reads=["x_bf", "identb"], writes=[pbk])
            P.op("dve", lambda e, l=l, pb=pb: e.tensor_copy(out=wTg[l][:].rearrange("p h t -> p (h t)"), in_=pb[:, 0:512]),
                 reads=[pbk], writes=["wTg%d" % l])
            pf, pfk = nf()
            P.op("pe", lambda e, l=l, pf=pf: e.matmul(pf[:, :], lhsT=onesb[:, :], rhs=wTg[l][:].rearrange("p h t -> p (h t)"),
                                                      start=True, stop=True), reads=["onesb", "wTg%d" % l], writes=[pfk])
            for h in range(4):
                P.op("dve", lambda e, l=l, h=h, pf=pf, bs_bc=I_["bs_bc"], lnb_col=lnb_col: e.scalar_tensor_tensor(
                    out=bias2[l][:, h, :], in0=pf[:, h * 128:(h + 1) * 128], scalar=lnb_col[:, h:h + 1], in1=bs_bc[:, h * 128:(h + 1) * 128],
                    op0=ALU.mult, op1=ALU.add), reads=[pfk] + RK, writes=["bias2_%d" % l])
            P.op("dve", lambda e, l=l: e.tensor_scalar_mul(out=bias2[l][:].rearrange("p h t -> p (h t)"), in0=bias2[l][:].rearrange("p h t -> p (h t)"), scalar1=0.5),
                 reads=["bias2_%d" % l], writes=["bias2_%d" % l])

            Cre = tmp32(128); Cim = tmp32(128)
            for key, dst in (("CTr", Cre), ("CTi", Cim)):
                for t in range(2):
                    pfc, pfck = nf()
                    P.op("pe", lambda e, pfc=pfc, key=key, t=t: e.transpose(pfc[:, 0:128], I_[key][:, t * 128:(t + 1) * 128], identf[:]),
                         reads=RK + ["identf"], writes=[pfck])
                    for gl in range(2):
                        rs_ = slice(gl * 64, (gl + 1) * 64)
                        P.op("dve", lambda e, pfc=pfc, gl=gl, rs_=rs_, t=t, dst=dst: e.tensor_copy(
                            out=dst[rs_, :].rearrange("p (g c) -> p g c", g=8)[:, 4 * t:4 * t + 4, :],
                            in_=pfc[rs_, 0:128].rearrange("p (gpl gl2 c) -> p gpl gl2 c", gl2=2, c=16)[:, :, gl, :]), reads=[pfck], writes=[K])
            dt = tmp32(8); ar = tmp32(8); th = tmp32(8)
            P.op("act", lambda e, dt=dt, lst=lst: e.activation(out=dt, in_=lst, func=AF.Exp), reads=RK, writes=[K])
            P.op("dve", lambda e, ar=ar, lre=lre, dt=dt: e.tensor_tensor(out=ar, in0=lre, in1=dt, op=ALU.mult), reads=RK, writes=[K])
            P.op("dve", lambda e, th=th, lim=lim, dt=dt: e.tensor_tensor(out=th, in0=lim, in1=dt, op=ALU.mult), reads=RK, writes=[K])
            NE = 8 * NK
            mag = tmp32(NE); tn = tmp32(NE); tq = tmp32(NE); fr = tmp32(NE); sn = tmp32(NE); cs = tmp32(NE)
            ti = tmp32(NE).bitcast(I32)
            k3 = lambda a: a.rearrange("p (g k) -> p g k", g=8)
            P.op("dve", lambda e: e.tensor_tensor(out=k3(mag), in0=k3(kv3), in1=ar.unsqueeze(2).to_broadcast([128, 8, NK]), op=ALU.mult), reads=RK, writes=[K])
            P.op("act", lambda e: e.activation(out=mag, in_=mag, func=AF.Exp), reads=[K], writes=[K])
            P.op("dve", lambda e: e.scalar_tensor_tensor(out=k3(tn), in0=k3(kv3), scalar=1.0 / TWO_PI, in1=th.unsqueeze(2).to_broadcast([128, 8, NK]),
                                                         op0=ALU.mult, op1=ALU.mult), reads=RK, writes=[K])

            def reduce_turns(src, dst, ti=ti, tq=tq):
                n_ = src.shape[1]
                P.op("dve", lambda e: e.tensor_copy(out=ti[:, 0:n_], in_=src), reads=[K], writes=[K])
                P.op("dve", lambda e: e.tensor_copy(out=tq[:, 0:n_], in_=ti[:, 0:n_]), reads=[K], writes=[K])
                P.op("dve", lambda e: e.tensor_tensor(out=dst, in0=src, in1=tq[:, 0:n_], op=ALU.subtract), reads=[K], writes=[K])

            reduce_turns(tn, fr)
            P.op("act", lambda e: e.activation(out=sn, in_=fr, func=AF.Sin, scale=TWO_PI), reads=[K], writes=[K])
            P.op("dve", lambda e: e.tensor_scalar_add(out=cs, in0=fr, scalar1=0.25), reads=[K], writes=[K])
            reduce_turns(cs, cs)
            P.op("act", lambda e: e.activation(out=cs, in_=cs, func=AF.Sin, scale=TWO_PI), reads=[K], writes=[K])
            pwr = tmp32(NE); pwi = tmp32(NE)
            P.op("dve", lambda e: e.tensor_tensor(out=pwr, in0=mag, in1=cs, op=ALU.mult), reads=[K], writes=[K])
            P.op("dve", lambda e: e.tensor_tensor(out=pwi, in0=mag, in1=sn, op=ALU.mult), reads=[K], writes=[K])
            pwr3 = k3(pwr); pwi3 = k3(pwi)
            xr = tmp32(8); den = tmp32(8); t8a = tmp32(8); t8b = tmp32(8); cr = tmp32(8); ci = tmp32(8)
            yi = pwi3[:, :, 16]
            P.op("dve", lambda e: e.tensor_scalar_add(out=xr, in0=pwr3[:, :, 16], scalar1=-1.0), reads=[K], writes=[K])
            P.op("dve", lambda e: e.tensor_tensor(out=den, in0=lre, in1=lre, op=ALU.mult), reads=RK, writes=[K])
            P.op("dve", lambda e: e.tensor_tensor(out=t8a, in0=lim, in1=lim, op=ALU.mult), reads=RK, writes=[K])
            P.op("dve", lambda e: e.tensor_tensor(out=den, in0=den, in1=t8a, op=ALU.add), reads=[K], writes=[K])
            P.op("dve", lambda e: e.reciprocal(out=den, in_=den), reads=[K], writes=[K])
            P.op("dve", lambda e: e.tensor_tensor(out=t8a, in0=xr, in1=lre, op=ALU.mult), reads=RK, writes=[K])
            P.op("dve", lambda e: e.tensor_tensor(out=t8b, in0=yi, in1=lim, op=ALU.mult), reads=RK, writes=[K])
            P.op("dve", lambda e: e.tensor_tensor(out=t8a, in0=t8a, in1=t8b, op=ALU.add), reads=[K], writes=[K])
            P.op("dve", lambda e: e.tensor_tensor(out=cr, in0=t8a, in1=den, op=ALU.mult), reads=[K], writes=[K])
            P.op("dve", lambda e: e.tensor_tensor(out=t8a, in0=yi, in1=lre, op=ALU.mult), reads=RK, writes=[K])
            P.op("dve", lambda e: e.tensor_tensor(out=t8b, in0=xr, in1=lim, op=ALU.mult), reads=RK, writes=[K])
            P.op("dve", lambda e: e.tensor_tensor(out=t8a, in0=t8a, in1=t8b, op=ALU.subtract), reads=[K], writes=[K])
            P.op("dve", lambda e: e.tensor_tensor(out=ci, in0=t8a, in1=den, op=ALU.mult), reads=[K], writes=[K])
            Bbr = tmp32(128); Bbi = tmp32(128)
            ta = tmpw(1024); tb = tmpw(1024)
            g3 = lambda a: a.rearrange("p (g c) -> p g c", g=8)
            g4 = lambda a: a.rearrange("p (g s c) -> p g s c", g=8, s=8)
            bc8 = lambda a: a.unsqueeze(2).to_broadcast([128, 8, 16])

            def cmul(out_r, out_i, ar_, ai_, br_, bi_, tv, neg_i=False):
                ta_, tb_ = tv(ta), tv(tb)
                P.op("dve", lambda e: e.tensor_tensor(out=ta_, in0=br_, in1=ar_, op=ALU.mult), reads=RK, writes=[K])
                P.op("dve", lambda e: e.tensor_tensor(out=tb_, in0=bi_, in1=ai_, op=ALU.mult), reads=RK, writes=[K])
                P.op("dve", lambda e: e.tensor_tensor(out=out_r, in0=ta_, in1=tb_, op=ALU.subtract), reads=[K], writes=[K])
                P.op("dve", lambda e: e.tensor_tensor(out=ta_, in0=bi_, in1=ar_, op=ALU.mult), reads=RK, writes=[K])
                P.op("dve", lambda e: e.tensor_tensor(out=tb_, in0=br_, in1=ai_, op=ALU.mult), reads=RK, writes=[K])
                if neg_i:
                    P.op("dve", lambda e: e.scalar_tensor_tensor(out=out_i, in0=ta_, scalar=-1.0, in1=tb_, op0=ALU.mult, op1=ALU.subtract),
                         reads=[K], writes=[K])
                else:
                    P.op("dve", lambda e: e.tensor_tensor(out=out_i, in0=ta_, in1=tb_, op=ALU.add), reads=[K], writes=[K])

            cmul(g3(Bbr), g3(Bbi), bc8(cr), bc8(ci), g3(Bre), g3(Bim), lambda a: g3(a[:, 0:128]))
            Ar = x_bf[:, 1, :].rearrange("p (g s c) -> p g s c", g=8, s=8)
            Ai = x_bf[:, 2, :].rearrange("p (g s c) -> p g s c", g=8, s=8)
            Rr = x_bf[:, 3, :].rearrange("p (g s c) -> p g s c", g=8, s=8)
            Rin = x_bf[:, 0, :].rearrange("p (g s c) -> p g s c", g=8, s=8)
            Etr = yT[:, 0:2, :].rearrange("p a n -> p (a n)").rearrange("p (g s c) -> p g s c", g=8, s=8)
            Etin = yT[:, 2:4, :].rearrange("p a n -> p (a n)").rearrange("p (g s c) -> p g s c", g=8, s=8)
            S4 = [128, 8, 8, 16]
            pwb = lambda p3, i0: p3[:, :, i0:i0 + 8].unsqueeze(3).to_broadcast(S4)
            vb = lambda a: g3(a).unsqueeze(2).to_broadcast(S4)
            cmul(Ar, Ai, pwb(pwr3, 0), pwb(pwi3, 0), vb(Bbr), vb(Bbi), g4)
            cmul(Rr, Rin, pwb(pwr3, 8), pwb(pwi3, 8), vb(Cre), vb(Cim), g4, neg_i=True)
            cmul(Etr, Etin, pwb(pwr3, 16), pwb(pwi3, 16), vb(Cre), vb(Cim), g4, neg_i=True)
            Rm = arX[:, 0:4096].rearrange("p (g r n) -> p g r n", g=16, r=2)
            P.op("pool", lambda e: e.memset(Rm, 0.0), reads=[K], writes=["RmZ"])
            P.op("pool", lambda e: e.memset(ez[:], 0.0), reads=[K], writes=["ez"])
            for gl in range(2):
                rs_ = slice(gl * 64, (gl + 1) * 64)
                for r, (srcR, srcE) in enumerate(((Rr, Etr), (Rin, Etin))):
                    P.op("dve", lambda e, rs_=rs_, gl=gl, r=r, srcR=srcR: e.tensor_copy(
                        out=Rm[rs_, gl:16:2, r, :], in_=srcR[rs_].rearrange("p g s c -> p g (s c)")), reads=[K, "RmZ"], writes=[K])
                    P.op("dve", lambda e, rs_=rs_, gl=gl, r=r, srcE=srcE: e.tensor_copy(
                        out=ez[rs_, gl:16:2, r, :], in_=srcE[rs_].rearrange("p g s c -> p g (s c)")), reads=[K], writes=["ez"])
            for gp in range(8):
                pf, pfk = nf()
                for gl in range(2):
                    for r, A_ in enumerate((Ar, Ai)):
                        P.op("pe", lambda e, pf=pf, gp=gp, gl=gl, r=r, A_=A_: e.matmul(
                            pf[:, (gl * 2 + r) * 128:(gl * 2 + r + 1) * 128], lhsT=A_[:, gp].rearrange("p s c -> p (s c)"), rhs=imask[:, gl, :],
                            start=True, stop=True), reads=[K, "imask"], writes=[pfk])
                if gp % 2 == 0:
                    P.op("act", lambda e, pf=pf, gp=gp: e.activation(out=wsz[:, 2 * gp:2 * gp + 2].rearrange("p g r n -> p (g r n)"), in_=pf[:, :], func=AF.Identity),
                         reads=[pfk], writes=["wsz"])
                else:
                    P.op("dve", lambda e, pf=pf, gp=gp: e.tensor_copy(out=wsz[:, 2 * gp:2 * gp + 2].rearrange("p g r n -> p (g r n)"), in_=pf[:, :]),
                         reads=[pfk], writes=["wsz"])
            mtmp = tmp32(512)
            for g4i in range(4):
                pf, pfk = nf()
                for gi in range(4):
                    g = g4i * 4 + gi
                    gp = g // 2
                    P.op("pe", lambda e, pf=pf, gi=gi, g=g, gp=gp: e.matmul(pf[:, gi * 128:(gi + 1) * 128], lhsT=Ar[:, gp].rearrange("p s c -> p (s c)"),
                                                                    rhs=Rm[:, g, 0, :], start=True, stop=False), reads=[K], writes=[pfk])
                    P.op("pe", lambda e, pf=pf, gi=gi, g=g, gp=gp: e.matmul(pf[:, gi * 128:(gi + 1) * 128], lhsT=Ai[:, gp].rearrange("p s c -> p (s c)"),
                                                                    rhs=Rm[:, g, 1, :], start=False, stop=True), reads=[K], writes=[pfk])
                P.op("dve", lambda e, pf=pf: e.tensor_tensor(out=mtmp.rearrange("p (g n) -> p g n", g=4), in0=pf[:, :].rearrange("p (g n) -> p g n", g=4),
                                                             in1=cm[:].unsqueeze(1).to_broadcast([128, 4, 128]), op=ALU.mult), reads=[pfk, "cm", K], writes=[K])
                for gi in range(4):
                    g = g4i * 4 + gi
                    P.op("dve", lambda e, gi=gi, g=g: e.scalar_tensor_tensor(out=mloc[:, g, :], in0=identf[:], scalar=dcol[:, g:g + 1],
                                                                             in1=mtmp[:, gi * 128:(gi + 1) * 128], op0=ALU.mult, op1=ALU.add),
                         reads=RK + ["identf"], writes=["mloc"])
            fr3 = k3(fr)
            tt_ = tmpw(8 * J); tf_ = tmpw(8 * J)
            tib = tmpw(8 * J).bitcast(I32)
            tqb = tmpw(8 * J)
            for gp in range(8):
                P.op("dve", lambda e, gp=gp: e.tensor_scalar(out=tt_[:, gp * J:(gp + 1) * J], in0=jv, scalar1=fr3[:, gp, 23:24], scalar2=None, op0=ALU.mult),
                     reads=RK, writes=[K])

            def reduce_big(src, dst):
                P.op("dve", lambda e: e.tensor_copy(out=tib, in_=src), reads=[K], writes=[K])
                P.op("dve", lambda e: e.tensor_copy(out=tqb, in_=tib), reads=[K], writes=[K])
                P.op("dve", lambda e: e.tensor_tensor(out=dst, in0=src, in1=tqb, op=ALU.subtract), reads=[K], writes=[K])

            reduce_big(tt_, tf_)
            P.op("act", lambda e: e.activation(out=sinT, in_=tf_, func=AF.Sin, scale=TWO_PI), reads=[K], writes=["tabs"])
            P.op("dve", lambda e: e.tensor_scalar_add(out=tt_, in0=tf_, scalar1=0.25), reads=[K], writes=[K])
            reduce_big(tt_, tf_)
            P.op("act", lambda e: e.activation(out=cosT, in_=tf_, func=AF.Sin, scale=TWO_PI), reads=[K], writes=["tabs"])
            P.op("dve", lambda e: e.tensor_copy(out=r8, in_=k3(mag)[:, :, 23]), reads=[K], writes=["tabs"])
            P.op("dve", lambda e: e.tensor_copy(out=decT.rearrange("p (g j) -> p g j", g=8), in_=k3(mag)[:, :, 23:24].to_broadcast([128, 8, J])), reads=[K], writes=["tabs"])
            P.op("dve", lambda e: e.memset(decT.rearrange("p (g j) -> p g j", g=8)[:, :, 0], 0.0), reads=["tabs"], writes=["tabs"])
            P.dma(lambda e, l=l: e.dma_start(out=sc_ws[l], in_=wsz[:].rearrange("p g r n -> p (g r n)")), reads=["wsz"], writes=["scws%d" % l])
            P.dma(lambda e, l=l: e.dma_start(out=sc_ml[l], in_=mloc[:].rearrange("p g n -> p (g n)")), reads=["mloc"], writes=["scml%d" % l])
            P.dma(lambda e, l=l: e.dma_start(out=sc_ez[l], in_=ez[:].rearrange("p g r n -> p (g r n)")), reads=["ez"], writes=["scez%d" % l])
            P.dma(lambda e, l=l: e.dma_start(out=sc_tb[l], in_=tabs[:]), reads=["tabs"], writes=["sctb%d" % l])

        for l in reversed(range(DEPTH)):
            setup_layer(l)

        segs = [(b, sg, l) for b in range(NB) for sg in range(NSEG) for l in range(DEPTH)]
        if n_segs_limit is not None:
            segs = segs[:n_segs_limit]

        def load_wout_ln(l):
            for kq in range(4):
                P.dma(lambda e, l=l, kq=kq: e.dma_start(out=wout[:, 2 * kq:2 * kq + 2, :],
                                                        in_=dr["w_out"][l, kq * 256:(kq + 1) * 256, :].rearrange("(k p) n -> p k n", p=128)),
                      writes=["wout"], q="pool")
            P.dma(lambda e, l=l: e.dma_start(out=lnG[:], in_=dr["ln_g"][l].rearrange("(o n) -> o n", o=1).to_broadcast([128, D])), writes=["lnG"])
            P.dma(lambda e, l=l: e.dma_start(out=lnB[:], in_=dr["ln_b"][l].rearrange("(o n) -> o n", o=1).to_broadcast([128, D])), writes=["lnB"])

        def load_derived(l):
            P.dma(lambda e, l=l: e.dma_start(out=wsz[:].rearrange("p g r n -> p (g r n)"), in_=sc_ws[l]), reads=["scws%d" % l], writes=["wsz"])
            P.dma(lambda e, l=l: e.dma_start(out=mloc[:].rearrange("p g n -> p (g n)"), in_=sc_ml[l]), reads=["scml%d" % l], writes=["mloc"])
            P.dma(lambda e, l=l: e.dma_start(out=ez[:].rearrange("p g r n -> p (g r n)"), in_=sc_ez[l]), reads=["scez%d" % l], writes=["ez"])
            P.dma(lambda e, l=l: e.dma_start(out=tabs[:], in_=sc_tb[l]), reads=["sctb%d" % l], writes=["tabs"])

        def rstd_newton(y, v, t, ti, n, reads, wkey):
            P.op("dve", lambda e: e.tensor_scalar_add(out=v, in0=v, scalar1=EPS), reads=reads, writes=[wkey])
            P.op("dve", lambda e: e.tensor_single_scalar(out=ti, in_=v.bitcast(I32), scalar=1, op=ALU.arith_shift_right), reads=[wkey], writes=[wkey])
            P.op("dve", lambda e: e.tensor_scalar(out=y.bitcast(I32), in0=ti, scalar1=-1.0, scalar2=1597463007.0, op0=ALU.mult, op1=ALU.add), reads=[wkey], writes=[wkey])
            for it in range(3):
                P.op("dve", lambda e: e.tensor_tensor(out=t, in0=y, in1=y, op=ALU.mult), reads=[wkey], writes=[wkey])
                P.op("dve", lambda e: e.tensor_tensor(out=t, in0=t, in1=v, op=ALU.mult), reads=[wkey], writes=[wkey])
                P.op("dve", lambda e: e.tensor_scalar(out=t, in0=t, scalar1=-0.5, scalar2=1.5, op0=ALU.mult, op1=ALU.add), reads=[wkey], writes=[wkey])
                P.op("dve", lambda e: e.tensor_tensor(out=y, in0=y, in1=t, op=ALU.mult), reads=[wkey], writes=[wkey])

        ev = [0]

        def evac(out_ap, in_ap, reads, writes, func=None, eng=None, **kw):
            if func is not None:
                P.op("act", lambda e: e.activation(out=out_ap, in_=in_ap, func=func, **kw), reads=reads, writes=writes)
                return
            if eng is None:
                eng = "act"
            if eng == "act":
                P.op("act", lambda e: e.activation(out=out_ap, in_=in_ap, func=AF.Identity), reads=reads, writes=writes)
            else:
                P.op(eng, lambda e: e.tensor_copy(out=out_ap, in_=in_ap), reads=reads, writes=writes)

        t1, t2, t3, hdr = [x_bf[:, i, :].bitcast(F32) for i in range(4)]
        hdi = hdi_t[:, :]
        KS = "s5t"
        j3 = lambda a: a.rearrange("p (g j) -> p g j", g=8)

        class Stages:
            pass

        def x_load(si, tiles):
            b, sg, l = segs[si]
            tok0 = sg * SEG
            for tt in tiles:
                P.dma(lambda e, b=b, tt=tt, tok0=tok0: e.dma_start(out=x_tok[:, tt, :], in_=dr["x"][b, tok0 + tt * 128: tok0 + (tt + 1) * 128, :]),
                      writes=["x_tok%d" % tt])

        def seg_prologue_x(si):
            b, sg, l = segs[si]
            if l == 0:
                x_load(si, range(4))
            if l == 0 and sg == 0:
                for r in range(2):
                    P.op("dve", lambda e, r=r: e.memset(cH[0][r][:], 0.0), writes=["cH0"])
                    P.op("dve", lambda e, r=r: e.memset(cH[1][r][:], 0.0), writes=["cH1"])

        def seg_prologue(si):
            b, sg, l = segs[si]
            if sg == 0:
                P.dma(lambda e, b=b: e.dma_start(out=mem_bf, in_=dr["mem"][b].rearrange("(t p) d -> p t d", p=128)), writes=["memb"], q="pool")
                for kt in range(8):
                    pb, pbk = nb_()
                    for mt in range(2):
                        P.op("pe", lambda e, pb=pb, kt=kt, mt=mt: e.transpose(pb[:, mt * 128:(mt + 1) * 128], mem_bf[:, mt, kt * 128:(kt + 1) * 128], identb[:]),
                             reads=["memb", "identb"], writes=[pbk])
                    evac(memT[:, kt, :], pb[:, 0:256], [pbk], ["memT"])
                for c2 in range(2):
                    pf, pfk = nf()
                    for kt in range(8):
                        P.op("pe", lambda e, pf=pf, kt=kt, c2=c2: e.matmul(pf[:, 0:256], lhsT=wk[:, kt, c2 * 128:(c2 + 1) * 128], rhs=memT[:, kt, :],
                                                                     start=(kt == 0), stop=(kt == 7)), reads=["wk", "memT"], writes=[pfk])
                    for hh in range(2):
                        rs_ = slice(hh * 64, (hh + 1) * 64)
                        evac(kTm[l][rs_, 2 * c2 + hh, :], pf[rs_, 0:256], [pfk], ["kTm%d" % l])
                for mt in range(2):
                    pf, pfk = nf()
                    for kt in range(8):
                        P.op("pe", lambda e, pf=pf, kt=kt, mt=mt: e.matmul(pf[:, 0:256], lhsT=memT[:, kt, mt * 128:(mt + 1) * 128], rhs=wv[:, kt, :],
                                                                     start=(kt == 0), stop=(kt == 7)), reads=["wv", "memT"], writes=[pfk])
                    for h in range(4):
                        evac(vm[l][:, mt, h, (h % 2) * 64:(h % 2) * 64 + 64], pf[:, h * 64:(h + 1) * 64], [pfk], ["vm%d" % l])
                for sj in range(si + 1, len(segs)):
                    if segs[sj][1] == 0:
                        load_wkv(segs[sj][2])
                        break

        def make_block(si, blk):
            b, sg, l = segs[si]
            tok0 = sg * SEG
            T0 = blk * 4
            last_blk = (blk == NBLK - 1)
            nxt_l = segs[si + 1][2] if si + 1 < len(segs) else None
            S = Stages()
            S.si, S.blk, S.l, S.last_blk = si, blk, l, last_blk

            def prefetch(name):
                if last_blk and nxt_l is not None:
                    load_win_group(nxt_l, name)

            def A0a():
                engs = ["act", "act", "act", "act"]
                for tt in range(4):
                    if engs[tt] == "act":
                        P.op("act", lambda e, tt=tt: e.activation(out=x_bf[:, tt, :], in_=x_tok[:, T0 + tt, :], func=AF.Identity),
                             reads=["x_tok%d" % (T0 + tt)], writes=["x_bf%d" % tt])
                    else:
                        P.op(engs[tt], lambda e, tt=tt: e.tensor_copy(out=x_bf[:, tt, :], in_=x_tok[:, T0 + tt, :]),
                             reads=["x_tok%d" % (T0 + tt)], writes=["x_bf%d" % tt])

            def A0():
                for kt in range(8):
                    pb, pbk = nb_()
                    for tt in range(4):
                        P.op("pe", lambda e, pb=pb, kt=kt, tt=tt: e.transpose(pb[:, tt * 128:(tt + 1) * 128], x_bf[:, tt, kt * 128:(kt + 1) * 128], identb[:]),
                             reads=["x_bf%d" % tt, "identb"], writes=[pbk])
                    evac(xT[:, kt, :], pb[:, 0:512], [pbk], ["xT"])

            def A1(tt):
                pf, pfk = nf()
                for kt in range(8):
                    P.op("pe", lambda e, pf=pf, kt=kt: e.matmul(pf[:, :], lhsT=xT[:, kt, tt * 128:(tt + 1) * 128], rhs=win[:, kt, 512:1024],
                                                          start=(kt == 0), stop=(kt == 7)), reads=["xT", "win_v"], writes=[pfk])
                evac(vn[:, tt, :], pf[:, :], [pfk], ["vn"], func=AF.Gelu)
                v3 = vn[:, tt, :].rearrange("p (h d) -> p h d", h=4)
                P.op("dve", lambda e: e.tensor_reduce(out=lnst[:, 0, tt * 4:(tt + 1) * 4], in_=v3, axis=mybir.AxisListType.X, op=ALU.add),
                     reads=["vn"], writes=["lnst"])
                P.op("dve", lambda e: e.tensor_tensor(out=tmpA, in0=vn[:, tt, :], in1=vn[:, tt, :], op=ALU.mult), reads=["vn"], writes=["tmpA"])
                P.op("dve", lambda e: e.tensor_reduce(out=lnst[:, 1, tt * 4:(tt + 1) * 4], in_=tmpA.rearrange("p (h d) -> p h d", h=4), axis=mybir.AxisListType.X, op=ALU.add),
                     reads=["tmpA"], writes=["lnst"])

            def A1_finish():
                prefetch("v")
                inv = 1.0 / 128.0
                P.op("dve", lambda e: e.tensor_scalar_mul(out=mv[:, :, 0], in0=lnst[:, 0, :], scalar1=inv), reads=["lnst"], writes=["mv"])
                P.op("dve", lambda e: e.tensor_tensor(out=mv[:, :, 1], in0=mv[:, :, 0], in1=mv[:, :, 0], op=ALU.mult), reads=["mv"], writes=["mv"])
                P.op("dve", lambda e: e.scalar_tensor_tensor(out=mv[:, :, 1], in0=lnst[:, 1, :], scalar=inv, in1=mv[:, :, 1], op0=ALU.mult, op1=ALU.subtract),
                     reads=["lnst", "mv"], writes=["mv"])
                P.op("act", lambda e: e.activation(out=rstd[:], in_=mv[:, :, 1], func=AF.Sqrt, bias=epst[:, 0:1], scale=1.0), reads=["mv", "epst"], writes=["rstd"])
                P.op("dve", lambda e: e.reciprocal(out=rstd[:], in_=rstd[:]), reads=["rstd"], writes=["rstd"])
                for tt in range(4):
                    for h in range(4):
                        i = tt * 4 + h
                        P.op("dve", lambda e, tt=tt, h=h, i=i: e.tensor_scalar(out=vn[:, tt, h * 128:(h + 1) * 128], in0=vn[:, tt, h * 128:(h + 1) * 128],
                                                                               scalar1=mv[:, i, 0:1], scalar2=rstd[:, i:i + 1], op0=ALU.subtract, op1=ALU.mult),
                             reads=["vn", "mv", "rstd"], writes=["vn"])

            def inproj(grp, out_ap, wkey, func=None, sub=0):
                col0 = WIN_G[grp][0] + sub
                pf, pfk = nf()
                for kt in range(8):
                    P.op("pe", lambda e, pf=pf, kt=kt, col0=col0: e.matmul(pf[:, :], lhsT=win[:, kt, col0:col0 + 128], rhs=xT[:, kt, :],
                                                                     start=(kt == 0), stop=(kt == 7)), reads=["xT", "win_" + grp], writes=[pfk])
                evac(out_ap, pf[:, :], [pfk], [wkey], func=func)

            def inproj_gate(grp, out_ap, wkey, sub=0):
                col0 = WIN_G[grp][0] + sub
                pf, pfk = nf()
                for kt in range(8):
                    P.op("pe", lambda e, pf=pf, kt=kt, col0=col0: e.matmul(pf[:, :], lhsT=win[:, kt, col0:col0 + 128], rhs=xT[:, kt, :],
                                                                     start=(kt == 0), stop=(kt == 7)), reads=["xT", "win_" + grp], writes=[pfk])
                P.op("act", lambda e, pf=pf: e.activation(out=out_ap, in_=pf[:, :], func=AF.Tanh, scale=0.5), reads=[pfk], writes=[wkey])
                P.op("dve", lambda e, pf=pf: e.scalar_tensor_tensor(out=out_ap, in0=out_ap, scalar=1.0, in1=pf[:, :], op0=ALU.add, op1=ALU.mult),
                     reads=[pfk, wkey], writes=[wkey])

            def Head(h):
                ug_, gs_ = ugs[h % 2]
                uk, gk = "ug", "gs"
                inproj("u%d" % h, ug_, uk, AF.Gelu)
                prefetch("u%d" % h)
                inproj_gate("g%d" % h, gs_, gk)
                prefetch("g%d" % h)
                P.op("dve", lambda e: e.tensor_tensor(out=ug_, in0=ug_, in1=gs_, op=ALU.mult), reads=[uk, gk], writes=[uk])
                pf3, pfk3 = nf()
                for tt in range(4):
                    P.op("pe", lambda e, pf3=pf3, tt=tt: e.matmul(pf3[:, tt * 128:(tt + 1) * 128], lhsT=vn[:, tt, h * 128:(h + 1) * 128], rhs=wTg[l][:, h, :],
                                                            start=True, stop=True), reads=["vn", "wTg%d" % l], writes=[pfk3])
                P.op("dve", lambda e, pf3=pf3: e.scalar_tensor_tensor(
                    out=tmpA.rearrange("p (a t) -> p a t", a=4), in0=pf3[:, :].rearrange("p (a t) -> p a t", a=4), scalar=lng[l][:, h:h + 1],
                    in1=bias2[l][:, h, :].unsqueeze(1).to_broadcast([128, 4, 128]), op0=ALU.mult, op1=ALU.add),
                    reads=[pfk3, "lng%d" % l, "bias2_%d" % l], writes=["tmpA"])
                P.op("dve", lambda e: e.tensor_tensor(out=yT[:, h, :], in0=tmpA, in1=ug_, op=ALU.mult), reads=["tmpA", uk], writes=["yT%d" % h])

            def A3():
                for c in range(2):
                    inproj("xb", xbT[:, c, :], "xbT", sub=c * 128)
                for c in range(2):
                    inproj_gate("gb", gbs[:, c, :], "gbs", sub=c * 128)
                prefetch("xb"); prefetch("gb")

            def Attn(c):
                inproj("q%d" % c, qT, "qT")
                prefetch("q%d" % c)
                inproj_gate("gx%d" % c, gxs, "gxs")
                prefetch("gx%d" % c)
                for hh in range(2):
                    h = 2 * c + hh
                    for mt in range(2):
                        pfs, pfsk = nf()
                        P.op("pe", lambda e, pfs=pfs, h=h, mt=mt: e.matmul(pfs[:, :], lhsT=kTm[l][:, h, mt * 128:(mt + 1) * 128], rhs=qT,
                                                                     start=True, stop=True), reads=["kTm%d" % l, "qT"], writes=[pfsk])
                        evac(pT[:, mt, hh, :], pfs[:, :], [pfsk], ["pT"], func=AF.Exp, scale=0.125)
                pfo, pfok = nf()
                pfd, pfdk = nf()
                n = 0
                for hh in range(2):
                    h = 2 * c + hh
                    for mt in range(2):
                        P.op("pe", lambda e, h=h, mt=mt, hh=hh, n=n: e.matmul(pfo[:, :], lhsT=vm[l][:, mt, h, :], rhs=pT[:, mt, hh, :],
                                                                        start=(n == 0), stop=(n == 3)), reads=["vm%d" % l, "pT"], writes=[pfok])
                        n += 1
                n = 0
                for hh in range(2):
                    for mt in range(2):
                        P.op("pe", lambda e, mt=mt, hh=hh, n=n: e.matmul(pfd[:, :], lhsT=hmask[:, hh, :], rhs=pT[:, mt, hh, :],
                                                                   start=(n == 0), stop=(n == 3)), reads=["hmask", "pT"], writes=[pfdk])
                        n += 1
                P.op("dve", lambda e: e.reciprocal(out=recip, in_=pfd[:, :]), reads=[pfdk], writes=["recip"])
                P.op("dve", lambda e: e.tensor_tensor(out=recip, in0=recip, in1=gxs, op=ALU.mult), reads=["recip", "gxs"], writes=["recip"])
                P.op("dve", lambda e: e.tensor_tensor(out=yT[:, 6 + c, :], in0=pfo[:, :], in1=recip, op=ALU.mult),
                     reads=[pfok, "recip"], writes=["yT%d" % (6 + c)])

            def B1():
                for c in range(2):
                    pf, pfk = nf()
                    for gl in range(8):
                        for s in range(8):
                            P.op("pe", lambda e, pf=pf, gl=gl, s=s, c=c: e.matmul(pf[:, gl * J:(gl + 1) * J], lhsT=zu[:, gl, 112 - 16 * s:240 - 16 * s],
                                                                            rhs=xbT[:, c, s:BLK:8], start=(s == 0), stop=(s == 7)),
                                 reads=["zu", "xbT"], writes=[pfk])
                    evac(Xg[:, c * 8:(c + 1) * 8, :].rearrange("p g j -> p (g j)"), pf[:, :], [pfk], ["Xg"])

            pS = []

            def B2():
                for r in range(2):
                    pf, pfk = nf()
                    pS.append((pf, pfk))
                    for gp in range(8):
                        for gl in range(2):
                            g = 2 * gp + gl
                            P.op("pe", lambda e, pf=pf, g=g, gp=gp, gl=gl, r=r: e.matmul(pf[:, gp * J:(gp + 1) * J], lhsT=wsz[:, g, r, :], rhs=Xg[:, g, :],
                                                                                   start=(gl == 0), stop=(gl == 1)), reads=["wsz", "Xg"], writes=[pfk])

            def B3():
                (pSr, pSrk), (pSi, pSik) = pS
                P.op("dve", lambda e: e.tensor_tensor(out=t1, in0=pSr[:, :], in1=cosT, op=ALU.mult), reads=[pSrk, "tabs"], writes=[KS])
                P.op("dve", lambda e: e.tensor_tensor(out=t3, in0=pSi[:, :], in1=sinT, op=ALU.mult), reads=[pSik, "tabs"], writes=[KS])
                P.op("dve", lambda e: e.tensor_tensor(out=t1, in0=t1, in1=t3, op=ALU.add), reads=[KS], writes=[KS])
                P.op("dve", lambda e: e.tensor_tensor(out=t2, in0=pSi[:, :], in1=cosT, op=ALU.mult), reads=[pSik, "tabs", KS], writes=[KS])
                P.op("dve", lambda e: e.tensor_tensor(out=t3, in0=pSr[:, :], in1=sinT, op=ALU.mult), reads=[pSrk, "tabs", KS], writes=[KS])
                P.op("dve", lambda e: e.tensor_tensor(out=t2, in0=t2, in1=t3, op=ALU.subtract), reads=[KS], writes=[KS])

            def B4():
                for r in range(2):
                    P.op("dve", lambda e, r=r: e.tensor_copy(out=Hb[:, r, :, 0], in_=cH[l][r][:]), reads=["cH%d" % l], writes=["Hb"])
                for r, (src, dst) in enumerate(((t1, hdr), (t2, hdi))):
                    P.op("dve", lambda e, r=r: e.tensor_tensor(out=car[:, r, :], in0=cH[l][r][:], in1=r8, op=ALU.mult), reads=["cH%d" % l, "tabs"], writes=["car"])
                    P.op("dve", lambda e, r=r, src=src: e.tensor_tensor(out=j3(src)[:, :, 0], in0=j3(src)[:, :, 0], in1=car[:, r, :], op=ALU.add), reads=[KS, "car"], writes=[KS])
                    P.op("dve", lambda e, src=src, dst=dst: e.tensor_tensor_scan(out=dst, data0=decT, data1=src, initial=0.0, op0=ALU.mult, op1=ALU.add),
                         reads=[KS, "tabs"], writes=[KS])

            def B5():
                P.op("dve", lambda e: e.tensor_tensor(out=t1, in0=hdr, in1=cosT, op=ALU.mult), reads=[KS, "tabs"], writes=[KS])
                P.op("dve", lambda e: e.tensor_tensor(out=t3, in0=hdi, in1=sinT, op=ALU.mult), reads=[KS, "tabs"], writes=[KS])
                P.op("dve", lambda e: e.tensor_tensor(out=Hb[:, 0, :, 1:J + 1], in0=j3(t1), in1=j3(t3), op=ALU.subtract), reads=[KS], writes=["Hb"])
                P.op("dve", lambda e: e.tensor_tensor(out=cH[l][0][:], in0=j3(t1)[:, :, J - 1], in1=j3(t3)[:, :, J - 1], op=ALU.subtract), reads=[KS], writes=["cH%d" % l])
                P.op("dve", lambda e: e.tensor_tensor(out=t1, in0=hdi, in1=cosT, op=ALU.mult), reads=[KS, "tabs"], writes=[KS])
                P.op("dve", lambda e: e.tensor_tensor(out=t3, in0=hdr, in1=sinT, op=ALU.mult), reads=[KS, "tabs"], writes=[KS])
                P.op("dve", lambda e: e.tensor_tensor(out=Hb[:, 1, :, 1:J + 1], in0=j3(t1), in1=j3(t3), op=ALU.add), reads=[KS], writes=["Hb"])
                P.op("dve", lambda e: e.tensor_tensor(out=cH[l][1][:], in0=j3(t1)[:, :, J - 1], in1=j3(t3)[:, :, J - 1], op=ALU.add), reads=[KS], writes=["cH%d" % l])

            def B6():
                for c in range(2):
                    pf, pfk = nf()
                    for gi in range(8):
                        g = c * 8 + gi
                        gp = g // 2
                        P.op("pe", lambda e, pf=pf, gi=gi, g=g: e.matmul(pf[:, gi * J:(gi + 1) * J], lhsT=mloc[:, g, :], rhs=Xg[:, g, :], start=True, stop=False),
                             reads=["mloc", "Xg"], writes=[pfk])
                        P.op("pe", lambda e, pf=pf, gi=gi, g=g, gp=gp: e.matmul(pf[:, gi * J:(gi + 1) * J], lhsT=ez[:, g, 0, :], rhs=Hb[:, 0, gp, 0:J], start=False, stop=False),
                             reads=["ez", "Hb"], writes=[pfk])
                        P.op("pe", lambda e, pf=pf, gi=gi, g=g, gp=gp: e.matmul(pf[:, gi * J:(gi + 1) * J], lhsT=ez[:, g, 1, :], rhs=Hb[:, 1, gp, 0:J], start=False, stop=True),
                             reads=["ez", "Hb"], writes=[pfk])
                    evac(Yg[:, c * 8:(c + 1) * 8, :].rearrange("p g j -> p (g j)"), pf[:, :], [pfk], ["Yg"], func=AF.Gelu)
                if last_blk and nxt_l is not None:
                    load_derived(nxt_l)

            def B7():
                for c in range(2):
                    pf, pfk = nf()
                    for t in range(8):
                        for gl in range(8):
                            P.op("pe", lambda e, pf=pf, t=t, gl=gl, c=c: e.matmul(pf[:, t * J:(t + 1) * J], lhsT=zu[:, t, 112 - 16 * gl:240 - 16 * gl],
                                                                            rhs=Yg[:, c * 8 + gl, :], start=(gl == 0), stop=(gl == 7)),
                                 reads=["zu", "Yg"], writes=[pfk])
                    evac(ybT[:, c, :].rearrange("p (j t) -> p t j", t=8), pf[:, :].rearrange("p (t j) -> p t j", t=8), [pfk], ["ybT"])

            def B8():
                for co in range(2):
                    pf, pfk = nf()
                    for kc in range(2):
                        P.op("pe", lambda e, pf=pf, kc=kc, co=co: e.matmul(pf[:, :], lhsT=gluw[l][:, kc, co * 128:(co + 1) * 128], rhs=ybT[:, kc, :],
                                                                     start=(kc == 0), stop=(kc == 1)), reads=["gluw%d" % l, "ybT"], writes=[pfk])
                    P.op("act", lambda e, pf=pf, co=co: e.activation(out=sgt[:], in_=pf[:, :], func=AF.Tanh, bias=glub[l][:, co:co + 1], scale=0.5),
                         reads=[pfk, "glub%d" % l], writes=["sg"])
                    P.op("dve", lambda e, co=co: e.scalar_tensor_tensor(out=sgt[:], in0=sgt[:], scalar=1.0, in1=gbs[:, co, :], op0=ALU.add, op1=ALU.mult),
                         reads=["sg", "gbs"], writes=["sg"])
                    P.op("dve", lambda e, co=co: e.scalar_tensor_tensor(out=yT[:, 4 + co, :], in0=ybT[:, co, :], scalar=0.25, in1=sgt[:], op0=ALU.mult, op1=ALU.mult),
                         reads=["sg", "ybT"], writes=["yT%d" % (4 + co)])

            def C(tt):
                T = T0 + tt
                acc, ak = accs[tt % 2]
                for nh in range(2):
                    pf, pfk = nf()
                    for kt in range(8):
                        P.op("pe", lambda e, pf=pf, kt=kt, nh=nh: e.matmul(pf[:, :], lhsT=yT[:, kt, tt * 128:(tt + 1) * 128], rhs=wout[:, kt, nh * 512:(nh + 1) * 512],
                                                                     start=(kt == 0), stop=(kt == 7)), reads=["yT%d" % kt, "wout"], writes=[pfk])
                    P.op("dve", lambda e, pf=pf, nh=nh: e.scalar_tensor_tensor(out=acc[:, nh * 512:(nh + 1) * 512], in0=x_tok[:, T, nh * 512:(nh + 1) * 512], scalar=ALPHA,
                                                                       in1=pf[:, :], op0=ALU.mult, op1=ALU.add),
                         reads=[pfk, "x_tok%d" % T], writes=[ak])
                    P.op("dve", lambda e, nh=nh: e.bn_stats(out=stc[:, tt % 2, nh, :], in_=acc[:, nh * 512:(nh + 1) * 512]), reads=[ak], writes=["stc%d" % (tt % 2)])
                P.op("dve", lambda e: e.bn_aggr(out=mvc[:, tt % 2, :], in_=stc[:, tt % 2].rearrange("p a s -> p (a s)")), reads=["stc%d" % (tt % 2)], writes=["mvc%d" % (tt % 2)])
                rs_ = rsc[:, tt % 2, :]
                rk = "rsc%d" % (tt % 2)
                P.op("act", lambda e: e.activation(out=rs_[:, 0:1], in_=mvc[:, tt % 2, 1:2], func=AF.Sqrt, bias=epst[:, 0:1], scale=1.0), reads=["mvc%d" % (tt % 2), "epst"], writes=[rk])
                P.op("dve", lambda e: e.reciprocal(out=rs_[:, 0:1], in_=rs_[:, 0:1]), reads=[rk], writes=[rk])
                P.op("dve", lambda e: e.scalar_tensor_tensor(out=rs_[:, 1:2], in0=mvc[:, tt % 2, 0:1], scalar=-1.0, in1=rs_[:, 0:1], op0=ALU.mult, op1=ALU.mult),
                     reads=["mvc%d" % (tt % 2), rk], writes=[rk])
                P.op("act", lambda e: e.activation(out=acc, in_=acc, func=AF.Identity, bias=rs_[:, 1:2], scale=rs_[:, 0:1]), reads=[ak, rk], writes=[ak])

            def Cb(tt):
                T = T0 + tt
                acc, ak = accs[tt % 2]
                P.op("dve", lambda e: e.tensor_tensor(out=acc, in0=acc, in1=lnG[:], op=ALU.mult), reads=[ak, "lnG"], writes=[ak])
                P.op("dve", lambda e: e.tensor_tensor(out=x_tok[:, T, :], in0=acc, in1=lnB[:], op=ALU.add), reads=[ak, "lnB"], writes=["x_tok%d" % T])
                if l == DEPTH - 1:
                    P.dma(lambda e: e.dma_start(out=out[b, tok0 + T * 128: tok0 + (T + 1) * 128, :], in_=x_tok[:, T, :]),
                          reads=["x_tok%d" % T])

            S.A0, S.A1, S.A1_finish, S.A3, S.Head, S.Attn = A0, A1, A1_finish, A3, Head, Attn
            S.A0a = A0a
            S.B1, S.B2, S.B3, S.B4, S.B5, S.B6, S.B7, S.B8, S.C = B1, B2, B3, B4, B5, B6, B7, B8, C
            S.Cb = Cb
            return S

        blocks = [(si, blk) for si in range(len(segs)) for blk in range(NBLK)]
        l0 = segs[0][2]
        load_wout_ln(l0)
        seg_prologue_x(0)
        seg_prologue(0)
        cur = make_block(*blocks[0])
        cur.A0a()
        cur.A0()
        prev = None
        for k in range(len(blocks)):
            si, blk = blocks[k]
            nxt_blk = None
            if k + 1 < len(blocks):
                nsi, nblk = blocks[k + 1]
                nxt_blk = make_block(nsi, nblk)
            for tt in range(4):
                cur.A1(tt)
            cur.A1_finish()
            cur.A3(); cur.B1()
            if nxt_blk is not None and nblk == 0:
                seg_prologue(nsi)
            if prev is not None:
                prev.C(0); prev.C(1); prev.Cb(0); prev.C(2); prev.Cb(1); prev.C(3); prev.Cb(2); prev.Cb(3)
            if prev is not None and prev.last_blk:
                load_wout_ln(segs[si][2])
                if segs[si][2] == 0:
                    x_load(si, range(4, 8))
            elif prev is None and segs[si][2] == 0:
                x_load(si, range(4, 8))
            cur.Head(0); cur.B2(); cur.B3(); cur.Head(1); cur.B4(); cur.Head(2); cur.B5()
            cur.Head(3); cur.B6(); cur.B7()
            cur.Attn(0)
            if nxt_blk is not None:
                if nblk == 0:
                    seg_prologue_x(nsi)
                nxt_blk.A0a()
            cur.B8(); cur.Attn(1)
            if nxt_blk is not None:
                nxt_blk.A0()
            prev, cur = cur, nxt_blk
        for tt in range(4):
            prev.C(tt)
            prev.Cb(tt)
        P.emit()
    nc._plan = P
    return nc


_CACHE = {}


def kernel(**inputs):
    x = np.ascontiguousarray(inputs["x"], dtype=np.float32)
    mem = np.ascontiguousarray(inputs["mem"], dtype=np.float32)
    consts = host_consts()
    if "nc" not in _CACHE:
        _CACHE["nc"] = build_program()
    nc = _CACHE["nc"]
    in_maps = []
    for i in range(N_CORES):
        m = {"x": x[NB * i:NB * (i + 1)], "mem": mem[NB * i:NB * (i + 1)]}
        for k in W_SHAPES:
            m[k] = np.ascontiguousarray(inputs[k], dtype=np.float32)
        m.update(consts)
        in_maps.append(m)
    res = run_bass_kernel_spmd(nc, in_maps, core_ids=list(range(N_CORES)))
    outp = np.concatenate([r["out"] for r in res.results], axis=0)
    return outp.astype(np.float32)
```

```python
import math
import contextlib
import numpy as np
import concourse.bass as bass
import concourse.mybir as mybir
from concourse.bass_utils import run_bass_kernel_spmd

F32 = mybir.dt.float32
BF16 = mybir.dt.bfloat16
I32 = mybir.dt.int32
AF = mybir.ActivationFunctionType
ALU = mybir.AluOpType

N_CORES = 8
NB = 2
SEQ = 2048
D = 1024
MEM = 256
DEPTH = 2
SEG = 1024
NSEG = SEQ // SEG
BLK = 512
NBLK = SEG // BLK
J = BLK // 8
ALPHA = (2 * DEPTH) ** 0.25
EPS = 1e-5
TWO_PI = 2.0 * math.pi


class Plan:
    ENGS = ("pe", "act", "dve", "pool", "sp")

    def __init__(self, nc):
        self.nc = nc
        self.ops = {e: [] for e in self.ENGS}
        self.count = {e: 0 for e in self.ENGS}
        self.last_write = {}
        self.readers = {}
        self.known = {e: {} for e in self.ENGS}
        self.dma_sems = {}
        self.alias = {}

    def set_alias(self, a, others):
        for o in others:
            self.alias.setdefault(a, set()).add(o)
            self.alias.setdefault(o, set()).add(a)

    def _expand(self, keys):
        out = []
        for k in keys:
            out.append(k)
            for a in self.alias.get(k, ()):
                out.append(a)
        return out

    def _add(self, waits, name, val):
        if waits.get(name, 0) < val:
            waits[name] = val

    def _add_dep(self, waits, dep):
        if dep is None:
            return
        if dep[0] == "eng":
            self._add(waits, "c_" + dep[1], dep[2])
        else:
            self._add(waits, dep[1], dep[2])

    def _deps(self, eng, reads, writes):
        waits = {}
        for k in self._expand(reads):
            self._add_dep(waits, self.last_write.get(k))
        for k in self._expand(writes):
            self._add_dep(waits, self.last_write.get(k))
            for e, idx in self.readers.get(k, {}).items():
                if e.startswith("dma:"):
                    self._add(waits, e[4:], idx)
                else:
                    self._add(waits, "c_" + e, idx)
        if eng == "pe":
            waits.pop("c_pe", None)
        out = {}
        kn = self.known[eng]
        for name, val in waits.items():
            if kn.get(name, 0) >= val:
                continue
            kn[name] = val
            out[name] = val
        return out

    def op(self, eng, fn, reads=(), writes=(), tag=None):
        waits = self._deps(eng, reads, writes)
        if tag is not None:
            self.tags = getattr(self, "tags", {})
            self.tags.setdefault(tag, []).append((eng, dict(waits), dict(self.known[eng]), {k: self.last_write.get(k) for k in self._expand(reads)}))
        self.count[eng] += 1
        idx = self.count[eng]
        self.ops[eng].append((waits, fn, [("c_" + eng, 1)]))
        for k in reads:
            self.readers.setdefault(k, {})[eng] = idx
        for k in writes:
            self.last_write[k] = ("eng", eng, idx)
            self.readers[k] = {}
        return idx

    def dma(self, fn, reads=(), writes=(), sem=None, q="sp"):
        if len(writes) > 0:
            sem = "d_" + str(writes[0])
        else:
            sem = "d_o_" + str(reads[0])
        waits = self._deps(q, reads, writes)
        self.dma_sems[sem] = self.dma_sems.get(sem, 0) + 16
        val = self.dma_sems[sem]
        self.ops[q].append((waits, fn, [(sem, 16)]))
        for k in writes:
            self.last_write[k] = ("dma", sem, val)
            self.readers[k] = {}
        for k in reads:
            self.readers.setdefault(k, {})["dma:" + sem] = val
        return val

    def emit(self):
        nc = self.nc
        names = set(["c_" + e for e in self.ENGS]) | set(self.dma_sems.keys())
        with contextlib.ExitStack() as st:
            sems = {n: st.enter_context(nc.semaphore(n)) for n in sorted(names)}
            block = st.enter_context(nc.Block())

            def replay(ename):
                def body(eng):
                    for waits, fn, incs in self.ops[ename]:
                        for n, v in waits.items():
                            eng.wait_ge(sems[n], v)
                        ins = fn(eng)
                        for n, v in incs:
                            ins.then_inc(sems[n], v)
                    if ename == "sp":
                        for n, v in self.dma_sems.items():
                            eng.wait_ge(sems[n], v)
                return body

            block.tensor(replay("pe"))
            block.scalar(replay("act"))
            block.vector(replay("dve"))
            block.gpsimd(replay("pool"))
            block.sync(replay("sp"))


def host_consts():
    c = {}
    c["c_ident"] = np.eye(128, dtype=np.float32)
    zu = np.zeros((128, 8, 240), np.float32)
    for gl in range(8):
        for cc in range(16):
            zu[16 * gl + cc, gl, 112 + cc] = 1.0
    c["c_zu"] = zu.reshape(128, 8 * 240)
    sc = np.arange(128) // 16
    cm = (sc[None, :] >= sc[:, None]).astype(np.float32)
    c["c_cm"] = cm
    tri = (np.arange(128)[None, :] <= np.arange(128)[:, None]).astype(np.float32)
    c["c_tri"] = tri
    im = np.zeros((128, 2, 128), np.float32)
    for k in range(128):
        im[k, k // 64, k] = 1.0
    c["c_imask"] = im.reshape(128, 256)
    kvals = np.concatenate([np.arange(7, -1, -1), np.arange(-7, 1), np.arange(1, 9)]).astype(np.float32)
    kv = np.tile(kvals[None, None, :], (128, 8, 1))
    c["c_kv3"] = kv.reshape(128, 192)
    c["c_jv"] = np.tile(np.arange(1, J + 1, dtype=np.float32)[None, :], (128, 1))
    hm = np.zeros((128, 2, 128), np.float32)
    hm[:, 0, 0:64] = 2.0
    hm[:, 1, 64:128] = 2.0
    c["c_hmask"] = hm.reshape(128, 256)
    return c


CONST_SHAPES = {"c_ident": [128, 128], "c_zu": [128, 1920], "c_cm": [128, 128], "c_tri": [128, 128],
                "c_imask": [128, 256], "c_kv3": [128, 192], "c_jv": [128, J], "c_hmask": [128, 256]}

W_SHAPES = {
    "w_in": [DEPTH, D, 2560], "gm_w_s": [DEPTH, 4, 128, 128], "gm_b_s": [DEPTH, 4, 128],
    "gm_ln_g": [DEPTH, 4, 128], "gm_ln_b": [DEPTH, 4, 128], "ssm_lam_re": [DEPTH, 16, 64],
    "ssm_lam_im": [DEPTH, 16, 64], "ssm_log_step": [DEPTH, 16], "ssm_b_re": [DEPTH, 16, 64, 16],
    "ssm_b_im": [DEPTH, 16, 64, 16], "ssm_c_re": [DEPTH, 16, 16, 64], "ssm_c_im": [DEPTH, 16, 16, 64],
    "ssm_d": [DEPTH, 256], "glu_w": [DEPTH, 256, 256], "glu_b": [DEPTH, 256],
    "xa_w_k": [DEPTH, D, 256], "xa_w_v": [DEPTH, D, 256], "w_out": [DEPTH, D, D],
    "ln_g": [DEPTH, D], "ln_b": [DEPTH, D],
}


SBUF_FREE = [None]


def build_program(dbg=False, n_segs_limit=None):
    nc = bass.Bass("TRN2", target_bir_lowering=False)
    dr = {}
    dr["x"] = nc.dram_tensor("x", [NB, SEQ, D], F32, kind="ExternalInput").ap()
    dr["mem"] = nc.dram_tensor("mem", [NB, MEM, D], F32, kind="ExternalInput").ap()
    for k, shp in W_SHAPES.items():
        dr[k] = nc.dram_tensor(k, shp, F32, kind="ExternalInput").ap()
    for k, shp in CONST_SHAPES.items():
        dr[k] = nc.dram_tensor(k, shp, F32, kind="ExternalInput").ap()
    out = nc.dram_tensor("out", [NB, SEQ, D], F32, kind="ExternalOutput").ap()
    sck = dict(kind="ExternalOutput") if dbg else {}
    sc_ws = [nc.dram_tensor("sc_ws%d" % l, [128, 16 * 2 * 128], BF16, **sck).ap() for l in range(DEPTH)]
    sc_ml = [nc.dram_tensor("sc_ml%d" % l, [128, 16 * 128], BF16, **sck).ap() for l in range(DEPTH)]
    sc_ez = [nc.dram_tensor("sc_ez%d" % l, [128, 16 * 2 * 128], BF16, **sck).ap() for l in range(DEPTH)]
    sc_tb = [nc.dram_tensor("sc_tb%d" % l, [128, 3 * 8 * J + 8], F32, **sck).ap() for l in range(DEPTH)]
    if dbg:
        dbg_y = nc.dram_tensor("dbg_y", [128, 8 * BLK], F32, kind="ExternalOutput").ap()
        dbg_x = nc.dram_tensor("dbg_x", [128, 4 * D], F32, kind="ExternalOutput").ap()
    taps = {}

    def tap(P, name, ap, n, dt, reads):
        if not dbg:
            return
        t = nc.dram_tensor(name, [128, n], dt, kind="ExternalOutput").ap()
        P.dma(lambda e: e.dma_start(out=t, in_=ap), reads=reads, sem="d_dbg")

    P = Plan(nc)
    with contextlib.ExitStack() as st:
        def sb(name, shape, dt):
            return st.enter_context(nc.sbuf_tensor(name, shape, dt))

        def ps(name, shape, dt):
            return st.enter_context(nc.psum_tensor(name, shape, dt))

        psf = [ps("psf%d" % i, [128, 512], F32) for i in range(6)]
        psb = [ps("psb%d" % i, [128, 1024], BF16) for i in range(2)]
        rot = {"f": 0, "b": 0}

        def nf():
            i = rot["f"]; rot["f"] = (i + 1) % 6
            return psf[i], "psf%d" % i

        def nb_():
            i = rot["b"]; rot["b"] = (i + 1) % 2
            return psb[i], "psb%d" % i

        identf = sb("identf", [128, 128], F32)
        identb = sb("identb", [128, 128], BF16)
        zu = sb("zu", [128, 8, 240], BF16)
        cm = sb("cm", [128, 128], F32)
        imask = sb("imask", [128, 2, 128], BF16)
        hmask = sb("hmask", [128, 2, 128], BF16)
        onesb = sb("onesb", [128, 128], BF16)
        onesf = sb("onesf", [1, 128], F32)
        epst = sb("epst", [128, 1], F32)
        win = sb("win", [128, 8, 2560], BF16)
        wout = sb("wout", [128, 8, 1024], BF16)
        wk = sb("wk", [128, 8, 256], BF16)
        wv = sb("wv", [128, 8, 256], BF16)
        wsz = sb("wsz", [128, 16, 2, 128], BF16)
        mloc = sb("mloc", [128, 16, 128], BF16)
        ez = sb("ez", [128, 16, 2, 128], BF16)
        tabs = sb("tabs", [128, 3 * 8 * J + 8], F32)
        cosT = tabs[:, 0:8 * J]
        sinT = tabs[:, 8 * J:16 * J]
        r8 = tabs[:, 16 * J:16 * J + 8]
        decT = tabs[:, 16 * J + 8:24 * J + 8]
        lnG = sb("lnG", [128, D], F32)
        lnB = sb("lnB", [128, D], F32)
        wTg = [sb("wTg%d" % l, [128, 4, 128], BF16) for l in range(DEPTH)]
        bias2 = [sb("bias2_%d" % l, [128, 4, 128], F32) for l in range(DEPTH)]
        lng = [sb("lng%d" % l, [128, 4], F32) for l in range(DEPTH)]
        gluw = [sb("gluw%d" % l, [128, 2, 256], BF16) for l in range(DEPTH)]
        glub = [sb("glub%d" % l, [128, 2], F32) for l in range(DEPTH)]
        kTm = [sb("kTm%d" % l, [128, 4, 256], BF16) for l in range(DEPTH)]
        vm = [sb("vm%d" % l, [128, 2, 4, 128], BF16) for l in range(DEPTH)]
        cH = [[sb("cH%d_%d" % (l, r), [128, 8], F32) for r in range(2)] for l in range(DEPTH)]
        x_tok = sb("x_tok", [128, 8, D], F32)
        x_bf = sb("x_bf", [128, 4, D], BF16)
        xbT = sb("xbT", [128, 2, BLK], BF16)
        gbs = sb("gbs", [128, 2, BLK], BF16)
        yT = sb("yT", [128, 8, BLK], BF16)
        arX = sb("arX", [128, 6144], BF16)
        xT = arX[:, 0:4096].rearrange("p (k n) -> p k n", k=8)
        vn = arX[:, 4096:6144].rearrange("p (t n) -> p t n", t=4)
        hdi_t = sb("hdi_t", [128, 8 * J], F32)
        P.set_alias("s5t", ["x_bf0", "x_bf1", "x_bf2", "x_bf3"])
        mem_bf = x_bf[:, 0:2, :]
        memT = x_bf[:, 2:4, :].rearrange("p a n -> p (a n)").rearrange("p (k n) -> p k n", k=8)
        P.set_alias("memb", ["x_bf0", "x_bf1", "s5t"]); P.set_alias("memT", ["x_bf2", "x_bf3", "s5t"])
        arU = sb("arU", [128, 1024], BF16)
        ug = arU[:, 0:512]
        gs = arU[:, 512:1024]
        ugs = [(arU[:, 0:512], arU[:, 512:1024]), (arU[:, 0:512], arU[:, 512:1024])]
        Xg_t = sb("Xg_t", [128, 1024], BF16)
        Xg = Xg_t[:, :].rearrange("p (g j) -> p g j", g=16)
        arQ = sb("arQ", [128, 1024], BF16)
        qT = arQ[:, 0:512]
        gxs = arQ[:, 512:1024]
        Yg = arQ[:, :].rearrange("p (g j) -> p g j", g=16)
        P.set_alias("Yg", ["qT", "gxs"])
        arR = sb("arR", [128, 1056], BF16)
        recip = arR[:, 0:1024].bitcast(F32)
        Hb = arR[:, 0:2 * 8 * (J + 1)].rearrange("p (r g j) -> p r g j", r=2, g=8)
        P.set_alias("Hb", ["recip"])
        arV = sb("arV", [128, 1024], BF16)
        tmpA = arV[:, 0:512]
        ybT = arV[:, :].rearrange("p (c n) -> p c n", c=2)
        P.set_alias("ybT", ["tmpA"])
        sgt = sb("sgt", [128, BLK], BF16)
        arP = sb("arP", [128, 2048], BF16)
        pT = arP[:, :].rearrange("p (m h n) -> p m h n", m=2, h=2)
        acc0 = arP[:, :].bitcast(F32)
        acc1_t = sb("acc1_t", [128, D], F32)
        accs = [(acc0, "acc"), (acc1_t[:, :], "acc1")]
        P.set_alias("acc", ["pT"])
        st6 = sb("st6", [128, 16, 6], F32)
        mv = sb("mv", [128, 16, 2], F32)
        rstd = sb("rstd", [128, 16], F32)
        car = sb("car", [128, 2, 8], F32)
        lnst = sb("lnst", [128, 2, 16], F32)
        stc = sb("stc", [128, 2, 2, 6], F32)
        mvc = sb("mvc", [128, 2, 2], F32)
        rsc = sb("rsc", [128, 2, 2], F32)
        nw_v = sb("nw_v", [128, 18], F32)
        nw_t = sb("nw_t", [128, 18], F32)
        nw_i = sb("nw_i", [128, 18], I32)

        P.set_alias("setup", ["x_tok%d" % i for i in range(8)] + ["x_bf%d" % i for i in range(4)] + ["yT%d" % i for i in range(8)]
                    + ["x_bf", "xT", "vn", "s5t", "memb", "memT", "wout", "sin0", "sin1", "setupc", "RmZ"])
        sp_sem_ct = [0]
        SBUF_FREE[0] = nc.sbuf_bytes_remaining

        def sem_name(base):
            return "d_" + base

        P.dma(lambda e: e.dma_start(out=identf[:], in_=dr["c_ident"]), writes=["identf"], sem="d_c0")
        P.dma(lambda e: e.dma_start(out=cm[:], in_=dr["c_cm"]), writes=["cm"], sem="d_c1")
        P.dma(lambda e: e.dma_start(out=identb[:], in_=dr["c_ident"]), writes=["identb"], sem="d_c2", q="pool")
        P.dma(lambda e: e.dma_start(out=zu[:].rearrange("p g n -> p (g n)"), in_=dr["c_zu"]), writes=["zu"], sem="d_c3", q="pool")
        P.dma(lambda e: e.dma_start(out=imask[:].rearrange("p g n -> p (g n)"), in_=dr["c_imask"]), writes=["imask"], sem="d_c4", q="pool")
        P.dma(lambda e: e.dma_start(out=hmask[:].rearrange("p g n -> p (g n)"), in_=dr["c_hmask"]), writes=["hmask"], sem="d_c5", q="pool")
        P.op("dve", lambda e: e.memset(onesb[:], 1.0), writes=["onesb"])
        P.op("dve", lambda e: e.memset(onesf[:], 1.0), writes=["onesf"])
        P.op("dve", lambda e: e.memset(epst[:], EPS), writes=["epst"])
        for l in range(DEPTH):
            P.op("dve", lambda e, l=l: e.memset(kTm[l][:], 0.0), writes=["kTm%d" % l])
            P.op("pool", lambda e, l=l: e.memset(vm[l][:], 0.0), writes=["vm%d" % l])

        WIN_GROUPS = [("v", 512, 512), ("xb", 1536, 256), ("gb", 1792, 256)]
        for h in range(4):
            WIN_GROUPS += [("u%d" % h, h * 128, 128), ("g%d" % h, 1024 + h * 128, 128)]
        for c in range(2):
            WIN_GROUPS += [("q%d" % c, 2048 + c * 128, 128), ("gx%d" % c, 2304 + c * 128, 128)]
        WIN_G = {n: (c0, nc_) for n, c0, nc_ in WIN_GROUPS}

        def load_win_group(l, name):
            c0, ncol = WIN_G[name]
            P.dma(lambda e, l=l, c0=c0, ncol=ncol: e.dma_start(out=win[:, :, c0:c0 + ncol],
                                                               in_=dr["w_in"][l, :, c0:c0 + ncol].rearrange("(k p) n -> p k n", p=128)),
                  writes=["win_" + name], q="pool")

        def load_wkv(l):
            P.dma(lambda e, l=l: e.dma_start(out=wk[:], in_=dr["xa_w_k"][l].rearrange("(k p) n -> p k n", p=128)), writes=["wk"], q="pool")
            P.dma(lambda e, l=l: e.dma_start(out=wv[:], in_=dr["xa_w_v"][l].rearrange("(k p) n -> p k n", p=128)), writes=["wv"], q="pool")

        for name, _, _ in WIN_GROUPS:
            load_win_group(0, name)
        load_wkv(0)

        xt_flat = x_tok[:].rearrange("p t d -> p (t d)")
        wo_flat = wout[:].rearrange("p k n -> p (k n)").bitcast(F32)
        off = [0]
        off2 = [0]

        def tmp32(n):
            a = xt_flat[:, off[0]:off[0] + n]
            off[0] += n
            assert off[0] <= 8192, off[0]
            return a

        def tmpw(n):
            a = wo_flat[:, off2[0]:off2[0] + n]
            off2[0] += n
            assert off2[0] <= 4096
            return a

        NK = 24
        kv3 = tmp32(8 * NK)
        jv = tmp32(J)
        tri = tmp32(128)
        P.dma(lambda e: e.dma_start(out=kv3, in_=dr["c_kv3"]), writes=["setupc"])
        P.dma(lambda e: e.dma_start(out=jv, in_=dr["c_jv"]), writes=["setupc"])
        P.dma(lambda e: e.dma_start(out=tri, in_=dr["c_tri"]), writes=["setupc"])

        INP = []
        QS = ["sp", "act"]
        qi = [0]

        def qdma(fn, SK):
            P.dma(fn, writes=[SK], q=QS[qi[0] % 2])
            qi[0] += 1

        INP = [None] * DEPTH
        for l in reversed(range(DEPTH)):
            I_ = {}
            SK = "sin%d" % l
            I_["wraw"] = tmp32(512); I_["bs_bc"] = tmp32(512)
            I_["PA"] = tmp32(128); I_["LT"] = tmp32(128); I_["L16"] = tmp32(16)
            I_["Bre"] = tmp32(128); I_["Bim"] = tmp32(128); I_["CTr"] = tmp32(256); I_["CTi"] = tmp32(256)
            INP[l] = I_
            qdma(lambda e, l=l, a=I_["wraw"]: e.dma_start(out=a.rearrange("p (h s) -> p h s", h=4), in_=dr["gm_w_s"][l].rearrange("h t s -> t h s")), SK)
            qdma(lambda e, l=l, a=I_["bs_bc"]: e.dma_start(out=a, in_=dr["gm_b_s"][l].rearrange("(o h) d -> o (h d)", o=1).to_broadcast([128, 512])), SK)
            qdma(lambda e, l=l, a=I_["PA"]: e.dma_start(out=a[0:4, :], in_=dr["gm_ln_b"][l]), SK)
            qdma(lambda e, l=l, a=I_["PA"]: e.dma_start(out=a[4:8, :], in_=dr["gm_ln_g"][l]), SK)
            qdma(lambda e, l=l, a=I_["PA"]: e.dma_start(out=a[8:10, :], in_=dr["glu_b"][l].rearrange("(c p) -> c p", p=128)), SK)
            P.dma(lambda e, l=l: e.dma_start(out=gluw[l][:], in_=dr["glu_w"][l].rearrange("(k p) n -> p k n", p=128)), writes=["gluw%d" % l], q="pool")
            qdma(lambda e, l=l, a=I_["LT"]: e.dma_start(out=a[0:16, :].rearrange("g (o p) -> g o p", o=2),
                                                      in_=dr["ssm_lam_re"][l].unsqueeze(1).to_broadcast([16, 2, 64])), SK)
            qdma(lambda e, l=l, a=I_["LT"]: e.dma_start(out=a[16:32, :].rearrange("g (o p) -> g o p", o=2),
                                                      in_=dr["ssm_lam_im"][l].unsqueeze(1).to_broadcast([16, 2, 64])), SK)
            qdma(lambda e, l=l, a=I_["LT"]: e.dma_start(out=a[32:48, :].rearrange("g (s c) -> g s c", s=8),
                                                      in_=dr["ssm_d"][l].rearrange("(g o c) -> g o c", o=1, c=16).to_broadcast([16, 8, 16])), SK)
            qdma(lambda e, l=l, a=I_["L16"]: e.dma_start(out=a, in_=dr["ssm_log_step"][l].rearrange("(o g) -> o g", o=1).to_broadcast([128, 16])), SK)
            qdma(lambda e, l=l, a=I_["Bre"]: e.dma_start(out=a.rearrange("p (g c) -> p g c", g=8),
                                                       in_=dr["ssm_b_re"][l].rearrange("(gp gl) p c -> (gl p) gp c", gl=2)), SK)
            qdma(lambda e, l=l, a=I_["Bim"]: e.dma_start(out=a.rearrange("p (g c) -> p g c", g=8),
                                                       in_=dr["ssm_b_im"][l].rearrange("(gp gl) p c -> (gl p) gp c", gl=2)), SK)
            for nm, key in (("ssm_c_re", "CTr"), ("ssm_c_im", "CTi")):
                for t in range(2):
                    qdma(lambda e, l=l, t=t, nm=nm, a=I_[key]: e.dma_start(
                        out=a[:, t * 128:(t + 1) * 128].rearrange("r (o p) -> r o p", o=2),
                        in_=dr[nm][l].rearrange("(t gi) c p -> t (gi c) p", t=2)[t].unsqueeze(1).to_broadcast([128, 2, 64])), SK)
        base_off = off[0]

        def setup_layer(l):
            off[0] = base_off
            off2[0] = 0
            K = "setup"
            SK = "sin%d" % l
            RK = [K, SK, "setupc"]
            I_ = INP[l]
            Bre, Bim = I_["Bre"], I_["Bim"]
            lre = tmp32(8); lim = tmp32(8); lst = tmp32(8); dcol = tmp32(16); lnb_col = tmp32(4)
            pf, pfk = nf()
            P.op("pe", lambda e, pf=pf: e.transpose(pf[:, 0:10], I_["PA"][0:10, :], identf[0:10, 0:10]), reads=RK + ["identf"], writes=[pfk])
            P.op("dve", lambda e, pf=pf: e.tensor_copy(out=lnb_col, in_=pf[:, 0:4]), reads=[pfk], writes=[K])
            P.op("dve", lambda e, pf=pf: e.tensor_scalar_mul(out=lng[l][:], in0=pf[:, 4:8], scalar1=0.5), reads=[pfk], writes=["lng%d" % l])
            P.op("dve", lambda e, pf=pf: e.tensor_scalar_mul(out=glub[l][:], in0=pf[:, 8:10], scalar1=0.5), reads=[pfk], writes=["glub%d" % l])
            pf2, pfk2 = nf()
            P.op("pe", lambda e, pf2=pf2: e.transpose(pf2[:, 0:48], I_["LT"][0:48, :], identf[0:48, 0:48]), reads=RK + ["identf"], writes=[pfk2])
            for gl in range(2):
                rs_ = slice(gl * 64, (gl + 1) * 64)
                P.op("dve", lambda e, pf2=pf2, gl=gl, rs_=rs_: e.tensor_copy(out=lre[rs_, :], in_=pf2[rs_, gl:16:2]), reads=[pfk2], writes=[K])
                P.op("dve", lambda e, pf2=pf2, gl=gl, rs_=rs_: e.tensor_copy(out=lim[rs_, :], in_=pf2[rs_, 16 + gl:32:2]), reads=[pfk2], writes=[K])
                P.op("dve", lambda e, gl=gl, rs_=rs_: e.tensor_copy(out=lst[rs_, :], in_=I_["L16"][rs_, gl:16:2]), reads=RK, writes=[K])
            P.op("dve", lambda e, pf2=pf2: e.tensor_copy(out=dcol, in_=pf2[:, 32:48]), reads=[pfk2], writes=[K])
            wraw = I_["wraw"]
            wmb = x_bf[:, 0, 0:512]
            P.op("dve", lambda e, wraw=wraw, wmb=wmb: e.tensor_tensor(
                out=wmb.rearrange("p (h s) -> p h s", h=4), in0=wraw.rearrange("p (h s) -> p h s", h=4),
                in1=tri.unsqueeze(1).to_broadcast([128, 4, 128]), op=ALU.mult), reads=RK, writes=["x_bf"])
            pb, pbk = nb_()
            for h in range(4):
                P.op("pe", lambda e, h=h, pb=pb, wmb=wmb: e.transpose(pb[:, h * 128:(h + 1) * 128], wmb[:, h * 128:(h + 1) * 128], identb[:]),
                     reads=["x_bf", "identb"], writes=[pbk])
            P.op("dve", lambda e, l=l, pb=pb: e.tensor_copy(out=wTg[l][:].rearrange("p h t -> p (h t)"), in_=pb[:, 0:512]),
                 reads=[pbk], writes=["wTg%d" % l])
            pf, pfk = nf()
            P.op("pe", lambda e, l=l, pf=pf: e.matmul(pf[:, :], lhsT=onesb[:, :], rhs=wTg[l][:].rearrange("p h t -> p (h t)"),
                                                      start=True, stop=True), reads=["onesb", "wTg%d" % l], writes=[pfk])
            for h in range(4):
                P.op("dve", lambda e, l=l, h=h, pf=pf, bs_bc=I_["bs_bc"], lnb_col=lnb_col: e.scalar_tensor_tensor(
                    out=bias2[l][:, h, :], in0=pf[:, h * 128:(h + 1) * 128], scalar=lnb_col[:, h:h + 1], in1=bs_bc[:, h * 128:(h + 1) * 128],
                    op0=ALU.mult, op1=ALU.add), reads=[pfk] + RK, writes=["bias2_%d" % l])
            P.op("dve", lambda e, l=l: e.tensor_scalar_mul(out=bias2[l][:].rearrange("p h t -> p (h t)"), in0=bias2[l][:].rearrange("p h t -> p (h t)"), scalar1=0.5),
                 reads=["bias2_%d" % l], writes=["bias2_%d" % l])

            Cre = tmp32(128); Cim = tmp32(128)
            for key, dst in (("CTr", Cre), ("CTi", Cim)):
                for t in range(2):
                    pfc, pfck = nf()
                    P.op("pe", lambda e, pfc=pfc, key=key, t=t: e.transpose(pfc[:, 0:128], I_[key][:, t * 128:(t + 1) * 128], identf[:]),
                         reads=RK + ["identf"], writes=[pfck])
                    for gl in range(2):
                        rs_ = slice(gl * 64, (gl + 1) * 64)
                        P.op("dve", lambda e, pfc=pfc, gl=gl, rs_=rs_, t=t, dst=dst: e.tensor_copy(
                            out=dst[rs_, :].rearrange("p (g c) -> p g c", g=8)[:, 4 * t:4 * t + 4, :],
                            in_=pfc[rs_, 0:128].rearrange("p (gpl gl2 c) -> p gpl gl2 c", gl2=2, c=16)[:, :, gl, :]), reads=[pfck], writes=[K])
            dt = tmp32(8); ar = tmp32(8); th = tmp32(8)
            P.op("act", lambda e, dt=dt, lst=lst: e.activation(out=dt, in_=lst, func=AF.Exp), reads=RK, writes=[K])
            P.op("dve", lambda e, ar=ar, lre=lre, dt=dt: e.tensor_tensor(out=ar, in0=lre, in1=dt, op=ALU.mult), reads=RK, writes=[K])
            P.op("dve", lambda e, th=th, lim=lim, dt=dt: e.tensor_tensor(out=th, in0=lim, in1=dt, op=ALU.mult), reads=RK, writes=[K])
            NE = 8 * NK
            mag = tmp32(NE); tn = tmp32(NE); tq = tmp32(NE); fr = tmp32(NE); sn = tmp32(NE); cs = tmp32(NE)
            ti = tmp32(NE).bitcast(I32)
            k3 = lambda a: a.rearrange("p (g k) -> p g k", g=8)
            P.op("dve", lambda e: e.tensor_tensor(out=k3(mag), in0=k3(kv3), in1=ar.unsqueeze(2).to_broadcast([128, 8, NK]), op=ALU.mult), reads=RK, writes=[K])
            P.op("act", lambda e: e.activation(out=mag, in_=mag, func=AF.Exp), reads=[K], writes=[K])
            P.op("dve", lambda e: e.scalar_tensor_tensor(out=k3(tn), in0=k3(kv3), scalar=1.0 / TWO_PI, in1=th.unsqueeze(2).to_broadcast([128, 8, NK]),
                                                         op0=ALU.mult, op1=ALU.mult), reads=RK, writes=[K])

            def reduce_turns(src, dst, ti=ti, tq=tq):
                n_ = src.shape[1]
                P.op("dve", lambda e: e.tensor_copy(out=ti[:, 0:n_], in_=src), reads=[K], writes=[K])
                P.op("dve", lambda e: e.tensor_copy(out=tq[:, 0:n_], in_=ti[:, 0:n_]), reads=[K], writes=[K])
                P.op("dve", lambda e: e.tensor_tensor(out=dst, in0=src, in1=tq[:, 0:n_], op=ALU.subtract), reads=[K], writes=[K])

            reduce_turns(tn, fr)
            P.op("act", lambda e: e.activation(out=sn, in_=fr, func=AF.Sin, scale=TWO_PI), reads=[K], writes=[K])
            P.op("dve", lambda e: e.tensor_scalar_add(out=cs, in0=fr, scalar1=0.25), reads=[K], writes=[K])
            reduce_turns(cs, cs)
            P.op("act", lambda e: e.activation(out=cs, in_=cs, func=AF.Sin, scale=TWO_PI), reads=[K], writes=[K])
            pwr = tmp32(NE); pwi = tmp32(NE)
            P.op("dve", lambda e: e.tensor_tensor(out=pwr, in0=mag, in1=cs, op=ALU.mult), reads=[K], writes=[K])
            P.op("dve", lambda e: e.tensor_tensor(out=pwi, in0=mag, in1=sn, op=ALU.mult), reads=[K], writes=[K])
            pwr3 = k3(pwr); pwi3 = k3(pwi)
            xr = tmp32(8); den = tmp32(8); t8a = tmp32(8); t8b = tmp32(8); cr = tmp32(8); ci = tmp32(8)
            yi = pwi3[:, :, 16]
            P.op("dve", lambda e: e.tensor_scalar_add(out=xr, in0=pwr3[:, :, 16], scalar1=-1.0), reads=[K], writes=[K])
            P.op("dve", lambda e: e.tensor_tensor(out=den, in0=lre, in1=lre, op=ALU.mult), reads=RK, writes=[K])
            P.op("dve", lambda e: e.tensor_tensor(out=t8a, in0=lim, in1=lim, op=ALU.mult), reads=RK, writes=[K])
            P.op("dve", lambda e: e.tensor_tensor(out=den, in0=den, in1=t8a, op=ALU.add), reads=[K], writes=[K])
            P.op("dve", lambda e: e.reciprocal(out=den, in_=den), reads=[K], writes=[K])
            P.op("dve", lambda e: e.tensor_tensor(out=t8a, in0=xr, in1=lre, op=ALU.mult), reads=RK, writes=[K])
            P.op("dve", lambda e: e.tensor_tensor(out=t8b, in0=yi, in1=lim, op=ALU.mult), reads=RK, writes=[K])
            P.op("dve", lambda e: e.tensor_tensor(out=t8a, in0=t8a, in1=t8b, op=ALU.add), reads=[K], writes=[K])
            P.op("dve", lambda e: e.tensor_tensor(out=cr, in0=t8a, in1=den, op=ALU.mult), reads=[K], writes=[K])
            P.op("dve", lambda e: e.tensor_tensor(out=t8a, in0=yi, in1=lre, op=ALU.mult), reads=RK, writes=[K])
            P.op("dve", lambda e: e.tensor_tensor(out=t8b, in0=xr, in1=lim, op=ALU.mult), reads=RK, writes=[K])
            P.op("dve", lambda e: e.tensor_tensor(out=t8a, in0=t8a, in1=t8b, op=ALU.subtract), reads=[K], writes=[K])
            P.op("dve", lambda e: e.tensor_tensor(out=ci, in0=t8a, in1=den, op=ALU.mult), reads=[K], writes=[K])
            Bbr = tmp32(128); Bbi = tmp32(128)
            ta = tmpw(1024); tb = tmpw(1024)
            g3 = lambda a: a.rearrange("p (g c) -> p g c", g=8)
            g4 = lambda a: a.rearrange("p (g s c) -> p g s c", g=8, s=8)
            bc8 = lambda a: a.unsqueeze(2).to_broadcast([128, 8, 16])

            def cmul(out_r, out_i, ar_, ai_, br_, bi_, tv, neg_i=False):
                ta_, tb_ = tv(ta), tv(tb)
                P.op("dve", lambda e: e.tensor_tensor(out=ta_, in0=br_, in1=ar_, op=ALU.mult), reads=RK, writes=[K])
                P.op("dve", lambda e: e.tensor_tensor(out=tb_, in0=bi_, in1=ai_, op=ALU.mult), reads=RK, writes=[K])
                P.op("dve", lambda e: e.tensor_tensor(out=out_r, in0=ta_, in1=tb_, op=ALU.subtract), reads=[K], writes=[K])
                P.op("dve", lambda e: e.tensor_tensor(out=ta_, in0=bi_, in1=ar_, op=ALU.mult), reads=RK, writes=[K])
                P.op("dve", lambda e: e.tensor_tensor(out=tb_, in0=br_, in1=ai_, op=ALU.mult), reads=RK, writes=[K])
                if neg_i:
                    P.op("dve", lambda e: e.scalar_tensor_tensor(out=out_i, in0=ta_, scalar=-1.0, in1=tb_, op0=ALU.mult, op1=ALU.subtract),
                         reads=[K], writes=[K])
                else:
                    P.op("dve", lambda e: e.tensor_tensor(out=out_i, in0=ta_, in1=tb_, op=ALU.add), reads=[K], writes=[K])

            cmul(g3(Bbr), g3(Bbi), bc8(cr), bc8(ci), g3(Bre), g3(Bim), lambda a: g3(a[:, 0:128]))
            Ar = x_bf[:, 1, :].rearrange("p (g s c) -> p g s c", g=8, s=8)
            Ai = x_bf[:, 2, :].rearrange("p (g s c) -> p g s c", g=8, s=8)
            Rr = x_bf[:, 3, :].rearrange("p (g s c) -> p g s c", g=8, s=8)
            Rin = x_bf[:, 0, :].rearrange("p (g s c) -> p g s c", g=8, s=8)
            Etr = yT[:, 0:2, :].rearrange("p a n -> p (a n)").rearrange("p (g s c) -> p g s c", g=8, s=8)
            Etin = yT[:, 2:4, :].rearrange("p a n -> p (a n)").rearrange("p (g s c) -> p g s c", g=8, s=8)
            S4 = [128, 8, 8, 16]
            pwb = lambda p3, i0: p3[:, :, i0:i0 + 8].unsqueeze(3).to_broadcast(S4)
            vb = lambda a: g3(a).unsqueeze(2).to_broadcast(S4)
            cmul(Ar, Ai, pwb(pwr3, 0), pwb(pwi3, 0), vb(Bbr), vb(Bbi), g4)
            cmul(Rr, Rin, pwb(pwr3, 8), pwb(pwi3, 8), vb(Cre), vb(Cim), g4, neg_i=True)
            cmul(Etr, Etin, pwb(pwr3, 16), pwb(pwi3, 16), vb(Cre), vb(Cim), g4, neg_i=True)
            Rm = arX[:, 0:4096].rearrange("p (g r n) -> p g r n", g=16, r=2)
            P.op("pool", lambda e: e.memset(Rm, 0.0), reads=[K], writes=["RmZ"])
            P.op("pool", lambda e: e.memset(ez[:], 0.0), reads=[K], writes=["ez"])
            for gl in range(2):
                rs_ = slice(gl * 64, (gl + 1) * 64)
                for r, (srcR, srcE) in enumerate(((Rr, Etr), (Rin, Etin))):
                    P.op("dve", lambda e, rs_=rs_, gl=gl, r=r, srcR=srcR: e.tensor_copy(
                        out=Rm[rs_, gl:16:2, r, :], in_=srcR[rs_].rearrange("p g s c -> p g (s c)")), reads=[K, "RmZ"], writes=[K])
                    P.op("dve", lambda e, rs_=rs_, gl=gl, r=r, srcE=srcE: e.tensor_copy(
                        out=ez[rs_, gl:16:2, r, :], in_=srcE[rs_].rearrange("p g s c -> p g (s c)")), reads=[K], writes=["ez"])
            for gp in range(8):
                pf, pfk = nf()
                for gl in range(2):
                    for r, A_ in enumerate((Ar, Ai)):
                        P.op("pe", lambda e, pf=pf, gp=gp, gl=gl, r=r, A_=A_: e.matmul(
                            pf[:, (gl * 2 + r) * 128:(gl * 2 + r + 1) * 128], lhsT=A_[:, gp].rearrange("p s c -> p (s c)"), rhs=imask[:, gl, :],
                            start=True, stop=True), reads=[K, "imask"], writes=[pfk])
                if gp % 2 == 0:
                    P.op("act", lambda e, pf=pf, gp=gp: e.activation(out=wsz[:, 2 * gp:2 * gp + 2].rearrange("p g r n -> p (g r n)"), in_=pf[:, :], func=AF.Identity),
                         reads=[pfk], writes=["wsz"])
                else:
                    P.op("dve", lambda e, pf=pf, gp=gp: e.tensor_copy(out=wsz[:, 2 * gp:2 * gp + 2].rearrange("p g r n -> p (g r n)"), in_=pf[:, :]),
                         reads=[pfk], writes=["wsz"])
            mtmp = tmp32(512)
            for g4i in range(4):
                pf, pfk = nf()
                for gi in range(4):
                    g = g4i * 4 + gi
                    gp = g // 2
                    P.op("pe", lambda e, pf=pf, gi=gi, g=g, gp=gp: e.matmul(pf[:, gi * 128:(gi + 1) * 128], lhsT=Ar[:, gp].rearrange("p s c -> p (s c)"),
                                                                    rhs=Rm[:, g, 0, :], start=True, stop=False), reads=[K], writes=[pfk])
                    P.op("pe", lambda e, pf=pf, gi=gi, g=g, gp=gp: e.matmul(pf[:, gi * 128:(gi + 1) * 128], lhsT=Ai[:, gp].rearrange("p s c -> p (s c)"),
                                                                    rhs=Rm[:, g, 1, :], start=False, stop=True), reads=[K], writes=[pfk])
                P.op("dve", lambda e, pf=pf: e.tensor_tensor(out=mtmp.rearrange("p (g n) -> p g n", g=4), in0=pf[:, :].rearrange("p (g n) -> p g n", g=4),
                                                             in1=cm[:].unsqueeze(1).to_broadcast([128, 4, 128]), op=ALU.mult), reads=[pfk, "cm", K], writes=[K])
                for gi in range(4):
                    g = g4i * 4 + gi
                    P.op("dve", lambda e, gi=gi, g=g: e.scalar_tensor_tensor(out=mloc[:, g, :], in0=identf[:], scalar=dcol[:, g:g + 1],
                                                                             in1=mtmp[:, gi * 128:(gi + 1) * 128], op0=ALU.mult, op1=ALU.add),
                         reads=RK + ["identf"], writes=["mloc"])
            fr3 = k3(fr)
            tt_ = tmpw(8 * J); tf_ = tmpw(8 * J)
            tib = tmpw(8 * J).bitcast(I32)
            tqb = tmpw(8 * J)
            for gp in range(8):
                P.op("dve", lambda e, gp=gp: e.tensor_scalar(out=tt_[:, gp * J:(gp + 1) * J], in0=jv, scalar1=fr3[:, gp, 23:24], scalar2=None, op0=ALU.mult),
                     reads=RK, writes=[K])

            def reduce_big(src, dst):
                P.op("dve", lambda e: e.tensor_copy(out=tib, in_=src), reads=[K], writes=[K])
                P.op("dve", lambda e: e.tensor_copy(out=tqb, in_=tib), reads=[K], writes=[K])
                P.op("dve", lambda e: e.tensor_tensor(out=dst, in0=src, in1=tqb, op=ALU.subtract), reads=[K], writes=[K])

            reduce_big(tt_, tf_)
            P.op("act", lambda e: e.activation(out=sinT, in_=tf_, func=AF.Sin, scale=TWO_PI), reads=[K], writes=["tabs"])
            P.op("dve", lambda e: e.tensor_scalar_add(out=tt_, in0=tf_, scalar1=0.25), reads=[K], writes=[K])
            reduce_big(tt_, tf_)
            P.op("act", lambda e: e.activation(out=cosT, in_=tf_, func=AF.Sin, scale=TWO_PI), reads=[K], writes=["tabs"])
            P.op("dve", lambda e: e.tensor_copy(out=r8, in_=k3(mag)[:, :, 23]), reads=[K], writes=["tabs"])
            P.op("dve", lambda e: e.tensor_copy(out=decT.rearrange("p (g j) -> p g j", g=8), in_=k3(mag)[:, :, 23:24].to_broadcast([128, 8, J])), reads=[K], writes=["tabs"])
            P.op("dve", lambda e: e.memset(decT.rearrange("p (g j) -> p g j", g=8)[:, :, 0], 0.0), reads=["tabs"], writes=["tabs"])
            P.dma(lambda e, l=l: e.dma_start(out=sc_ws[l], in_=wsz[:].rearrange("p g r n -> p (g r n)")), reads=["wsz"], writes=["scws%d" % l])
            P.dma(lambda e, l=l: e.dma_start(out=sc_ml[l], in_=mloc[:].rearrange("p g n -> p (g n)")), reads=["mloc"], writes=["scml%d" % l])
            P.dma(lambda e, l=l: e.dma_start(out=sc_ez[l], in_=ez[:].rearrange("p g r n -> p (g r n)")), reads=["ez"], writes=["scez%d" % l])
            P.dma(lambda e, l=l: e.dma_start(out=sc_tb[l], in_=tabs[:]), reads=["tabs"], writes=["sctb%d" % l])

        for l in reversed(range(DEPTH)):
            setup_layer(l)

        segs = [(b, sg, l) for b in range(NB) for sg in range(NSEG) for l in range(DEPTH)]
        if n_segs_limit is not None:
            segs = segs[:n_segs_limit]

        def load_wout_ln(l):
            for kq in range(4):
                P.dma(lambda e, l=l, kq=kq: e.dma_start(out=wout[:, 2 * kq:2 * kq + 2, :],
                                                        in_=dr["w_out"][l, kq * 256:(kq + 1) * 256, :].rearrange("(k p) n -> p k n", p=128)),
                      writes=["wout"], q="pool")
            P.dma(lambda e, l=l: e.dma_start(out=lnG[:], in_=dr["ln_g"][l].rearrange("(o n) -> o n", o=1).to_broadcast([128, D])), writes=["lnG"])
            P.dma(lambda e, l=l: e.dma_start(out=lnB[:], in_=dr["ln_b"][l].rearrange("(o n) -> o n", o=1).to_broadcast([128, D])), writes=["lnB"])

        def load_derived(l):
            P.dma(lambda e, l=l: e.dma_start(out=wsz[:].rearrange("p g r n -> p (g r n)"), in_=sc_ws[l]), reads=["scws%d" % l], writes=["wsz"])
            P.dma(lambda e, l=l: e.dma_start(out=mloc[:].rearrange("p g n -> p (g n)"), in_=sc_ml[l]), reads=["scml%d" % l], writes=["mloc"])
            P.dma(lambda e, l=l: e.dma_start(out=ez[:].rearrange("p g r n -> p (g r n)"), in_=sc_ez[l]), reads=["scez%d" % l], writes=["ez"])
            P.dma(lambda e, l=l: e.dma_start(out=tabs[:], in_=sc_tb[l]), reads=["sctb%d" % l], writes=["tabs"])

        def rstd_newton(y, v, t, ti, n, reads, wkey):
            P.op("dve", lambda e: e.tensor_scalar_add(out=v, in0=v, scalar1=EPS), reads=reads, writes=[wkey])
            P.op("dve", lambda e: e.tensor_single_scalar(out=ti, in_=v.bitcast(I32), scalar=1, op=ALU.arith_shift_right), reads=[wkey], writes=[wkey])
            P.op("dve", lambda e: e.tensor_scalar(out=y.bitcast(I32), in0=ti, scalar1=-1.0, scalar2=1597463007.0, op0=ALU.mult, op1=ALU.add), reads=[wkey], writes=[wkey])
            for it in range(3):
                P.op("dve", lambda e: e.tensor_tensor(out=t, in0=y, in1=y, op=ALU.mult), reads=[wkey], writes=[wkey])
                P.op("dve", lambda e: e.tensor_tensor(out=t, in0=t, in1=v, op=ALU.mult), reads=[wkey], writes=[wkey])
                P.op("dve", lambda e: e.tensor_scalar(out=t, in0=t, scalar1=-0.5, scalar2=1.5, op0=ALU.mult, op1=ALU.add), reads=[wkey], writes=[wkey])
                P.op("dve", lambda e: e.tensor_tensor(out=y, in0=y, in1=t, op=ALU.mult), reads=[wkey], writes=[wkey])

        ev = [0]

        def evac(out_ap, in_ap, reads, writes, func=None, eng=None, **kw):
            if func is not None:
                P.op("act", lambda e: e.activation(out=out_ap, in_=in_ap, func=func, **kw), reads=reads, writes=writes)
                return
            if eng is None:
                eng = "act"
            if eng == "act":
                P.op("act", lambda e: e.activation(out=out_ap, in_=in_ap, func=AF.Identity), reads=reads, writes=writes)
            else:
                P.op(eng, lambda e: e.tensor_copy(out=out_ap, in_=in_ap), reads=reads, writes=writes)

        t1, t2, t3, hdr = [x_bf[:, i, :].bitcast(F32) for i in range(4)]
        hdi = hdi_t[:, :]
        KS = "s5t"
        j3 = lambda a: a.rearrange("p (g j) -> p g j", g=8)

        class Stages:
            pass

        def x_load(si, tiles):
            b, sg, l = segs[si]
            tok0 = sg * SEG
            for tt in tiles:
                P.dma(lambda e, b=b, tt=tt, tok0=tok0: e.dma_start(out=x_tok[:, tt, :], in_=dr["x"][b, tok0 + tt * 128: tok0 + (tt + 1) * 128, :]),
                      writes=["x_tok%d" % tt])

        def seg_prologue_x(si):
            b, sg, l = segs[si]
            if l == 0:
                x_load(si, range(4))
            if l == 0 and sg == 0:
                for r in range(2):
                    P.op("dve", lambda e, r=r: e.memset(cH[0][r][:], 0.0), writes=["cH0"])
                    P.op("dve", lambda e, r=r: e.memset(cH[1][r][:], 0.0), writes=["cH1"])

        def seg_prologue(si):
            b, sg, l = segs[si]
            if sg == 0:
                P.dma(lambda e, b=b: e.dma_start(out=mem_bf, in_=dr["mem"][b].rearrange("(t p) d -> p t d", p=128)), writes=["memb"], q="pool")
                for kt in range(8):
                    pb, pbk = nb_()
                    for mt in range(2):
                        P.op("pe", lambda e, pb=pb, kt=kt, mt=mt: e.transpose(pb[:, mt * 128:(mt + 1) * 128], mem_bf[:, mt, kt * 128:(kt + 1) * 128], identb[:]),
                             reads=["memb", "identb"], writes=[pbk])
                    evac(memT[:, kt, :], pb[:, 0:256], [pbk], ["memT"])
                for c2 in range(2):
                    pf, pfk = nf()
                    for kt in range(8):
                        P.op("pe", lambda e, pf=pf, kt=kt, c2=c2: e.matmul(pf[:, 0:256], lhsT=wk[:, kt, c2 * 128:(c2 + 1) * 128], rhs=memT[:, kt, :],
                                                                     start=(kt == 0), stop=(kt == 7)), reads=["wk", "memT"], writes=[pfk])
                    for hh in range(2):
                        rs_ = slice(hh * 64, (hh + 1) * 64)
                        evac(kTm[l][rs_, 2 * c2 + hh, :], pf[rs_, 0:256], [pfk], ["kTm%d" % l])
                for mt in range(2):
                    pf, pfk = nf()
                    for kt in range(8):
                        P.op("pe", lambda e, pf=pf, kt=kt, mt=mt: e.matmul(pf[:, 0:256], lhsT=memT[:, kt, mt * 128:(mt + 1) * 128], rhs=wv[:, kt, :],
                                                                     start=(kt == 0), stop=(kt == 7)), reads=["wv", "memT"], writes=[pfk])
                    for h in range(4):
                        evac(vm[l][:, mt, h, (h % 2) * 64:(h % 2) * 64 + 64], pf[:, h * 64:(h + 1) * 64], [pfk], ["vm%d" % l])
                for sj in range(si + 1, len(segs)):
                    if segs[sj][1] == 0:
                        load_wkv(segs[sj][2])
                        break

        def make_block(si, blk):
            b, sg, l = segs[si]
            tok0 = sg * SEG
            T0 = blk * 4
            last_blk = (blk == NBLK - 1)
            nxt_l = segs[si + 1][2] if si + 1 < len(segs) else None
            S = Stages()
            S.si, S.blk, S.l, S.last_blk = si, blk, l, last_blk

            def prefetch(name):
                if last_blk and nxt_l is not None:
                    load_win_group(nxt_l, name)

            def A0a():
                engs = ["act", "act", "act", "act"]
                for tt in range(4):
                    if engs[tt] == "act":
                        P.op("act", lambda e, tt=tt: e.activation(out=x_bf[:, tt, :], in_=x_tok[:, T0 + tt, :], func=AF.Identity),
                             reads=["x_tok%d" % (T0 + tt)], writes=["x_bf%d" % tt])
                    else:
                        P.op(engs[tt], lambda e, tt=tt: e.tensor_copy(out=x_bf[:, tt, :], in_=x_tok[:, T0 + tt, :]),
                             reads=["x_tok%d" % (T0 + tt)], writes=["x_bf%d" % tt])

            def A0():
                for kt in range(8):
                    pb, pbk = nb_()
                    for tt in range(4):
                        P.op("pe", lambda e, pb=pb, kt=kt, tt=tt: e.transpose(pb[:, tt * 128:(tt + 1) * 128], x_bf[:, tt, kt * 128:(kt + 1) * 128], identb[:]),
                             reads=["x_bf%d" % tt, "identb"], writes=[pbk])
                    evac(xT[:, kt, :], pb[:, 0:512], [pbk], ["xT"])

            def A1(tt):
                pf, pfk = nf()
                for kt in range(8):
                    P.op("pe", lambda e, pf=pf, kt=kt: e.matmul(pf[:, :], lhsT=xT[:, kt, tt * 128:(tt + 1) * 128], rhs=win[:, kt, 512:1024],
                                                          start=(kt == 0), stop=(kt == 7)), reads=["xT", "win_v"], writes=[pfk])
                evac(vn[:, tt, :], pf[:, :], [pfk], ["vn"], func=AF.Gelu)
                v3 = vn[:, tt, :].rearrange("p (h d) -> p h d", h=4)
                P.op("dve", lambda e: e.tensor_reduce(out=lnst[:, 0, tt * 4:(tt + 1) * 4], in_=v3, axis=mybir.AxisListType.X, op=ALU.add),
                     reads=["vn"], writes=["lnst"])
                P.op("dve", lambda e: e.tensor_tensor(out=tmpA, in0=vn[:, tt, :], in1=vn[:, tt, :], op=ALU.mult), reads=["vn"], writes=["tmpA"])
                P.op("dve", lambda e: e.tensor_reduce(out=lnst[:, 1, tt * 4:(tt + 1) * 4], in_=tmpA.rearrange("p (h d) -> p h d", h=4), axis=mybir.AxisListType.X, op=ALU.add),
                     reads=["tmpA"], writes=["lnst"])

            def A1_finish():
                prefetch("v")
                inv = 1.0 / 128.0
                P.op("dve", lambda e: e.tensor_scalar_mul(out=mv[:, :, 0], in0=lnst[:, 0, :], scalar1=inv), reads=["lnst"], writes=["mv"])
                P.op("dve", lambda e: e.tensor_tensor(out=mv[:, :, 1], in0=mv[:, :, 0], in1=mv[:, :, 0], op=ALU.mult), reads=["mv"], writes=["mv"])
                P.op("dve", lambda e: e.scalar_tensor_tensor(out=mv[:, :, 1], in0=lnst[:, 1, :], scalar=inv, in1=mv[:, :, 1], op0=ALU.mult, op1=ALU.subtract),
                     reads=["lnst", "mv"], writes=["mv"])
                P.op("act", lambda e: e.activation(out=rstd[:], in_=mv[:, :, 1], func=AF.Sqrt, bias=epst[:, 0:1], scale=1.0), reads=["mv", "epst"], writes=["rstd"])
                P.op("dve", lambda e: e.reciprocal(out=rstd[:], in_=rstd[:]), reads=["rstd"], writes=["rstd"])
                for tt in range(4):
                    for h in range(4):
                        i = tt * 4 + h
                        P.op("dve", lambda e, tt=tt, h=h, i=i: e.tensor_scalar(out=vn[:, tt, h * 128:(h + 1) * 128], in0=vn[:, tt, h * 128:(h + 1) * 128],
                                                                               scalar1=mv[:, i, 0:1], scalar2=rstd[:, i:i + 1], op0=ALU.subtract, op1=ALU.mult),
                             reads=["vn", "mv", "rstd"], writes=["vn"])

            def inproj(grp, out_ap, wkey, func=None, sub=0):
                col0 = WIN_G[grp][0] + sub
                pf, pfk = nf()
                for kt in range(8):
                    P.op("pe", lambda e, pf=pf, kt=kt, col0=col0: e.matmul(pf[:, :], lhsT=win[:, kt, col0:col0 + 128], rhs=xT[:, kt, :],
                                                                     start=(kt == 0), stop=(kt == 7)), reads=["xT", "win_" + grp], writes=[pfk])
                evac(out_ap, pf[:, :], [pfk], [wkey], func=func)

            def inproj_gate(grp, out_ap, wkey, sub=0):
                col0 = WIN_G[grp][0] + sub
                pf, pfk = nf()
                for kt in range(8):
                    P.op("pe", lambda e, pf=pf, kt=kt, col0=col0: e.matmul(pf[:, :], lhsT=win[:, kt, col0:col0 + 128], rhs=xT[:, kt, :],
                                                                     start=(kt == 0), stop=(kt == 7)), reads=["xT", "win_" + grp], writes=[pfk])
                P.op("act", lambda e, pf=pf: e.activation(out=out_ap, in_=pf[:, :], func=AF.Tanh, scale=0.5), reads=[pfk], writes=[wkey])
                P.op("dve", lambda e, pf=pf: e.scalar_tensor_tensor(out=out_ap, in0=out_ap, scalar=1.0, in1=pf[:, :], op0=ALU.add, op1=ALU.mult),
                     reads=[pfk, wkey], writes=[wkey])

            def Head(h):
                ug_, gs_ = ugs[h % 2]
                uk, gk = "ug", "gs"
                inproj("u%d" % h, ug_, uk, AF.Gelu)
                prefetch("u%d" % h)
                inproj_gate("g%d" % h, gs_, gk)
                prefetch("g%d" % h)
                P.op("dve", lambda e: e.tensor_tensor(out=ug_, in0=ug_, in1=gs_, op=ALU.mult), reads=[uk, gk], writes=[uk])
                pf3, pfk3 = nf()
                for tt in range(4):
                    P.op("pe", lambda e, pf3=pf3, tt=tt: e.matmul(pf3[:, tt * 128:(tt + 1) * 128], lhsT=vn[:, tt, h * 128:(h + 1) * 128], rhs=wTg[l][:, h, :],
                                                            start=True, stop=True), reads=["vn", "wTg%d" % l], writes=[pfk3])
                P.op("dve", lambda e, pf3=pf3: e.scalar_tensor_tensor(
                    out=tmpA.rearrange("p (a t) -> p a t", a=4), in0=pf3[:, :].rearrange("p (a t) -> p a t", a=4), scalar=lng[l][:, h:h + 1],
                    in1=bias2[l][:, h, :].unsqueeze(1).to_broadcast([128, 4, 128]), op0=ALU.mult, op1=ALU.add),
                    reads=[pfk3, "lng%d" % l, "bias2_%d" % l], writes=["tmpA"])
                P.op("dve", lambda e: e.tensor_tensor(out=yT[:, h, :], in0=tmpA, in1=ug_, op=ALU.mult), reads=["tmpA", uk], writes=["yT%d" % h])

            def A3():
                for c in range(2):
                    inproj("xb", xbT[:, c, :], "xbT", sub=c * 128)
                for c in range(2):
                    inproj_gate("gb", gbs[:, c, :], "gbs", sub=c * 128)
                prefetch("xb"); prefetch("gb")

            def Attn(c):
                inproj("q%d" % c, qT, "qT")
                prefetch("q%d" % c)
                inproj_gate("gx%d" % c, gxs, "gxs")
                prefetch("gx%d" % c)
                for hh in range(2):
                    h = 2 * c + hh
                    for mt in range(2):
                        pfs, pfsk = nf()
                        P.op("pe", lambda e, pfs=pfs, h=h, mt=mt: e.matmul(pfs[:, :], lhsT=kTm[l][:, h, mt * 128:(mt + 1) * 128], rhs=qT,
                                                                     start=True, stop=True), reads=["kTm%d" % l, "qT"], writes=[pfsk])
                        evac(pT[:, mt, hh, :], pfs[:, :], [pfsk], ["pT"], func=AF.Exp, scale=0.125)
                pfo, pfok = nf()
                pfd, pfdk = nf()
                n = 0
                for hh in range(2):
                    h = 2 * c + hh
                    for mt in range(2):
                        P.op("pe", lambda e, h=h, mt=mt, hh=hh, n=n: e.matmul(pfo[:, :], lhsT=vm[l][:, mt, h, :], rhs=pT[:, mt, hh, :],
                                                                        start=(n == 0), stop=(n == 3)), reads=["vm%d" % l, "pT"], writes=[pfok])
                        n += 1
                n = 0
                for hh in range(2):
                    for mt in range(2):
                        P.op("pe", lambda e, mt=mt, hh=hh, n=n: e.matmul(pfd[:, :], lhsT=hmask[:, hh, :], rhs=pT[:, mt, hh, :],
                                                                   start=(n == 0), stop=(n == 3)), reads=["hmask", "pT"], writes=[pfdk])
                        n += 1
                P.op("dve", lambda e: e.reciprocal(out=recip, in_=pfd[:, :]), reads=[pfdk], writes=["recip"])
                P.op("dve", lambda e: e.tensor_tensor(out=recip, in0=recip, in1=gxs, op=ALU.mult), reads=["recip", "gxs"], writes=["recip"])
                P.op("dve", lambda e: e.tensor_tensor(out=yT[:, 6 + c, :], in0=pfo[:, :], in1=recip, op=ALU.mult),
                     reads=[pfok, "recip"], writes=["yT%d" % (6 + c)])

            def B1():
                for c in range(2):
                    pf, pfk = nf()
                    for gl in range(8):
                        for s in range(8):
                            P.op("pe", lambda e, pf=pf, gl=gl, s=s, c=c: e.matmul(pf[:, gl * J:(gl + 1) * J], lhsT=zu[:, gl, 112 - 16 * s:240 - 16 * s],
                                                                            rhs=xbT[:, c, s:BLK:8], start=(s == 0), stop=(s == 7)),
                                 reads=["zu", "xbT"], writes=[pfk])
                    evac(Xg[:, c * 8:(c + 1) * 8, :].rearrange("p g j -> p (g j)"), pf[:, :], [pfk], ["Xg"])

            pS = []

            def B2():
                for r in range(2):
                    pf, pfk = nf()
                    pS.append((pf, pfk))
                    for gp in range(8):
                        for gl in range(2):
                            g = 2 * gp + gl
                            P.op("pe", lambda e, pf=pf, g=g, gp=gp, gl=gl, r=r: e.matmul(pf[:, gp * J:(gp + 1) * J], lhsT=wsz[:, g, r, :], rhs=Xg[:, g, :],
                                                                                   start=(gl == 0), stop=(gl == 1)), reads=["wsz", "Xg"], writes=[pfk])

            def B3():
                (pSr, pSrk), (pSi, pSik) = pS
                P.op("dve", lambda e: e.tensor_tensor(out=t1, in0=pSr[:, :], in1=cosT, op=ALU.mult), reads=[pSrk, "tabs"], writes=[KS])
                P.op("dve", lambda e: e.tensor_tensor(out=t3, in0=pSi[:, :], in1=sinT, op=ALU.mult), reads=[pSik, "tabs"], writes=[KS])
                P.op("dve", lambda e: e.tensor_tensor(out=t1, in0=t1, in1=t3, op=ALU.add), reads=[KS], writes=[KS])
                P.op("dve", lambda e: e.tensor_tensor(out=t2, in0=pSi[:, :], in1=cosT, op=ALU.mult), reads=[pSik, "tabs", KS], writes=[KS])
                P.op("dve", lambda e: e.tensor_tensor(out=t3, in0=pSr[:, :], in1=sinT, op=ALU.mult), reads=[pSrk, "tabs", KS], writes=[KS])
                P.op("dve", lambda e: e.tensor_tensor(out=t2, in0=t2, in1=t3, op=ALU.subtract), reads=[KS], writes=[KS])

            def B4():
                for r in range(2):
                    P.op("dve", lambda e, r=r: e.tensor_copy(out=Hb[:, r, :, 0], in_=cH[l][r][:]), reads=["cH%d" % l], writes=["Hb"])
                for r, (src, dst) in enumerate(((t1, hdr), (t2, hdi))):
                    P.op("dve", lambda e, r=r: e.tensor_tensor(out=car[:, r, :], in0=cH[l][r][:], in1=r8, op=ALU.mult), reads=["cH%d" % l, "tabs"], writes=["car"])
                    P.op("dve", lambda e, r=r, src=src: e.tensor_tensor(out=j3(src)[:, :, 0], in0=j3(src)[:, :, 0], in1=car[:, r, :], op=ALU.add), reads=[KS, "car"], writes=[KS])
                    P.op("dve", lambda e, src=src, dst=dst: e.tensor_tensor_scan(out=dst, data0=decT, data1=src, initial=0.0, op0=ALU.mult, op1=ALU.add),
                         reads=[KS, "tabs"], writes=[KS])

            def B5():
                P.op("dve", lambda e: e.tensor_tensor(out=t1, in0=hdr, in1=cosT, op=ALU.mult), reads=[KS, "tabs"], writes=[KS])
                P.op("dve", lambda e: e.tensor_tensor(out=t3, in0=hdi, in1=sinT, op=ALU.mult), reads=[KS, "tabs"], writes=[KS])
                P.op("dve", lambda e: e.tensor_tensor(out=Hb[:, 0, :, 1:J + 1], in0=j3(t1), in1=j3(t3), op=ALU.subtract), reads=[KS], writes=["Hb"])
                P.op("dve", lambda e: e.tensor_tensor(out=cH[l][0][:], in0=j3(t1)[:, :, J - 1], in1=j3(t3)[:, :, J - 1], op=ALU.subtract), reads=[KS], writes=["cH%d" % l])
                P.op("dve", lambda e: e.tensor_tensor(out=t1, in0=hdi, in1=cosT, op=ALU.mult), reads=[KS, "tabs"], writes=[KS])
                P.op("dve", lambda e: e.tensor_tensor(out=t3, in0=hdr, in1=sinT, op=ALU.mult), reads=[KS, "tabs"], writes=[KS])
                P.op("dve", lambda e: e.tensor_tensor(out=Hb[:, 1, :, 1:J + 1], in0=j3(t1), in1=j3(t3), op=ALU.add), reads=[KS], writes=["Hb"])
                P.op("dve", lambda e: e.tensor_tensor(out=cH[l][1][:], in0=j3(t1)[:, :, J - 1], in1=j3(t3)[:, :, J - 1], op=ALU.add), reads=[KS], writes=["cH%d" % l])

            def B6():
                for c in range(2):
                    pf, pfk = nf()
                    for gi in range(8):
                        g = c * 8 + gi
                        gp = g // 2
                        P.op("pe", lambda e, pf=pf, gi=gi, g=g: e.matmul(pf[:, gi * J:(gi + 1) * J], lhsT=mloc[:, g, :], rhs=Xg[:, g, :], start=True, stop=False),
                             reads=["mloc", "Xg"], writes=[pfk])
                        P.op("pe", lambda e, pf=pf, gi=gi, g=g, gp=gp: e.matmul(pf[:, gi * J:(gi + 1) * J], lhsT=ez[:, g, 0, :], rhs=Hb[:, 0, gp, 0:J], start=False, stop=False),
                             reads=["ez", "Hb"], writes=[pfk])
                        P.op("pe", lambda e, pf=pf, gi=gi, g=g, gp=gp: e.matmul(pf[:, gi * J:(gi + 1) * J], lhsT=ez[:, g, 1, :], rhs=Hb[:, 1, gp, 0:J], start=False, stop=True),
                             reads=["ez", "Hb"], writes=[pfk])
                    evac(Yg[:, c * 8:(c + 1) * 8, :].rearrange("p g j -> p (g j)"), pf[:, :], [pfk], ["Yg"], func=AF.Gelu)
                if last_blk and nxt_l is not None:
                    load_derived(nxt_l)

            def B7():
                for c in range(2):
                    pf, pfk = nf()
                    for t in range(8):
                        for gl in range(8):
                            P.op("pe", lambda e, pf=pf, t=t, gl=gl, c=c: e.matmul(pf[:, t * J:(t + 1) * J], lhsT=zu[:, t, 112 - 16 * gl:240 - 16 * gl],
                                                                            rhs=Yg[:, c * 8 + gl, :], start=(gl == 0), stop=(gl == 7)),
                                 reads=["zu", "Yg"], writes=[pfk])
                    evac(ybT[:, c, :].rearrange("p (j t) -> p t j", t=8), pf[:, :].rearrange("p (t j) -> p t j", t=8), [pfk], ["ybT"])

            def B8():
                for co in range(2):
                    pf, pfk = nf()
                    for kc in range(2):
                        P.op("pe", lambda e, pf=pf, kc=kc, co=co: e.matmul(pf[:, :], lhsT=gluw[l][:, kc, co * 128:(co + 1) * 128], rhs=ybT[:, kc, :],
                                                                     start=(kc == 0), stop=(kc == 1)), reads=["gluw%d" % l, "ybT"], writes=[pfk])
                    P.op("act", lambda e, pf=pf, co=co: e.activation(out=sgt[:], in_=pf[:, :], func=AF.Tanh, bias=glub[l][:, co:co + 1], scale=0.5),
                         reads=[pfk, "glub%d" % l], writes=["sg"])
                    P.op("dve", lambda e, co=co: e.scalar_tensor_tensor(out=sgt[:], in0=sgt[:], scalar=1.0, in1=gbs[:, co, :], op0=ALU.add, op1=ALU.mult),
                         reads=["sg", "gbs"], writes=["sg"])
                    P.op("dve", lambda e, co=co: e.scalar_tensor_tensor(out=yT[:, 4 + co, :], in0=ybT[:, co, :], scalar=0.25, in1=sgt[:], op0=ALU.mult, op1=ALU.mult),
                         reads=["sg", "ybT"], writes=["yT%d" % (4 + co)])

            def C(tt):
                T = T0 + tt
                acc, ak = accs[tt % 2]
                for nh in range(2):
                    pf, pfk = nf()
                    for kt in range(8):
                        P.op("pe", lambda e, pf=pf, kt=kt, nh=nh: e.matmul(pf[:, :], lhsT=yT[:, kt, tt * 128:(tt + 1) * 128], rhs=wout[:, kt, nh * 512:(nh + 1) * 512],
                                                                     start=(kt == 0), stop=(kt == 7)), reads=["yT%d" % kt, "wout"], writes=[pfk])
                    P.op("dve", lambda e, pf=pf, nh=nh: e.scalar_tensor_tensor(out=acc[:, nh * 512:(nh + 1) * 512], in0=x_tok[:, T, nh * 512:(nh + 1) * 512], scalar=ALPHA,
                                                                       in1=pf[:, :], op0=ALU.mult, op1=ALU.add),
                         reads=[pfk, "x_tok%d" % T], writes=[ak])
                    P.op("dve", lambda e, nh=nh: e.bn_stats(out=stc[:, tt % 2, nh, :], in_=acc[:, nh * 512:(nh + 1) * 512]), reads=[ak], writes=["stc%d" % (tt % 2)])
                P.op("dve", lambda e: e.bn_aggr(out=mvc[:, tt % 2, :], in_=stc[:, tt % 2].rearrange("p a s -> p (a s)")), reads=["stc%d" % (tt % 2)], writes=["mvc%d" % (tt % 2)])
                rs_ = rsc[:, tt % 2, :]
                rk = "rsc%d" % (tt % 2)
                P.op("act", lambda e: e.activation(out=rs_[:, 0:1], in_=mvc[:, tt % 2, 1:2], func=AF.Sqrt, bias=epst[:, 0:1], scale=1.0), reads=["mvc%d" % (tt % 2), "epst"], writes=[rk])
                P.op("dve", lambda e: e.reciprocal(out=rs_[:, 0:1], in_=rs_[:, 0:1]), reads=[rk], writes=[rk])
                P.op("dve", lambda e: e.scalar_tensor_tensor(out=rs_[:, 1:2], in0=mvc[:, tt % 2, 0:1], scalar=-1.0, in1=rs_[:, 0:1], op0=ALU.mult, op1=ALU.mult),
                     reads=["mvc%d" % (tt % 2), rk], writes=[rk])
                P.op("act", lambda e: e.activation(out=acc, in_=acc, func=AF.Identity, bias=rs_[:, 1:2], scale=rs_[:, 0:1]), reads=[ak, rk], writes=[ak])

            def Cb(tt):
                T = T0 + tt
                acc, ak = accs[tt % 2]
                P.op("dve", lambda e: e.tensor_tensor(out=acc, in0=acc, in1=lnG[:], op=ALU.mult), reads=[ak, "lnG"], writes=[ak])
                P.op("dve", lambda e: e.tensor_tensor(out=x_tok[:, T, :], in0=acc, in1=lnB[:], op=ALU.add), reads=[ak, "lnB"], writes=["x_tok%d" % T])
                if l == DEPTH - 1:
                    P.dma(lambda e: e.dma_start(out=out[b, tok0 + T * 128: tok0 + (T + 1) * 128, :], in_=x_tok[:, T, :]),
                          reads=["x_tok%d" % T])

            S.A0, S.A1, S.A1_finish, S.A3, S.Head, S.Attn = A0, A1, A1_finish, A3, Head, Attn
            S.A0a = A0a
            S.B1, S.B2, S.B3, S.B4, S.B5, S.B6, S.B7, S.B8, S.C = B1, B2, B3, B4, B5, B6, B7, B8, C
            S.Cb = Cb
            return S

        blocks = [(si, blk) for si in range(len(segs)) for blk in range(NBLK)]
        l0 = segs[0][2]
        load_wout_ln(l0)
        seg_prologue_x(0)
        seg_prologue(0)
        cur = make_block(*blocks[0])
        cur.A0a()
        cur.A0()
        prev = None
        for k in range(len(blocks)):
            si, blk = blocks[k]
            nxt_blk = None
            if k + 1 < len(blocks):
                nsi, nblk = blocks[k + 1]
                nxt_blk = make_block(nsi, nblk)
            for tt in range(4):
                cur.A1(tt)
            cur.A1_finish()
            cur.A3(); cur.B1()
            if nxt_blk is not None and nblk == 0:
                seg_prologue(nsi)
            if prev is not None:
                prev.C(0); prev.C(1); prev.Cb(0); prev.C(2); prev.Cb(1); prev.C(3); prev.Cb(2); prev.Cb(3)
            if prev is not None and prev.last_blk:
                load_wout_ln(segs[si][2])
                if segs[si][2] == 0:
                    x_load(si, range(4, 8))
            elif prev is None and segs[si][2] == 0:
                x_load(si, range(4, 8))
            cur.Head(0); cur.B2(); cur.B3(); cur.Head(1); cur.B4(); cur.Head(2); cur.B5()
            cur.Head(3); cur.B6(); cur.B7()
            cur.Attn(0)
            if nxt_blk is not None:
                if nblk == 0:
                    seg_prologue_x(nsi)
                nxt_blk.A0a()
            cur.B8(); cur.Attn(1)
            if nxt_blk is not None:
                nxt_blk.A0()
            prev, cur = cur, nxt_blk
        for tt in range(4):
            prev.C(tt)
            prev.Cb(tt)
        P.emit()
    nc._plan = P
    return nc


_CACHE = {}


def kernel(**inputs):
    x = np.ascontiguousarray(inputs["x"], dtype=np.float32)
    mem = np.ascontiguousarray(inputs["mem"], dtype=np.float32)
    consts = host_consts()
    if "nc" not in _CACHE:
        _CACHE["nc"] = build_program()
    nc = _CACHE["nc"]
    in_maps = []
    for i in range(N_CORES):
        m = {"x": x[NB * i:NB * (i + 1)], "mem": mem[NB * i:NB * (i + 1)]}
        for k in W_SHAPES:
            m[k] = np.ascontiguousarray(inputs[k], dtype=np.float32)
        m.update(consts)
        in_maps.append(m)
    res = run_bass_kernel_spmd(nc, in_maps, core_ids=list(range(N_CORES)))
    outp = np.concatenate([r["out"] for r in res.results], axis=0)
    return outp.astype(np.float32)
```

```python
import math
import contextlib
import numpy as np
import concourse.bass as bass
import concourse.mybir as mybir
from concourse.bass_utils import run_bass_kernel_spmd

F32 = mybir.dt.float32
BF16 = mybir.dt.bfloat16
I32 = mybir.dt.int32
AF = mybir.ActivationFunctionType
ALU = mybir.AluOpType

N_CORES = 8
NB = 2
SEQ = 2048
D = 1024
MEM = 256
DEPTH = 2
SEG = 1024
NSEG = SEQ // SEG
BLK = 512
NBLK = SEG // BLK
J = BLK // 8
ALPHA = (2 * DEPTH) ** 0.25
EPS = 1e-5
TWO_PI = 2.0 * math.pi


class Plan:
    ENGS = ("pe", "act", "dve", "pool", "sp")

    def __init__(self, nc):
        self.nc = nc
        self.ops = {e: [] for e in self.ENGS}
        self.count = {e: 0 for e in self.ENGS}
        self.last_write = {}
        self.readers = {}
        self.known = {e: {} for e in self.ENGS}
        self.dma_sems = {}
        self.alias = {}

    def set_alias(self, a, others):
        for o in others:
            self.alias.setdefault(a, set()).add(o)
            self.alias.setdefault(o, set()).add(a)

    def _expand(self, keys):
        out = []
        for k in keys:
            out.append(k)
            for a in self.alias.get(k, ()):
                out.append(a)
        return out

    def _add(self, waits, name, val):
        if waits.get(name, 0) < val:
            waits[name] = val

    def _add_dep(self, waits, dep):
        if dep is None:
            return
        if dep[0] == "eng":
            self._add(waits, "c_" + dep[1], dep[2])
        else:
            self._add(waits, dep[1], dep[2])

    def _deps(self, eng, reads, writes):
        waits = {}
        for k in self._expand(reads):
            self._add_dep(waits, self.last_write.get(k))
        for k in self._expand(writes):
            self._add_dep(waits, self.last_write.get(k))
            for e, idx in self.readers.get(k, {}).items():
                if e.startswith("dma:"):
                    self._add(waits, e[4:], idx)
                else:
                    self._add(waits, "c_" + e, idx)
        if eng == "pe":
            waits.pop("c_pe", None)
        out = {}
        kn = self.known[eng]
        for name, val in waits.items():
            if kn.get(name, 0) >= val:
                continue
            kn[name] = val
            out[name] = val
        return out

    def op(self, eng, fn, reads=(), writes=(), tag=None):
        waits = self._deps(eng, reads, writes)
        if tag is not None:
            self.tags = getattr(self, "tags", {})
            self.tags.setdefault(tag, []).append((eng, dict(waits), dict(self.known[eng]), {k: self.last_write.get(k) for k in self._expand(reads)}))
        self.count[eng] += 1
        idx = self.count[eng]
        self.ops[eng].append((waits, fn, [("c_" + eng, 1)]))
        for k in reads:
            self.readers.setdefault(k, {})[eng] = idx
        for k in writes:
            self.last_write[k] = ("eng", eng, idx)
            self.readers[k] = {}
        return idx

    def dma(self, fn, reads=(), writes=(), sem=None, q="sp"):
        if len(writes) > 0:
            sem = "d_" + str(writes[0])
        else:
            sem = "d_o_" + str(reads[0])
        waits = self._deps(q, reads, writes)
        self.dma_sems[sem] = self.dma_sems.get(sem, 0) + 16
        val = self.dma_sems[sem]
        self.ops[q].append((waits, fn, [(sem, 16)]))
        for k in writes:
            self.last_write[k] = ("dma", sem, val)
            self.readers[k] = {}
        for k in reads:
            self.readers.setdefault(k, {})["dma:" + sem] = val
        return val

    def emit(self):
        nc = self.nc
        names = set(["c_" + e for e in self.ENGS]) | set(self.dma_sems.keys())
        with contextlib.ExitStack() as st:
            sems = {n: st.enter_context(nc.semaphore(n)) for n in sorted(names)}
            block = st.enter_context(nc.Block())

            def replay(ename):
                def body(eng):
                    for waits, fn, incs in self.ops[ename]:
                        for n, v in waits.items():
                            eng.wait_ge(sems[n], v)
                        ins = fn(eng)
                        for n, v in incs:
                            ins.then_inc(sems[n], v)
                    if ename == "sp":
                        for n, v in self.dma_sems.items():
                            eng.wait_ge(sems[n], v)
                return body

            block.tensor(replay("pe"))
            block.scalar(replay("act"))
            block.vector(replay("dve"))
            block.gpsimd(replay("pool"))
            block.sync(replay("sp"))


def host_consts():
    c = {}
    c["c_ident"] = np.eye(128, dtype=np.float32)
    zu = np.zeros((128, 8, 240), np.float32)
    for gl in range(8):
        for cc in range(16):
            zu[16 * gl + cc, gl, 112 + cc] = 1.0
    c["c_zu"] = zu.reshape(128, 8 * 240)
    sc = np.arange(128) // 16
    cm = (sc[None, :] >= sc[:, None]).astype(np.float32)
    c["c_cm"] = cm
    tri = (np.arange(128)[None, :] <= np.arange(128)[:, None]).astype(np.float32)
    c["c_tri"] = tri
    im = np.zeros((128, 2, 128), np.float32)
    for k in range(128):
        im[k, k // 64, k] = 1.0
    c["c_imask"] = im.reshape(128, 256)
    kvals = np.concatenate([np.arange(7, -1, -1), np.arange(-7, 1), np.arange(1, 9)]).astype(np.float32)
    kv = np.tile(kvals[None, None, :], (128, 8, 1))
    c["c_kv3"] = kv.reshape(128, 192)
    c["c_jv"] = np.tile(np.arange(1, J + 1, dtype=np.float32)[None, :], (128, 1))
    hm = np.zeros((128, 2, 128), np.float32)
    hm[:, 0, 0:64] = 2.0
    hm[:, 1, 64:128] = 2.0
    c["c_hmask"] = hm.reshape(128, 256)
    return c


CONST_SHAPES = {"c_ident": [128, 128], "c_zu": [128, 1920], "c_cm": [128, 128], "c_tri": [128, 128],
                "c_imask": [128, 256], "c_kv3": [128, 192], "c_jv": [128, J], "c_hmask": [128, 256]}

W_SHAPES = {
    "w_in": [DEPTH, D, 2560], "gm_w_s": [DEPTH, 4, 128, 128], "gm_b_s": [DEPTH, 4, 128],
    "gm_ln_g": [DEPTH, 4, 128], "gm_ln_b": [DEPTH, 4, 128], "ssm_lam_re": [DEPTH, 16, 64],
    "ssm_lam_im": [DEPTH, 16, 64], "ssm_log_step": [DEPTH, 16], "ssm_b_re": [DEPTH, 16, 64, 16],
    "ssm_b_im": [DEPTH, 16, 64, 16], "ssm_c_re": [DEPTH, 16, 16, 64], "ssm_c_im": [DEPTH, 16, 16, 64],
    "ssm_d": [DEPTH, 256], "glu_w": [DEPTH, 256, 256], "glu_b": [DEPTH, 256],
    "xa_w_k": [DEPTH, D, 256], "xa_w_v": [DEPTH, D, 256], "w_out": [DEPTH, D, D],
    "ln_g": [DEPTH, D], "ln_b": [DEPTH, D],
}


SBUF_FREE = [None]


def build_program(dbg=False, n_segs_limit=None):
    nc = bass.Bass("TRN2", target_bir_lowering=False)
    dr = {}
    dr["x"] = nc.dram_tensor("x", [NB, SEQ, D], F32, kind="ExternalInput").ap()
    dr["mem"] = nc.dram_tensor("mem", [NB, MEM, D], F32, kind="ExternalInput").ap()
    for k, shp in W_SHAPES.items():
        dr[k] = nc.dram_tensor(k, shp, F32, kind="ExternalInput").ap()
    for k, shp in CONST_SHAPES.items():
        dr[k] = nc.dram_tensor(k, shp, F32, kind="ExternalInput").ap()
    out = nc.dram_tensor("out", [NB, SEQ, D], F32, kind="ExternalOutput").ap()
    sck = dict(kind="ExternalOutput") if dbg else {}
    sc_ws = [nc.dram_tensor("sc_ws%d" % l, [128, 16 * 2 * 128], BF16, **sck).ap() for l in range(DEPTH)]
    sc_ml = [nc.dram_tensor("sc_ml%d" % l, [128, 16 * 128], BF16, **sck).ap() for l in range(DEPTH)]
    sc_ez = [nc.dram_tensor("sc_ez%d" % l, [128, 16 * 2 * 128], BF16, **sck).ap() for l in range(DEPTH)]
    sc_tb = [nc.dram_tensor("sc_tb%d" % l, [128, 3 * 8 * J + 8], F32, **sck).ap() for l in range(DEPTH)]
    if dbg:
        dbg_y = nc.dram_tensor("dbg_y", [128, 8 * BLK], F32, kind="ExternalOutput").ap()
        dbg_x = nc.dram_tensor("dbg_x", [128, 4 * D], F32, kind="ExternalOutput").ap()
    taps = {}

    def tap(P, name, ap, n, dt, reads):
        if not dbg:
            return
        t = nc.dram_tensor(name, [128, n], dt, kind="ExternalOutput").ap()
        P.dma(lambda e: e.dma_start(out=t, in_=ap), reads=reads, sem="d_dbg")

    P = Plan(nc)
    with contextlib.ExitStack() as st:
        def sb(name, shape, dt):
            return st.enter_context(nc.sbuf_tensor(name, shape, dt))

        def ps(name, shape, dt):
            return st.enter_context(nc.psum_tensor(name, shape, dt))

        psf = [ps("psf%d" % i, [128, 512], F32) for i in range(6)]
        psb = [ps("psb%d" % i, [128, 1024], BF16) for i in range(2)]
        rot = {"f": 0, "b": 0}

        def nf():
            i = rot["f"]; rot["f"] = (i + 1) % 6
            return psf[i], "psf%d" % i

        def nb_():
            i = rot["b"]; rot["b"] = (i + 1) % 2
            return psb[i], "psb%d" % i

        identf = sb("identf", [128, 128], F32)
        identb = sb("identb", [128, 128], BF16)
        zu = sb("zu", [128, 8, 240], BF16)
        cm = sb("cm", [128, 128], F32)
        imask = sb("imask", [128, 2, 128], BF16)
        hmask = sb("hmask", [128, 2, 128], BF16)
        onesb = sb("onesb", [128, 128], BF16)
        onesf = sb("onesf", [1, 128], F32)
        epst = sb("epst", [128, 1], F32)
        win = sb("win", [128, 8, 2560], BF16)
        wout = sb("wout", [128, 8, 1024], BF16)
        wk = sb("wk", [128, 8, 256], BF16)
        wv = sb("wv", [128, 8, 256], BF16)
        wsz = sb("wsz", [128, 16, 2, 128], BF16)
        mloc = sb("mloc", [128, 16, 128], BF16)
        ez = sb("ez", [128, 16, 2, 128], BF16)
        tabs = sb("tabs", [128, 3 * 8 * J + 8], F32)
        cosT = tabs[:, 0:8 * J]
        sinT = tabs[:, 8 * J:16 * J]
        r8 = tabs[:, 16 * J:16 * J + 8]
        decT = tabs[:, 16 * J + 8:24 * J + 8]
        lnG = sb("lnG", [128, D], F32)
        lnB = sb("lnB", [128, D], F32)
        wTg = [sb("wTg%d" % l, [128, 4, 128], BF16) for l in range(DEPTH)]
        bias2 = [sb("bias2_%d" % l, [128, 4, 128], F32) for l in range(DEPTH)]
        lng = [sb("lng%d" % l, [128, 4], F32) for l in range(DEPTH)]
        gluw = [sb("gluw%d" % l, [128, 2, 256], BF16) for l in range(DEPTH)]
        glub = [sb("glub%d" % l, [128, 2], F32) for l in range(DEPTH)]
        kTm = [sb("kTm%d" % l, [128, 4, 256], BF16) for l in range(DEPTH)]
        vm = [sb("vm%d" % l, [128, 2, 4, 128], BF16) for l in range(DEPTH)]
        cH = [[sb("cH%d_%d" % (l, r), [128, 8], F32) for r in range(2)] for l in range(DEPTH)]
        x_tok = sb("x_tok", [128, 8, D], F32)
        x_bf = sb("x_bf", [128, 4, D], BF16)
        xbT = sb("xbT", [128, 2, BLK], BF16)
        gbs = sb("gbs", [128, 2, BLK], BF16)
        yT = sb("yT", [128, 8, BLK], BF16)
        arX = sb("arX", [128, 6144], BF16)
        xT = arX[:, 0:4096].rearrange("p (k n) -> p k n", k=8)
        vn = arX[:, 4096:6144].rearrange("p (t n) -> p t n", t=4)
        hdi_t = sb("hdi_t", [128, 8 * J], F32)
        P.set_alias("s5t", ["x_bf0", "x_bf1", "x_bf2", "x_bf3"])
        mem_bf = x_bf[:, 0:2, :]
        memT = x_bf[:, 2:4, :].rearrange("p a n -> p (a n)").rearrange("p (k n) -> p k n", k=8)
        P.set_alias("memb", ["x_bf0", "x_bf1", "s5t"]); P.set_alias("memT", ["x_bf2", "x_bf3", "s5t"])
        arU = sb("arU", [128, 1024], BF16)
        ug = arU[:, 0:512]
        gs = arU[:, 512:1024]
        ugs = [(arU[:, 0:512], arU[:, 512:1024]), (arU[:, 0:512], arU[:, 512:1024])]
        Xg_t = sb("Xg_t", [128, 1024], BF16)
        Xg = Xg_t[:, :].rearrange("p (g j) -> p g j", g=16)
        arQ = sb("arQ", [128, 1024], BF16)
        qT = arQ[:, 0:512]
        gxs = arQ[:, 512:1024]
        Yg = arQ[:, :].rearrange("p (g j) -> p g j", g=16)
        P.set_alias("Yg", ["qT", "gxs"])
        arR = sb("arR", [128, 1056], BF16)
        recip = arR[:, 0:1024].bitcast(F32)
        Hb = arR[:, 0:2 * 8 * (J + 1)].rearrange("p (r g j) -> p r g j", r=2, g=8)
        P.set_alias("Hb", ["recip"])
        arV = sb("arV", [128, 1024], BF16)
        tmpA = arV[:, 0:512]
        ybT = arV[:, :].rearrange("p (c n) -> p c n", c=2)
        P.set_alias("ybT", ["tmpA"])
        sgt = sb("sgt", [128, BLK], BF16)
        arP = sb("arP", [128, 2048], BF16)
        pT = arP[:, :].rearrange("p (m h n) -> p m h n", m=2, h=2)
        acc0 = arP[:, :].bitcast(F32)
        acc1_t = sb("acc1_t", [128, D], F32)
        accs = [(acc0, "acc"), (acc1_t[:, :], "acc1")]
        P.set_alias("acc", ["pT"])
        st6 = sb("st6", [128, 16, 6], F32)
        mv = sb("mv", [128, 16, 2], F32)
        rstd = sb("rstd", [128, 16], F32)
        car = sb("car", [128, 2, 8], F32)
        lnst = sb("lnst", [128, 2, 16], F32)
        stc = sb("stc", [128, 2, 2, 6], F32)
        mvc = sb("mvc", [128, 2, 2], F32)
        rsc = sb("rsc", [128, 2, 2], F32)
        nw_v = sb("nw_v", [128, 18], F32)
        nw_t = sb("nw_t", [128, 18], F32)
        nw_i = sb("nw_i", [128, 18], I32)

        P.set_alias("setup", ["x_tok%d" % i for i in range(8)] + ["x_bf%d" % i for i in range(4)] + ["yT%d" % i for i in range(8)]
                    + ["x_bf", "xT", "vn", "s5t", "memb", "memT", "wout", "sin0", "sin1", "setupc", "RmZ"])
        sp_sem_ct = [0]
        SBUF_FREE[0] = nc.sbuf_bytes_remaining

        def sem_name(base):
            return "d_" + base

        P.dma(lambda e: e.dma_start(out=identf[:], in_=dr["c_ident"]), writes=["identf"], sem="d_c0")
        P.dma(lambda e: e.dma_start(out=cm[:], in_=dr["c_cm"]), writes=["cm"], sem="d_c1")
        P.dma(lambda e: e.dma_start(out=identb[:], in_=dr["c_ident"]), writes=["identb"], sem="d_c2", q="pool")
        P.dma(lambda e: e.dma_start(out=zu[:].rearrange("p g n -> p (g n)"), in_=dr["c_zu"]), writes=["zu"], sem="d_c3", q="pool")
        P.dma(lambda e: e.dma_start(out=imask[:].rearrange("p g n -> p (g n)"), in_=dr["c_imask"]), writes=["imask"], sem="d_c4", q="pool")
        P.dma(lambda e: e.dma_start(out=hmask[:].rearrange("p g n -> p (g n)"), in_=dr["c_hmask"]), writes=["hmask"], sem="d_c5", q="pool")
        P.op("dve", lambda e: e.memset(onesb[:], 1.0), writes=["onesb"])
        P.op("dve", lambda e: e.memset(onesf[:], 1.0), writes=["onesf"])
        P.op("dve", lambda e: e.memset(epst[:], EPS), writes=["epst"])
        for l in range(DEPTH):
            P.op("dve", lambda e, l=l: e.memset(kTm[l][:], 0.0), writes=["kTm%d" % l])
            P.op("pool", lambda e, l=l: e.memset(vm[l][:], 0.0), writes=["vm%d" % l])

        WIN_GROUPS = [("v", 512, 512), ("xb", 1536, 256), ("gb", 1792, 256)]
        for h in range(4):
            WIN_GROUPS += [("u%d" % h, h * 128, 128), ("g%d" % h, 1024 + h * 128, 128)]
        for c in range(2):
            WIN_GROUPS += [("q%d" % c, 2048 + c * 128, 128), ("gx%d" % c, 2304 + c * 128, 128)]
        WIN_G = {n: (c0, nc_) for n, c0, nc_ in WIN_GROUPS}

        def load_win_group(l, name):
            c0, ncol = WIN_G[name]
            P.dma(lambda e, l=l, c0=c0, ncol=ncol: e.dma_start(out=win[:, :, c0:c0 + ncol],
                                                               in_=dr["w_in"][l, :, c0:c0 + ncol].rearrange("(k p) n -> p k n", p=128)),
                  writes=["win_" + name], q="pool")

        def load_wkv(l):
            P.dma(lambda e, l=l: e.dma_start(out=wk[:], in_=dr["xa_w_k"][l].rearrange("(k p) n -> p k n", p=128)), writes=["wk"], q="pool")
            P.dma(lambda e, l=l: e.dma_start(out=wv[:], in_=dr["xa_w_v"][l].rearrange("(k p) n -> p k n", p=128)), writes=["wv"], q="pool")

        for name, _, _ in WIN_GROUPS:
            load_win_group(0, name)
        load_wkv(0)

        xt_flat = x_tok[:].rearrange("p t d -> p (t d)")
        wo_flat = wout[:].rearrange("p k n -> p (k n)").bitcast(F32)
        off = [0]
        off2 = [0]

        def tmp32(n):
            a = xt_flat[:, off[0]:off[0] + n]
            off[0] += n
            assert off[0] <= 8192, off[0]
            return a

        def tmpw(n):
            a = wo_flat[:, off2[0]:off2[0] + n]
            off2[0] += n
            assert off2[0] <= 4096
            return a

        NK = 24
        kv3 = tmp32(8 * NK)
        jv = tmp32(J)
        tri = tmp32(128)
        P.dma(lambda e: e.dma_start(out=kv3, in_=dr["c_kv3"]), writes=["setupc"])
        P.dma(lambda e: e.dma_start(out=jv, in_=dr["c_jv"]), writes=["setupc"])
        P.dma(lambda e: e.dma_start(out=tri, in_=dr["c_tri"]), writes=["setupc"])

        INP = []
        QS = ["sp", "act"]
        qi = [0]

        def qdma(fn, SK):
            P.dma(fn, writes=[SK], q=QS[qi[0] % 2])
            qi[0] += 1

        INP = [None] * DEPTH
        for l in reversed(range(DEPTH)):
            I_ = {}
            SK = "sin%d" % l
            I_["wraw"] = tmp32(512); I_["bs_bc"] = tmp32(512)
            I_["PA"] = tmp32(128); I_["LT"] = tmp32(128); I_["L16"] = tmp32(16)
            I_["Bre"] = tmp32(128); I_["Bim"] = tmp32(128); I_["CTr"] = tmp32(256); I_["CTi"] = tmp32(256)
            INP[l] = I_
            qdma(lambda e, l=l, a=I_["wraw"]: e.dma_start(out=a.rearrange("p (h s) -> p h s", h=4), in_=dr["gm_w_s"][l].rearrange("h t s -> t h s")), SK)
            qdma(lambda e, l=l, a=I_["bs_bc"]: e.dma_start(out=a, in_=dr["gm_b_s"][l].rearrange("(o h) d -> o (h d)", o=1).to_broadcast([128, 512])), SK)
            qdma(lambda e, l=l, a=I_["PA"]: e.dma_start(out=a[0:4, :], in_=dr["gm_ln_b"][l]), SK)
            qdma(lambda e, l=l, a=I_["PA"]: e.dma_start(out=a[4:8, :], in_=dr["gm_ln_g"][l]), SK)
            qdma(lambda e, l=l, a=I_["PA"]: e.dma_start(out=a[8:10, :], in_=dr["glu_b"][l].rearrange("(c p) -> c p", p=128)), SK)
            P.dma(lambda e, l=l: e.dma_start(out=gluw[l][:], in_=dr["glu_w"][l].rearrange("(k p) n -> p k n", p=128)), writes=["gluw%d" % l], q="pool")
            qdma(lambda e, l=l, a=I_["LT"]: e.dma_start(out=a[0:16, :].rearrange("g (o p) -> g o p", o=2),
                                                      in_=dr["ssm_lam_re"][l].unsqueeze(1).to_broadcast([16, 2, 64])), SK)
            qdma(lambda e, l=l, a=I_["LT"]: e.dma_start(out=a[16:32, :].rearrange("g (o p) -> g o p", o=2),
                                                      in_=dr["ssm_lam_im"][l].unsqueeze(1).to_broadcast([16, 2, 64])), SK)
            qdma(lambda e, l=l, a=I_["LT"]: e.dma_start(out=a[32:48, :].rearrange("g (s c) -> g s c", s=8),
                                                      in_=dr["ssm_d"][l].rearrange("(g o c) -> g o c", o=1, c=16).to_broadcast([16, 8, 16])), SK)
            qdma(lambda e, l=l, a=I_["L16"]: e.dma_start(out=a, in_=dr["ssm_log_step"][l].rearrange("(o g) -> o g", o=1).to_broadcast([128, 16])), SK)
            qdma(lambda e, l=l, a=I_["Bre"]: e.dma_start(out=a.rearrange("p (g c) -> p g c", g=8),
                                                       in_=dr["ssm_b_re"][l].rearrange("(gp gl) p c -> (gl p) gp c", gl=2)), SK)
            qdma(lambda e, l=l, a=I_["Bim"]: e.dma_start(out=a.rearrange("p (g c) -> p g c", g=8),
                                                       in_=dr["ssm_b_im"][l].rearrange("(gp gl) p c -> (gl p) gp c", gl=2)), SK)
            for nm, key in (("ssm_c_re", "CTr"), ("ssm_c_im", "CTi")):
                for t in range(2):
                    qdma(lambda e, l=l, t=t, nm=nm, a=I_[key]: e.dma_start(
                        out=a[:, t * 128:(t + 1) * 128].rearrange("r (o p) -> r o p", o=2),
                        in_=dr[nm][l].rearrange("(t gi) c p -> t (gi c) p", t=2)[t].unsqueeze(1).to_broadcast([128, 2, 64])), SK)
        base_off = off[0]

        def setup_layer(l):
            off[0] = base_off
            off2[0] = 0
            K = "setup"
            SK = "sin%d" % l
            RK = [K, SK, "setupc"]
            I_ = INP[l]
            Bre, Bim = I_["Bre"], I_["Bim"]
            lre = tmp32(8); lim = tmp32(8); lst = tmp32(8); dcol = tmp32(16); lnb_col = tmp32(4)
            pf, pfk = nf()
            P.op("pe", lambda e, pf=pf: e.transpose(pf[:, 0:10], I_["PA"][0:10, :], identf[0:10, 0:10]), reads=RK + ["identf"], writes=[pfk])
            P.op("dve", lambda e, pf=pf: e.tensor_copy(out=lnb_col, in_=pf[:, 0:4]), reads=[pfk], writes=[K])
            P.op("dve", lambda e, pf=pf: e.tensor_scalar_mul(out=lng[l][:], in0=pf[:, 4:8], scalar1=0.5), reads=[pfk], writes=["lng%d" % l])
            P.op("dve", lambda e, pf=pf: e.tensor_scalar_mul(out=glub[l][:], in0=pf[:, 8:10], scalar1=0.5), reads=[pfk], writes=["glub%d" % l])
            pf2, pfk2 = nf()
            P.op("pe", lambda e, pf2=pf2: e.transpose(pf2[:, 0:48], I_["LT"][0:48, :], identf[0:48, 0:48]), reads=RK + ["identf"], writes=[pfk2])
            for gl in range(2):
                rs_ = slice(gl * 64, (gl + 1) * 64)
                P.op("dve", lambda e, pf2=pf2, gl=gl, rs_=rs_: e.tensor_copy(out=lre[rs_, :], in_=pf2[rs_, gl:16:2]), reads=[pfk2], writes=[K])
                P.op("dve", lambda e, pf2=pf2, gl=gl, rs_=rs_: e.tensor_copy(out=lim[rs_, :], in_=pf2[rs_, 16 + gl:32:2]), reads=[pfk2], writes=[K])
                P.op("dve", lambda e, gl=gl, rs_=rs_: e.tensor_copy(out=lst[rs_, :], in_=I_["L16"][rs_, gl:16:2]), reads=RK, writes=[K])
            P.op("dve", lambda e, pf2=pf2: e.tensor_copy(out=dcol, in_=pf2[:, 32:48]), reads=[pfk2], writes=[K])
            wraw = I_["wraw"]
            wmb = x_bf[:, 0, 0:512]
            P.op("dve", lambda e, wraw=wraw, wmb=wmb: e.tensor_tensor(
                out=wmb.rearrange("p (h s) -> p h s", h=4), in0=wraw.rearrange("p (h s) -> p h s", h=4),
                in1=tri.unsqueeze(1).to_broadcast([128, 4, 128]), op=ALU.mult), reads=RK, writes=["x_bf"])
            pb, pbk = nb_()
            for h in range(4):
                P.op("pe", lambda e, h=h, pb=pb, wmb=wmb: e.transpose(pb[:, h * 128:(h + 1) * 128], wmb[:, h * 128:(h + 1) * 128], identb[:]),
                     reads=["x_bf", "identb"], writes=[pbk])
            P.op("dve", lambda e, l=l, pb=pb: e.tensor_copy(out=wTg[l][:].rearrange("p h t -> p (h t)"), in_=pb[:, 0:512]),
                 reads=[pbk], writes=["wTg%d" % l])
            pf, pfk = nf()
            P.op("pe", lambda e, l=l, pf=pf: e.matmul(pf[:, :], lhsT=onesb[:, :], rhs=wTg[l][:].rearrange("p h t -> p (h t)"),
                                                      start=True, stop=True), reads=["onesb", "wTg%d" % l], writes=[pfk])
            for h in range(4):
                P.op("dve", lambda e, l=l, h=h, pf=pf, bs_bc=I_["bs_bc"], lnb_col=lnb_col: e.scalar_tensor_tensor(
                    out=bias2[l][:, h, :], in0=pf[:, h * 128:(h + 1) * 128], scalar=lnb_col[:, h:h + 1], in1=bs_bc[:, h * 128:(h + 1) * 128],
                    op0=ALU.mult, op1=ALU.add), reads=[pfk] + RK, writes=["bias2_%d" % l])
            P.op("dve", lambda e, l=l: e.tensor_scalar_mul(out=bias2[l][:].rearrange("p h t -> p (h t)"), in0=bias2[l][:].rearrange("p h t -> p (h t)"), scalar1=0.5),
                 reads=["bias2_%d" % l], writes=["bias2_%d" % l])

            Cre = tmp32(128); Cim = tmp32(128)
            for key, dst in (("CTr", Cre), ("CTi", Cim)):
                for t in range(2):
                    pfc, pfck = nf()
                    P.op("pe", lambda e, pfc=pfc, key=key, t=t: e.transpose(pfc[:, 0:128], I_[key][:, t * 128:(t + 1) * 128], identf[:]),
                         reads=RK + ["identf"], writes=[pfck])
                    for gl in range(2):
                        rs_ = slice(gl * 64, (gl + 1) * 64)
                        P.op("dve", lambda e, pfc=pfc, gl=gl, rs_=rs_, t=t, dst=dst: e.tensor_copy(
                            out=dst[rs_, :].rearrange("p (g c) -> p g c", g=8)[:, 4 * t:4 * t + 4, :],
                            in_=pfc[rs_, 0:128].rearrange("p (gpl gl2 c) -> p gpl gl2 c", gl2=2, c=16)[:, :, gl, :]), reads=[pfck], writes=[K])
            dt = tmp32(8); ar = tmp32(8); th = tmp32(8)
            P.op("act", lambda e, dt=dt, lst=lst: e.activation(out=dt, in_=lst, func=AF.Exp), reads=RK, writes=[K])
            P.op("dve", lambda e, ar=ar, lre=lre, dt=dt: e.tensor_tensor(out=ar, in0=lre, in1=dt, op=ALU.mult), reads=RK, writes=[K])
            P.op("dve", lambda e, th=th, lim=lim, dt=dt: e.tensor_tensor(out=th, in0=lim, in1=dt, op=ALU.mult), reads=RK, writes=[K])
            NE = 8 * NK
            mag = tmp32(NE); tn = tmp32(NE); tq = tmp32(NE); fr = tmp32(NE); sn = tmp32(NE); cs = tmp32(NE)
            ti = tmp32(NE).bitcast(I32)
            k3 = lambda a: a.rearrange("p (g k) -> p g k", g=8)
            P.op("dve", lambda e: e.tensor_tensor(out=k3(mag), in0=k3(kv3), in1=ar.unsqueeze(2).to_broadcast([128, 8, NK]), op=ALU.mult), reads=RK, writes=[K])
            P.op("act", lambda e: e.activation(out=mag, in_=mag, func=AF.Exp), reads=[K], writes=[K])
            P.op("dve", lambda e: e.scalar_tensor_tensor(out=k3(tn), in0=k3(kv3), scalar=1.0 / TWO_PI, in1=th.unsqueeze(2).to_broadcast([128, 8, NK]),
                                                         op0=ALU.mult, op1=ALU.mult), reads=RK, writes=[K])

            def reduce_turns(src, dst, ti=ti, tq=tq):
                n_ = src.shape[1]
                P.op("dve", lambda e: e.tensor_copy(out=ti[:, 0:n_], in_=src), reads=[K], writes=[K])
                P.op("dve", lambda e: e.tensor_copy(out=tq[:, 0:n_], in_=ti[:, 0:n_]), reads=[K], writes=[K])
                P.op("dve", lambda e: e.tensor_tensor(out=dst, in0=src, in1=tq[:, 0:n_], op=ALU.subtract), reads=[K], writes=[K])

            reduce_turns(tn, fr)
            P.op("act", lambda e: e.activation(out=sn, in_=fr, func=AF.Sin, scale=TWO_PI), reads=[K], writes=[K])
            P.op("dve", lambda e: e.tensor_scalar_add(out=cs, in0=fr, scalar1=0.25), reads=[K], writes=[K])
            reduce_turns(cs, cs)
            P.op("act", lambda e: e.activation(out=cs, in_=cs, func=AF.Sin, scale=TWO_PI), reads=[K], writes=[K])
            pwr = tmp32(NE); pwi = tmp32(NE)
            P.op("dve", lambda e: e.tensor_tensor(out=pwr, in0=mag, in1=cs, op=ALU.mult), reads=[K], writes=[K])
            P.op("dve", lambda e: e.tensor_tensor(out=pwi, in0=mag, in1=sn, op=ALU.mult), reads=[K], writes=[K])
            pwr3 = k3(pwr); pwi3 = k3(pwi)
            xr = tmp32(8); den = tmp32(8); t8a = tmp32(8); t8b = tmp32(8); cr = tmp32(8); ci = tmp32(8)
            yi = pwi3[:, :, 16]
            P.op("dve", lambda e: e.tensor_scalar_add(out=xr, in0=pwr3[:, :, 16], scalar1=-1.0), reads=[K], writes=[K])
            P.op("dve", lambda e: e.tensor_tensor(out=den, in0=lre, in1=lre, op=ALU.mult), reads=RK, writes=[K])
            P.op("dve", lambda e: e.tensor_tensor(out=t8a, in0=lim, in1=lim, op=ALU.mult), reads=RK, writes=[K])
            P.op("dve", lambda e: e.tensor_tensor(out=den, in0=den, in1=t8a, op=ALU.add), reads=[K], writes=[K])
            P.op("dve", lambda e: e.reciprocal(out=den, in_=den), reads=[K], writes=[K])
            P.op("dve", lambda e: e.tensor_tensor(out=t8a, in0=xr, in1=lre, op=ALU.mult), reads=RK, writes=[K])
            P.op("dve", lambda e: e.tensor_tensor(out=t8b, in0=yi, in1=lim, op=ALU.mult), reads=RK, writes=[K])
            P.op("dve", lambda e: e.tensor_tensor(out=t8a, in0=t8a, in1=t8b, op=ALU.add), reads=[K], writes=[K])
            P.op("dve", lambda e: e.tensor_tensor(out=cr, in0=t8a, in1=den, op=ALU.mult), reads=[K], writes=[K])
            P.op("dve", lambda e: e.tensor_tensor(out=t8a, in0=yi, in1=lre, op=ALU.mult), reads=RK, writes=[K])
            P.op("dve", lambda e: e.tensor_tensor(out=t8b, in0=xr, in1=lim, op=ALU.mult), reads=RK, writes=[K])
            P.op("dve", lambda e: e.tensor_tensor(out=t8a, in0=t8a, in1=t8b, op=ALU.subtract), reads=[K], writes=[K])
            P.op("dve", lambda e: e.tensor_tensor(out=ci, in0=t8a, in1=den, op=ALU.mult), reads=[K], writes=[K])
            Bbr = tmp32(128); Bbi = tmp32(128)
            ta = tmpw(1024); tb = tmpw(1024)
            g3 = lambda a: a.rearrange("p (g c) -> p g c", g=8)
            g4 = lambda a: a.rearrange("p (g s c) -> p g s c", g=8, s=8)
            bc8 = lambda a: a.unsqueeze(2).to_broadcast([128, 8, 16])

            def cmul(out_r, out_i, ar_, ai_, br_, bi_, tv, neg_i=False):
                ta_, tb_ = tv(ta), tv(tb)
                P.op("dve", lambda e: e.tensor_tensor(out=ta_, in0=br_, in1=ar_, op=ALU.mult), reads=RK, writes=[K])
                P.op("dve", lambda e: e.tensor_tensor(out=tb_, in0=bi_, in1=ai_, op=ALU.mult), reads=RK, writes=[K])
                P.op("dve", lambda e: e.tensor_tensor(out=out_r, in0=ta_, in1=tb_, op=ALU.subtract), reads=[K], writes=[K])
                P.op("dve", lambda e: e.tensor_tensor(out=ta_, in0=bi_, in1=ar_, op=ALU.mult), reads=RK, writes=[K])
                P.op("dve", lambda e: e.tensor_tensor(out=tb_, in0=br_, in1=ai_, op=ALU.mult), reads=RK, writes=[K])
                if neg_i:
                    P.op("dve", lambda e: e.scalar_tensor_tensor(out=out_i, in0=ta_, scalar=-1.0, in1=tb_, op0=ALU.mult, op1=ALU.subtract),
                         reads=[K], writes=[K])
                else:
                    P.op("dve", lambda e: e.tensor_tensor(out=out_i, in0=ta_, in1=tb_, op=ALU.add), reads=[K], writes=[K])

            cmul(g3(Bbr), g3(Bbi), bc8(cr), bc8(ci), g3(Bre), g3(Bim), lambda a: g3(a[:, 0:128]))
            Ar = x_bf[:, 1, :].rearrange("p (g s c) -> p g s c", g=8, s=8)
            Ai = x_bf[:, 2, :].rearrange("p (g s c) -> p g s c", g=8, s=8)
            Rr = x_bf[:, 3, :].rearrange("p (g s c) -> p g s c", g=8, s=8)
            Rin = x_bf[:, 0, :].rearrange("p (g s c) -> p g s c", g=8, s=8)
            Etr = yT[:, 0:2, :].rearrange("p a n -> p (a n)").rearrange("p (g s c) -> p g s c", g=8, s=8)
            Etin = yT[:, 2:4, :].rearrange("p a n -> p (a n)").rearrange("p (g s c) -> p g s c", g=8, s=8)
            S4 = [128, 8, 8, 16]
            pwb = lambda p3, i0: p3[:, :, i0:i0 + 8].unsqueeze(3).to_broadcast(S4)
            vb = lambda a: g3(a).unsqueeze(2).to_broadcast(S4)
            cmul(Ar, Ai, pwb(pwr3, 0), pwb(pwi3, 0), vb(Bbr), vb(Bbi), g4)
            cmul(Rr, Rin, pwb(pwr3, 8), pwb(pwi3, 8), vb(Cre), vb(Cim), g4, neg_i=True)
            cmul(Etr, Etin, pwb(pwr3, 16), pwb(pwi3, 16), vb(Cre), vb(Cim), g4, neg_i=True)
            Rm = arX[:, 0:4096].rearrange("p (g r n) -> p g r n", g=16, r=2)
            P.op("pool", lambda e: e.memset(Rm, 0.0), reads=[K], writes=["RmZ"])
            P.op("pool", lambda e: e.memset(ez[:], 0.0), reads=[K], writes=["ez"])
            for gl in range(2):
                rs_ = slice(gl * 64, (gl + 1) * 64)
                for r, (srcR, srcE) in enumerate(((Rr, Etr), (Rin, Etin))):
                    P.op("dve", lambda e, rs_=rs_, gl=gl, r=r, srcR=srcR: e.tensor_copy(
                        out=Rm[rs_, gl:16:2, r, :], in_=srcR[rs_].rearrange("p g s c -> p g (s c)")), reads=[K, "RmZ"], writes=[K])
                    P.op("dve", lambda e, rs_=rs_, gl=gl, r=r, srcE=srcE: e.tensor_copy(
                        out=ez[rs_, gl:16:2, r, :], in_=srcE[rs_].rearrange("p g s c -> p g (s c)")), reads=[K], writes=["ez"])
            for gp in range(8):
                pf, pfk = nf()
                for gl in range(2):
                    for r, A_ in enumerate((Ar, Ai)):
                        P.op("pe", lambda e, pf=pf, gp=gp, gl=gl, r=r, A_=A_: e.matmul(
                            pf[:, (gl * 2 + r) * 128:(gl * 2 + r + 1) * 128], lhsT=A_[:, gp].rearrange("p s c -> p (s c)"), rhs=imask[:, gl, :],
                            start=True, stop=True), reads=[K, "imask"], writes=[pfk])
                if gp % 2 == 0:
                    P.op("act", lambda e, pf=pf, gp=gp: e.activation(out=wsz[:, 2 * gp:2 * gp + 2].rearrange("p g r n -> p (g r n)"), in_=pf[:, :], func=AF.Identity),
                         reads=[pfk], writes=["wsz"])
                else:
                    P.op("dve", lambda e, pf=pf, gp=gp: e.tensor_copy(out=wsz[:, 2 * gp:2 * gp + 2].rearrange("p g r n -> p (g r n)"), in_=pf[:, :]),
                         reads=[pfk], writes=["wsz"])
            mtmp = tmp32(512)
            for g4i in range(4):
                pf, pfk = nf()
                for gi in range(4):
                    g = g4i * 4 + gi
                    gp = g // 2
                    P.op("pe", lambda e, pf=pf, gi=gi, g=g, gp=gp: e.matmul(pf[:, gi * 128:(gi + 1) * 128], lhsT=Ar[:, gp].rearrange("p s c -> p (s c)"),
                                                                    rhs=Rm[:, g, 0, :], start=True, stop=False), reads=[K], writes=[pfk])
                    P.op("pe", lambda e, pf=pf, gi=gi, g=g, gp=gp: e.matmul(pf[:, gi * 128:(gi + 1) * 128], lhsT=Ai[:, gp].rearrange("p s c -> p (s c)"),
                                                                    rhs=Rm[:, g, 1, :], start=False, stop=True), reads=[K], writes=[pfk])
                P.op("dve", lambda e, pf=pf: e.tensor_tensor(out=mtmp.rearrange("p (g n) -> p g n", g=4), in0=pf[:, :].rearrange("p (g n) -> p g n", g=4),
                                                             in1=cm[:].unsqueeze(1).to_broadcast([128, 4, 128]), op=ALU.mult), reads=[pfk, "cm", K], writes=[K])
                for gi in range(4):
                    g = g4i * 4 + gi
                    P.op("dve", lambda e, gi=gi, g=g: e.scalar_tensor_tensor(out=mloc[:, g, :], in0=identf[:], scalar=dcol[:, g:g + 1],
                                                                             in1=mtmp[:, gi * 128:(gi + 1) * 128], op0=ALU.mult, op1=ALU.add),
                         reads=RK + ["identf"], writes=["mloc"])
            fr3 = k3(fr)
            tt_ = tmpw(8 * J); tf_ = tmpw(8 * J)
            tib = tmpw(8 * J).bitcast(I32)
            tqb = tmpw(8 * J)
            for gp in range(8):
                P.op("dve", lambda e, gp=gp: e.tensor_scalar(out=tt_[:, gp * J:(gp + 1) * J], in0=jv, scalar1=fr3[:, gp, 23:24], scalar2=None, op0=ALU.mult),
                     reads=RK, writes=[K])

            def reduce_big(src, dst):
                P.op("dve", lambda e: e.tensor_copy(out=tib, in_=src), reads=[K], writes=[K])
                P.op("dve", lambda e: e.tensor_copy(out=tqb, in_=tib), reads=[K], writes=[K])
                P.op("dve", lambda e: e.tensor_tensor(out=dst, in0=src, in1=tqb, op=ALU.subtract), reads=[K], writes=[K])

            reduce_big(tt_, tf_)
            P.op("act", lambda e: e.activation(out=sinT, in_=tf_, func=AF.Sin, scale=TWO_PI), reads=[K], writes=["tabs"])
            P.op("dve", lambda e: e.tensor_scalar_add(out=tt_, in0=tf_, scalar1=0.25), reads=[K], writes=[K])
            reduce_big(tt_, tf_)
            P.op("act", lambda e: e.activation(out=cosT, in_=tf_, func=AF.Sin, scale=TWO_PI), reads=[K], writes=["tabs"])
            P.op("dve", lambda e: e.tensor_copy(out=r8, in_=k3(mag)[:, :, 23]), reads=[K], writes=["tabs"])
            P.op("dve", lambda e: e.tensor_copy(out=decT.rearrange("p (g j) -> p g j", g=8), in_=k3(mag)[:, :, 23:24].to_broadcast([128, 8, J])), reads=[K], writes=["tabs"])
            P.op("dve", lambda e: e.memset(decT.rearrange("p (g j) -> p g j", g=8)[:, :, 0], 0.0), reads=["tabs"], writes=["tabs"])
            P.dma(lambda e, l=l: e.dma_start(out=sc_ws[l], in_=wsz[:].rearrange("p g r n -> p (g r n)")), reads=["wsz"], writes=["scws%d" % l])
            P.dma(lambda e, l=l: e.dma_start(out=sc_ml[l], in_=mloc[:].rearrange("p g n -> p (g n)")), reads=["mloc"], writes=["scml%d" % l])
            P.dma(lambda e, l=l: e.dma_start(out=sc_ez[l], in_=ez[:].rearrange("p g r n -> p (g r n)")), reads=["ez"], writes=["scez%d" % l])
            P.dma(lambda e, l=l: e.dma_start(out=sc_tb[l], in_=tabs[:]), reads=["tabs"], writes=["sctb%d" % l])

        for l in reversed(range(DEPTH)):
            setup_layer(l)

        segs = [(b, sg, l) for b in range(NB) for sg in range(NSEG) for l in range(DEPTH)]
        if n_segs_limit is not None:
            segs = segs[:n_segs_limit]

        def load_wout_ln(l):
            for kq in range(4):
                P.dma(lambda e, l=l, kq=kq: e.dma_start(out=wout[:, 2 * kq:2 * kq + 2, :],
                                                        in_=dr["w_out"][l, kq * 256:(kq + 1) * 256, :].rearrange("(k p) n -> p k n", p=128)),
                      writes=["wout"], q="pool")
            P.dma(lambda e, l=l: e.dma_start(out=lnG[:], in_=dr["ln_g"][l].rearrange("(o n) -> o n", o=1).to_broadcast([128, D])), writes=["lnG"])
            P.dma(lambda e, l=l: e.dma_start(out=lnB[:], in_=dr["ln_b"][l].rearrange("(o n) -> o n", o=1).to_broadcast([128, D])), writes=["lnB"])

        def load_derived(l):
            P.dma(lambda e, l=l: e.dma_start(out=wsz[:].rearrange("p g r n -> p (g r n)"), in_=sc_ws[l]), reads=["scws%d" % l], writes=["wsz"])
            P.dma(lambda e, l=l: e.dma_start(out=mloc[:].rearrange("p g n -> p (g n)"), in_=sc_ml[l]), reads=["scml%d" % l], writes=["mloc"])
            P.dma(lambda e, l=l: e.dma_start(out=ez[:].rearrange("p g r n -> p (g r n)"), in_=sc_ez[l]), reads=["scez%d" % l], writes=["ez"])
            P.dma(lambda e, l=l: e.dma_start(out=tabs[:], in_=sc_tb[l]), reads=["sctb%d" % l], writes=["tabs"])

        def rstd_newton(y, v, t, ti, n, reads, wkey):
            P.op("dve", lambda e: e.tensor_scalar_add(out=v, in0=v, scalar1=EPS), reads=reads, writes=[wkey])
            P.op("dve", lambda e: e.tensor_single_scalar(out=ti, in_=v.bitcast(I32), scalar=1, op=ALU.arith_shift_right), reads=[wkey], writes=[wkey])
            P.op("dve", lambda e: e.tensor_scalar(out=y.bitcast(I32), in0=ti, scalar1=-1.0, scalar2=1597463007.0, op0=ALU.mult, op1=ALU.add), reads=[wkey], writes=[wkey])
            for it in range(3):
                P.op("dve", lambda e: e.tensor_tensor(out=t, in0=y, in1=y, op=ALU.mult), reads=[wkey], writes=[wkey])
                P.op("dve", lambda e: e.tensor_tensor(out=t, in0=t, in1=v, op=ALU.mult), reads=[wkey], writes=[wkey])
                P.op("dve", lambda e: e.tensor_scalar(out=t, in0=t, scalar1=-0.5, scalar2=1.5, op0=ALU.mult, op1=ALU.add), reads=[wkey], writes=[wkey])
                P.op("dve", lambda e: e.tensor_tensor(out=y, in0=y, in1=t, op=ALU.mult), reads=[wkey], writes=[wkey])

        ev = [0]

        def evac(out_ap, in_ap, reads, writes, func=None, eng=None, **kw):
            if func is not None:
                P.op("act", lambda e: e.activation(out=out_ap, in_=in_ap, func=func, **kw), reads=reads, writes=writes)
                return
            if eng is None:
                eng = "act"
            if eng == "act":
                P.op("act", lambda e: e.activation(out=out_ap, in_=in_ap, func=AF.Identity), reads=reads, writes=writes)
            else:
                P.op(eng, lambda e: e.tensor_copy(out=out_ap, in_=in_ap), reads=reads, writes=writes)

        t1, t2, t3, hdr = [x_bf[:, i, :].bitcast(F32) for i in range(4)]
        hdi = hdi_t[:, :]
        KS = "s5t"
        j3 = lambda a: a.rearrange("p (g j) -> p g j", g=8)

        class Stages:
            pass

        def x_load(si, tiles):
            b, sg, l = segs[si]
            tok0 = sg * SEG
            for tt in tiles:
                P.dma(lambda e, b=b, tt=tt, tok0=tok0: e.dma_start(out=x_tok[:, tt, :], in_=dr["x"][b, tok0 + tt * 128: tok0 + (tt + 1) * 128, :]),
                      writes=["x_tok%d" % tt])

        def seg_prologue_x(si):
            b, sg, l = segs[si]
            if l == 0:
                x_load(si, range(4))
            if l == 0 and sg == 0:
                for r in range(2):
                    P.op("dve", lambda e, r=r: e.memset(cH[0][r][:], 0.0), writes=["cH0"])
                    P.op("dve", lambda e, r=r: e.memset(cH[1][r][:], 0.0), writes=["cH1"])

        def seg_prologue(si):
            b, sg, l = segs[si]
            if sg == 0:
                P.dma(lambda e, b=b: e.dma_start(out=mem_bf, in_=dr["mem"][b].rearrange("(t p) d -> p t d", p=128)), writes=["memb"], q="pool")
                for kt in range(8):
                    pb, pbk = nb_()
                    for mt in range(2):
                        P.op("pe", lambda e, pb=pb, kt=kt, mt=mt: e.transpose(pb[:, mt * 128:(mt + 1) * 128], mem_bf[:, mt, kt * 128:(kt + 1) * 128], identb[:]),
                             reads=["memb", "identb"], writes=[pbk])
                    evac(memT[:, kt, :], pb[:, 0:256], [pbk], ["memT"])
                for c2 in range(2):
                    pf, pfk = nf()
                    for kt in range(8):
                        P.op("pe", lambda e, pf=pf, kt=kt, c2=c2: e.matmul(pf[:, 0:256], lhsT=wk[:, kt, c2 * 128:(c2 + 1) * 128], rhs=memT[:, kt, :],
                                                                     start=(kt == 0), stop=(kt == 7)), reads=["wk", "memT"], writes=[pfk])
                    for hh in range(2):
                        rs_ = slice(hh * 64, (hh + 1) * 64)
                        evac(kTm[l][rs_, 2 * c2 + hh, :], pf[rs_, 0:256], [pfk], ["kTm%d" % l])
                for mt in range(2):
                    pf, pfk = nf()
                    for kt in range(8):
                        P.op("pe", lambda e, pf=pf, kt=kt, mt=mt: e.matmul(pf[:, 0:256], lhsT=memT[:, kt, mt * 128:(mt + 1) * 128], rhs=wv[:, kt, :],
                                                                     start=(kt == 0), stop=(kt == 7)), reads=["wv", "memT"], writes=[pfk])
                    for h in range(4):
                        evac(vm[l][:, mt, h, (h % 2) * 64:(h % 2) * 64 + 64], pf[:, h * 64:(h + 1) * 64], [pfk], ["vm%d" % l])
                for sj in range(si + 1, len(segs)):
                    if segs[sj][1] == 0:
                        load_wkv(segs[sj][2])
                        break

        def make_block(si, blk):
            b, sg, l = segs[si]
            tok0 = sg * SEG
            T0 = blk * 4
            last_blk = (blk == NBLK - 1)
            nxt_l = segs[si + 1][2] if si + 1 < len(segs) else None
            S = Stages()
            S.si, S.blk, S.l, S.last_blk = si, blk, l, last_blk

            def prefetch(name):
                if last_blk and nxt_l is not None:
                    load_win_group(nxt_l, name)

            def A0a():
                engs = ["act", "act", "act", "act"]
                for tt in range(4):
                    if engs[tt] == "act":
                        P.op("act", lambda e, tt=tt: e.activation(out=x_bf[:, tt, :], in_=x_tok[:, T0 + tt, :], func=AF.Identity),
                             reads=["x_tok%d" % (T0 + tt)], writes=["x_bf%d" % tt])
                    else:
                        P.op(engs[tt], lambda e, tt=tt: e.tensor_copy(out=x_bf[:, tt, :], in_=x_tok[:, T0 + tt, :]),
                             reads=["x_tok%d" % (T0 + tt)], writes=["x_bf%d" % tt])

            def A0():
                for kt in range(8):
                    pb, pbk = nb_()
                    for tt in range(4):
                        P.op("pe", lambda e, pb=pb, kt=kt, tt=tt: e.transpose(pb[:, tt * 128:(tt + 1) * 128], x_bf[:, tt, kt * 128:(kt + 1) * 128], identb[:]),
                             reads=["x_bf%d" % tt, "identb"], writes=[pbk])
                    evac(xT[:, kt, :], pb[:, 0:512], [pbk], ["xT"])

            def A1(tt):
                pf, pfk = nf()
                for kt in range(8):
                    P.op("pe", lambda e, pf=pf, kt=kt: e.matmul(pf[:, :], lhsT=xT[:, kt, tt * 128:(tt + 1) * 128], rhs=win[:, kt, 512:1024],
                                                          start=(kt == 0), stop=(kt == 7)), reads=["xT", "win_v"], writes=[pfk])
                evac(vn[:, tt, :], pf[:, :], [pfk], ["vn"], func=AF.Gelu)
                v3 = vn[:, tt, :].rearrange("p (h d) -> p h d", h=4)
                P.op("dve", lambda e: e.tensor_reduce(out=lnst[:, 0, tt * 4:(tt + 1) * 4], in_=v3, axis=mybir.AxisListType.X, op=ALU.add),
                     reads=["vn"], writes=["lnst"])
                P.op("dve", lambda e: e.tensor_tensor(out=tmpA, in0=vn[:, tt, :], in1=vn[:, tt, :], op=ALU.mult), reads=["vn"], writes=["tmpA"])
                P.op("dve", lambda e: e.tensor_reduce(out=lnst[:, 1, tt * 4:(tt + 1) * 4], in_=tmpA.rearrange("p (h d) -> p h d", h=4), axis=mybir.AxisListType.X, op=ALU.add),
                     reads=["tmpA"], writes=["lnst"])

            def A1_finish():
                prefetch("v")
                inv = 1.0 / 128.0
                P.op("dve", lambda e: e.tensor_scalar_mul(out=mv[:, :, 0], in0=lnst[:, 0, :], scalar1=inv), reads=["lnst"], writes=["mv"])
                P.op("dve", lambda e: e.tensor_tensor(out=mv[:, :, 1], in0=mv[:, :, 0], in1=mv[:, :, 0], op=ALU.mult), reads=["mv"], writes=["mv"])
                P.op("dve", lambda e: e.scalar_tensor_tensor(out=mv[:, :, 1], in0=lnst[:, 1, :], scalar=inv, in1=mv[:, :, 1], op0=ALU.mult, op1=ALU.subtract),
                     reads=["lnst", "mv"], writes=["mv"])
                P.op("act", lambda e: e.activation(out=rstd[:], in_=mv[:, :, 1], func=AF.Sqrt, bias=epst[:, 0:1], scale=1.0), reads=["mv", "epst"], writes=["rstd"])
                P.op("dve", lambda e: e.reciprocal(out=rstd[:], in_=rstd[:]), reads=["rstd"], writes=["rstd"])
                for tt in range(4):
                    for h in range(4):
                        i = tt * 4 + h
                        P.op("dve", lambda e, tt=tt, h=h, i=i: e.tensor_scalar(out=vn[:, tt, h * 128:(h + 1) * 128], in0=vn[:, tt, h * 128:(h + 1) * 128],
                                                                               scalar1=mv[:, i, 0:1], scalar2=rstd[:, i:i + 1], op0=ALU.subtract, op1=ALU.mult),
                             reads=["vn", "mv", "rstd"], writes=["vn"])

            def inproj(grp, out_ap, wkey, func=None, sub=0):
                col0 = WIN_G[grp][0] + sub
                pf, pfk = nf()
                for kt in range(8):
                    P.op("pe", lambda e, pf=pf, kt=kt, col0=col0: e.matmul(pf[:, :], lhsT=win[:, kt, col0:col0 + 128], rhs=xT[:, kt, :],
                                                                     start=(kt == 0), stop=(kt == 7)), reads=["xT", "win_" + grp], writes=[pfk])
                evac(out_ap, pf[:, :], [pfk], [wkey], func=func)

            def inproj_gate(grp, out_ap, wkey, sub=0):
                col0 = WIN_G[grp][0] + sub
                pf, pfk = nf()
                for kt in range(8):
                    P.op("pe", lambda e, pf=pf, kt=kt, col0=col0: e.matmul(pf[:, :], lhsT=win[:, kt, col0:col0 + 128], rhs=xT[:, kt, :],
                                                                     start=(kt == 0), stop=(kt == 7)), reads=["xT", "win_" + grp], writes=[pfk])
                P.op("act", lambda e, pf=pf: e.activation(out=out_ap, in_=pf[:, :], func=AF.Tanh, scale=0.5), reads=[pfk], writes=[wkey])
                P.op("dve", lambda e, pf=pf: e.scalar_tensor_tensor(out=out_ap, in0=out_ap, scalar=1.0, in1=pf[:, :], op0=ALU.add, op1=ALU.mult),
                     reads=[pfk, wkey], writes=[wkey])

            def Head(h):
                ug_, gs_ = ugs[h % 2]
                uk, gk = "ug", "gs"
                inproj("u%d" % h, ug_, uk, AF.Gelu)
                prefetch("u%d" % h)
                inproj_gate("g%d" % h, gs_, gk)
                prefetch("g%d" % h)
                P.op("dve", lambda e: e.tensor_tensor(out=ug_, in0=ug_, in1=gs_, op=ALU.mult), reads=[uk, gk], writes=[uk])
                pf3, pfk3 = nf()
                for tt in range(4):
                    P.op("pe", lambda e, pf3=pf3, tt=tt: e.matmul(pf3[:, tt * 128:(tt + 1) * 128], lhsT=vn[:, tt, h * 128:(h + 1) * 128], rhs=wTg[l][:, h, :],
                                                            start=True, stop=True), reads=["vn", "wTg%d" % l], writes=[pfk3])
                P.op("dve", lambda e, pf3=pf3: e.scalar_tensor_tensor(
                    out=tmpA.rearrange("p (a t) -> p a t", a=4), in0=pf3[:, :].rearrange("p (a t) -> p a t", a=4), scalar=lng[l][:, h:h + 1],
                    in1=bias2[l][:, h, :].unsqueeze(1).to_broadcast([128, 4, 128]), op0=ALU.mult, op1=ALU.add),
                    reads=[pfk3, "lng%d" % l, "bias2_%d" % l], writes=["tmpA"])
                P.op("dve", lambda e: e.tensor_tensor(out=yT[:, h, :], in0=tmpA, in1=ug_, op=ALU.mult), reads=["tmpA", uk], writes=["yT%d" % h])

            def A3():
                for c in range(2):
                    inproj("xb", xbT[:, c, :], "xbT", sub=c * 128)
                for c in range(2):
                    inproj_gate("gb", gbs[:, c, :], "gbs", sub=c * 128)
                prefetch("xb"); prefetch("gb")

            def Attn(c):
                inproj("q%d" % c, qT, "qT")
                prefetch("q%d" % c)
                inproj_gate("gx%d" % c, gxs, "gxs")
                prefetch("gx%d" % c)
                for hh in range(2):
                    h = 2 * c + hh
                    for mt in range(2):
                        pfs, pfsk = nf()
                        P.op("pe", lambda e, pfs=pfs, h=h, mt=mt: e.matmul(pfs[:, :], lhsT=kTm[l][:, h, mt * 128:(mt + 1) * 128], rhs=qT,
                                                                     start=True, stop=True), reads=["kTm%d" % l, "qT"], writes=[pfsk])
                        evac(pT[:, mt, hh, :], pfs[:, :], [pfsk], ["pT"], func=AF.Exp, scale=0.125)
                pfo, pfok = nf()
                pfd, pfdk = nf()
                n = 0
                for hh in range(2):
                    h = 2 * c + hh
                    for mt in range(2):
                        P.op("pe", lambda e, h=h, mt=mt, hh=hh, n=n: e.matmul(pfo[:, :], lhsT=vm[l][:, mt, h, :], rhs=pT[:, mt, hh, :],
                                                                        start=(n == 0), stop=(n == 3)), reads=["vm%d" % l, "pT"], writes=[pfok])
                        n += 1
                n = 0
                for hh in range(2):
                    for mt in range(2):
                        P.op("pe", lambda e, mt=mt, hh=hh, n=n: e.matmul(pfd[:, :], lhsT=hmask[:, hh, :], rhs=pT[:, mt, hh, :],
                                                                   start=(n == 0), stop=(n == 3)), reads=["hmask", "pT"], writes=[pfdk])
                        n += 1
                P.op("dve", lambda e: e.reciprocal(out=recip, in_=pfd[:, :]), reads=[pfdk], writes=["recip"])
                P.op("dve", lambda e: e.tensor_tensor(out=recip, in0=recip, in1=gxs, op=ALU.mult), reads=["recip", "gxs"], writes=["recip"])
                P.op("dve", lambda e: e.tensor_tensor(out=yT[:, 6 + c, :], in0=pfo[:, :], in1=recip, op=ALU.mult),
                     reads=[pfok, "recip"], writes=["yT%d" % (6 + c)])

            def B1():
                for c in range(2):
                    pf, pfk = nf()
                    for gl in range(8):
                        for s in range(8):
                            P.op("pe", lambda e, pf=pf, gl=gl, s=s, c=c: e.matmul(pf[:, gl * J:(gl + 1) * J], lhsT=zu[:, gl, 112 - 16 * s:240 - 16 * s],
                                                                            rhs=xbT[:, c, s:BLK:8], start=(s == 0), stop=(s == 7)),
                                 reads=["zu", "xbT"], writes=[pfk])
                    evac(Xg[:, c * 8:(c + 1) * 8, :].rearrange("p g j -> p (g j)"), pf[:, :], [pfk], ["Xg"])

            pS = []

            def B2():
                for r in range(2):
                    pf, pfk = nf()
                    pS.append((pf, pfk))
                    for gp in range(8):
                        for gl in range(2):
                            g = 2 * gp + gl
                            P.op("pe", lambda e, pf=pf, g=g, gp=gp, gl=gl, r=r: e.matmul(pf[:, gp * J:(gp + 1) * J], lhsT=wsz[:, g, r, :], rhs=Xg[:, g, :],
                                                                                   start=(gl == 0), stop=(gl == 1)), reads=["wsz", "Xg"], writes=[pfk])

            def B3():
                (pSr, pSrk), (pSi, pSik) = pS
                P.op("dve", lambda e: e.tensor_tensor(out=t1, in0=pSr[:, :], in1=cosT, op=ALU.mult), reads=[pSrk, "tabs"], writes=[KS])
                P.op("dve", lambda e: e.tensor_tensor(out=t3, in0=pSi[:, :], in1=sinT, op=ALU.mult), reads=[pSik, "tabs"], writes=[KS])
                P.op("dve", lambda e: e.tensor_tensor(out=t1, in0=t1, in1=t3, op=ALU.add), reads=[KS], writes=[KS])
                P.op("dve", lambda e: e.tensor_tensor(out=t2, in0=pSi[:, :], in1=cosT, op=ALU.mult), reads=[pSik, "tabs", KS], writes=[KS])
                P.op("dve", lambda e: e.tensor_tensor(out=t3, in0=pSr[:, :], in1=sinT, op=ALU.mult), reads=[pSrk, "tabs", KS], writes=[KS])
                P.op("dve", lambda e: e.tensor_tensor(out=t2, in0=t2, in1=t3, op=ALU.subtract), reads=[KS], writes=[KS])

            def B4():
                for r in range(2):
                    P.op("dve", lambda e, r=r: e.tensor_copy(out=Hb[:, r, :, 0], in_=cH[l][r][:]), reads=["cH%d" % l], writes=["Hb"])
                for r, (src, dst) in enumerate(((t1, hdr), (t2, hdi))):
                    P.op("dve", lambda e, r=r: e.tensor_tensor(out=car[:, r, :], in0=cH[l][r][:], in1=r8, op=ALU.mult), reads=["cH%d" % l, "tabs"], writes=["car"])
                    P.op("dve", lambda e, r=r, src=src: e.tensor_tensor(out=j3(src)[:, :, 0], in0=j3(src)[:, :, 0], in1=car[:, r, :], op=ALU.add), reads=[KS, "car"], writes=[KS])
                    P.op("dve", lambda e, src=src, dst=dst: e.tensor_tensor_scan(out=dst, data0=decT, data1=src, initial=0.0, op0=ALU.mult, op1=ALU.add),
                         reads=[KS, "tabs"], writes=[KS])

            def B5():
                P.op("dve", lambda e: e.tensor_tensor(out=t1, in0=hdr, in1=cosT, op=ALU.mult), reads=[KS, "tabs"], writes=[KS])
                P.op("dve", lambda e: e.tensor_tensor(out=t3, in0=hdi, in1=sinT, op=ALU.mult), reads=[KS, "tabs"], writes=[KS])
                P.op("dve", lambda e: e.tensor_tensor(out=Hb[:, 0, :, 1:J + 1], in0=j3(t1), in1=j3(t3), op=ALU.subtract), reads=[KS], writes=["Hb"])
                P.op("dve", lambda e: e.tensor_tensor(out=cH[l][0][:], in0=j3(t1)[:, :, J - 1], in1=j3(t3)[:, :, J - 1], op=ALU.subtract), reads=[KS], writes=["cH%d" % l])
                P.op("dve", lambda e: e.tensor_tensor(out=t1, in0=hdi, in1=cosT, op=ALU.mult), reads=[KS, "tabs"], writes=[KS])
                P.op("dve", lambda e: e.tensor_tensor(out=t3, in0=hdr, in1=sinT, op=ALU.mult), reads=[KS, "tabs"], writes=[KS])
                P.op("dve", lambda e: e.tensor_tensor(out=Hb[:, 1, :, 1:J + 1], in0=j3(t1), in1=j3(t3), op=ALU.add), reads=[KS], writes=["Hb"])
                P.op("dve", lambda e: e.tensor_tensor(out=cH[l][1][:], in0=j3(t1)[:, :, J - 1], in1=j3(t3)[:, :, J - 1], op=ALU.add), reads=[KS], writes=["cH%d" % l])

            def B6():
                for c in range(2):
                    pf, pfk = nf()
                    for gi in range(8):
                        g = c * 8 + gi
                        gp = g // 2
                        P.op("pe", lambda e, pf=pf, gi=gi, g=g: e.matmul(pf[:, gi * J:(gi + 1) * J], lhsT=mloc[:, g, :], rhs=Xg[:, g, :], start=True, stop=False),
                             reads=["mloc", "Xg"], writes=[pfk])
                        P.op("pe", lambda e, pf=pf, gi=gi, g=g, gp=gp: e.matmul(pf[:, gi * J:(gi + 1) * J], lhsT=ez[:, g, 0, :], rhs=Hb[:, 0, gp, 0:J], start=False, stop=False),
                             reads=["ez", "Hb"], writes=[pfk])
                        P.op("pe", lambda e, pf=pf, gi=gi, g=g, gp=gp: e.matmul(pf[:, gi * J:(gi + 1) * J], lhsT=ez[:, g, 1, :], rhs=Hb[:, 1, gp, 0:J], start=False, stop=True),
                             reads=["ez", "Hb"], writes=[pfk])
                    evac(Yg[:, c * 8:(c + 1) * 8, :].rearrange("p g j -> p (g j)"), pf[:, :], [pfk], ["Yg"], func=AF.Gelu)
                if last_blk and nxt_l is not None:
                    load_derived(nxt_l)

            def B7():
                for c in range(2):
                    pf, pfk = nf()
                    for t in range(8):
                        for gl in range(8):
                            P.op("pe", lambda e, pf=pf, t=t, gl=gl, c=c: e.matmul(pf[:, t * J:(t + 1) * J], lhsT=zu[:, t, 112 - 16 * gl:240 - 16 * gl],
                                                                            rhs=Yg[:, c * 8 + gl, :], start=(gl == 0), stop=(gl == 7)),
                                 reads=["zu", "Yg"], writes=[pfk])
                    evac(ybT[:, c, :].rearrange("p (j t) -> p t j", t=8), pf[:, :].rearrange("p (t j) -> p t j", t=8), [pfk], ["ybT"])

            def B8():
                for co in range(2):
                    pf, pfk = nf()
                    for kc in range(2):
                        P.op("pe", lambda e, pf=pf, kc=kc, co=co: e.matmul(pf[:, :], lhsT=gluw[l][:, kc, co * 128:(co + 1) * 128], rhs=ybT[:, kc, :],
                                                                     start=(kc == 0), stop=(kc == 1)), reads=["gluw%d" % l, "ybT"], writes=[pfk])
                    P.op("act", lambda e, pf=pf, co=co: e.activation(out=sgt[:], in_=pf[:, :], func=AF.Tanh, bias=glub[l][:, co:co + 1], scale=0.5),
                         reads=[pfk, "glub%d" % l], writes=["sg"])
                    P.op("dve", lambda e, co=co: e.scalar_tensor_tensor(out=sgt[:], in0=sgt[:], scalar=1.0, in1=gbs[:, co, :], op0=ALU.add, op1=ALU.mult),
                         reads=["sg", "gbs"], writes=["sg"])
                    P.op("dve", lambda e, co=co: e.scalar_tensor_tensor(out=yT[:, 4 + co, :], in0=ybT[:, co, :], scalar=0.25, in1=sgt[:], op0=ALU.mult, op1=ALU.mult),
                         reads=["sg", "ybT"], writes=["yT%d" % (4 + co)])

            def C(tt):
                T = T0 + tt
                acc, ak = accs[tt % 2]
                for nh in range(2):
                    pf, pfk = nf()
                    for kt in range(8):
                        P.op("pe", lambda e, pf=pf, kt=kt, nh=nh: e.matmul(pf[:, :], lhsT=yT[:, kt, tt * 128:(tt + 1) * 128], rhs=wout[:, kt, nh * 512:(nh + 1) * 512],
                                                                     start=(kt == 0), stop=(kt == 7)), reads=["yT%d" % kt, "wout"], writes=[pfk])
                    P.op("dve", lambda e, pf=pf, nh=nh: e.scalar_tensor_tensor(out=acc[:, nh * 512:(nh + 1) * 512], in0=x_tok[:, T, nh * 512:(nh + 1) * 512], scalar=ALPHA,
                                                                       in1=pf[:, :], op0=ALU.mult, op1=ALU.add),
                         reads=[pfk, "x_tok%d" % T], writes=[ak])
                    P.op("dve", lambda e, nh=nh: e.bn_stats(out=stc[:, tt % 2, nh, :], in_=acc[:, nh * 512:(nh + 1) * 512]), reads=[ak], writes=["stc%d" % (tt % 2)])
                P.op("dve", lambda e: e.bn_aggr(out=mvc[:, tt % 2, :], in_=stc[:, tt % 2].rearrange("p a s -> p (a s)")), reads=["stc%d" % (tt % 2)], writes=["mvc%d" % (tt % 2)])
                rs_ = rsc[:, tt % 2, :]
                rk = "rsc%d" % (tt % 2)
                P.op("act", lambda e: e.activation(out=rs_[:, 0:1], in_=mvc[:, tt % 2, 1:2], func=AF.Sqrt, bias=epst[:, 0:1], scale=1.0), reads=["mvc%d" % (tt % 2), "epst"], writes=[rk])
                P.op("dve", lambda e: e.reciprocal(out=rs_[:, 0:1], in_=rs_[:, 0:1]), reads=[rk], writes=[rk])
                P.op("dve", lambda e: e.scalar_tensor_tensor(out=rs_[:, 1:2], in0=mvc[:, tt % 2, 0:1], scalar=-1.0, in1=rs_[:, 0:1], op0=ALU.mult, op1=ALU.mult),
                     reads=["mvc%d" % (tt % 2), rk], writes=[rk])
                P.op("act", lambda e: e.activation(out=acc, in_=acc, func=AF.Identity, bias=rs_[:, 1:2], scale=rs_[:, 0:1]), reads=[ak, rk], writes=[ak])

            def Cb(tt):
                T = T0 + tt
                acc, ak = accs[tt % 2]
                P.op("dve", lambda e: e.tensor_tensor(out=acc, in0=acc, in1=lnG[:], op=ALU.mult), reads=[ak, "lnG"], writes=[ak])
                P.op("dve", lambda e: e.tensor_tensor(out=x_tok[:, T, :], in0=acc, in1=lnB[:], op=ALU.add), reads=[ak, "lnB"], writes=["x_tok%d" % T])
                if l == DEPTH - 1:
                    P.dma(lambda e: e.dma_start(out=out[b, tok0 + T * 128: tok0 + (T + 1) * 128, :], in_=x_tok[:, T, :]),
                          reads=["x_tok%d" % T])

            S.A0, S.A1, S.A1_finish, S.A3, S.Head, S.Attn = A0, A1, A1_finish, A3, Head, Attn
            S.A0a = A0a
            S.B1, S.B2, S.B3, S.B4, S.B5, S.B6, S.B7, S.B8, S.C = B1, B2, B3, B4, B5, B6, B7, B8, C
            S.Cb = Cb
            return S

        blocks = [(si, blk) for si in range(len(segs)) for blk in range(NBLK)]
        l0 = segs[0][2]
        seg_prologue_x(0)
        seg_prologue(0)
        load_wout_ln(l0)
        cur = make_block(*blocks[0])
        cur.A0a()
        cur.A0()
        prev = None
        for k in range(len(blocks)):
            si, blk = blocks[k]
            nxt_blk = None
            if k + 1 < len(blocks):
                nsi, nblk = blocks[k + 1]
                nxt_blk = make_block(nsi, nblk)
            for tt in range(4):
                cur.A1(tt)
            cur.A1_finish()
            cur.A3(); cur.B1()
            if nxt_blk is not None and nblk == 0:
                seg_prologue(nsi)
            if prev is not None:
                prev.C(0); prev.C(1); prev.Cb(0); prev.C(2); prev.Cb(1); prev.C(3); prev.Cb(2); prev.Cb(3)
            if prev is not None and prev.last_blk:
                load_wout_ln(segs[si][2])
                if segs[si][2] == 0:
                    x_load(si, range(4, 8))
            elif prev is None and segs[si][2] == 0:
                x_load(si, range(4, 8))
            cur.Head(0); cur.B2(); cur.B3(); cur.Head(1); cur.B4(); cur.Head(2); cur.B5()
            cur.Head(3); cur.B6(); cur.B7()
            cur.Attn(0)
            if nxt_blk is not None:
                if nblk == 0:
                    seg_prologue_x(nsi)
                nxt_blk.A0a()
            cur.B8(); cur.Attn(1)
            if nxt_blk is not None:
                nxt_blk.A0()
            prev, cur = cur, nxt_blk
        for tt in range(4):
            prev.C(tt)
            prev.Cb(tt)
        P.emit()
    nc._plan = P
    return nc


_CACHE = {}


def kernel(**inputs):
    x = np.ascontiguousarray(inputs["x"], dtype=np.float32)
    mem = np.ascontiguousarray(inputs["mem"], dtype=np.float32)
    consts = host_consts()
    if "nc" not in _CACHE:
        _CACHE["nc"] = build_program()
    nc = _CACHE["nc"]
    in_maps = []
    for i in range(N_CORES):
        m = {"x": x[NB * i:NB * (i + 1)], "mem": mem[NB * i:NB * (i + 1)]}
        for k in W_SHAPES:
            m[k] = np.ascontiguousarray(inputs[k], dtype=np.float32)
        m.update(consts)
        in_maps.append(m)
    res = run_bass_kernel_spmd(nc, in_maps, core_ids=list(range(N_CORES)))
    outp = np.concatenate([r["out"] for r in res.results], axis=0)
    return outp.astype(np.float32)
```

```python
import math
import contextlib
import numpy as np
import concourse.bass as bass
import concourse.mybir as mybir
from concourse.bass_utils import run_bass_kernel_spmd

F32 = mybir.dt.float32
BF16 = mybir.dt.bfloat16
I32 = mybir.dt.int32
AF = mybir.ActivationFunctionType
ALU = mybir.AluOpType

N_CORES = 8
NB = 2
SEQ = 2048
D = 1024
MEM = 256
DEPTH = 2
SEG = 1024
NSEG = SEQ // SEG
BLK = 512
NBLK = SEG // BLK
J = BLK // 8
ALPHA = (2 * DEPTH) ** 0.25
EPS = 1e-5
TWO_PI = 2.0 * math.pi


class Plan:
    ENGS = ("pe", "act", "dve", "pool", "sp")

    def __init__(self, nc):
        self.nc = nc
        self.ops = {e: [] for e in self.ENGS}
        self.count = {e: 0 for e in self.ENGS}
        self.last_write = {}
        self.readers = {}
        self.known = {e: {} for e in self.ENGS}
        self.dma_sems = {}
        self.alias = {}

    def set_alias(self, a, others):
        for o in others:
            self.alias.setdefault(a, set()).add(o)
            self.alias.setdefault(o, set()).add(a)

    def _expand(self, keys):
        out = []
        for k in keys:
            out.append(k)
            for a in self.alias.get(k, ()):
                out.append(a)
        return out

    def _add(self, waits, name, val):
        if waits.get(name, 0) < val:
            waits[name] = val

    def _add_dep(self, waits, dep):
        if dep is None:
            return
        if dep[0] == "eng":
            self._add(waits, "c_" + dep[1], dep[2])
        else:
            self._add(waits, dep[1], dep[2])

    def _deps(self, eng, reads, writes):
        waits = {}
        for k in self._expand(reads):
            self._add_dep(waits, self.last_write.get(k))
        for k in self._expand(writes):
            self._add_dep(waits, self.last_write.get(k))
            for e, idx in self.readers.get(k, {}).items():
                if e.startswith("dma:"):
                    self._add(waits, e[4:], idx)
                else:
                    self._add(waits, "c_" + e, idx)
        if eng == "pe":
            waits.pop("c_pe", None)
        out = {}
        kn = self.known[eng]
        for name, val in waits.items():
            if kn.get(name, 0) >= val:
                continue
            kn[name] = val
            out[name] = val
        return out

    def op(self, eng, fn, reads=(), writes=(), tag=None):
        waits = self._deps(eng, reads, writes)
        if tag is not None:
            self.tags = getattr(self, "tags", {})
            self.tags.setdefault(tag, []).append((eng, dict(waits), dict(self.known[eng]), {k: self.last_write.get(k) for k in self._expand(reads)}))
        self.count[eng] += 1
        idx = self.count[eng]
        self.ops[eng].append((waits, fn, [("c_" + eng, 1)]))
        for k in reads:
            self.readers.setdefault(k, {})[eng] = idx
        for k in writes:
            self.last_write[k] = ("eng", eng, idx)
            self.readers[k] = {}
        return idx

    def dma(self, fn, reads=(), writes=(), sem=None, q="sp"):
        if len(writes) > 0:
            sem = "d_" + str(writes[0])
        else:
            sem = "d_o_" + str(reads[0])
        waits = self._deps(q, reads, writes)
        self.dma_sems[sem] = self.dma_sems.get(sem, 0) + 16
        val = self.dma_sems[sem]
        self.ops[q].append((waits, fn, [(sem, 16)]))
        for k in writes:
            self.last_write[k] = ("dma", sem, val)
            self.readers[k] = {}
        for k in reads:
            self.readers.setdefault(k, {})["dma:" + sem] = val
        return val

    def emit(self):
        nc = self.nc
        names = set(["c_" + e for e in self.ENGS]) | set(self.dma_sems.keys())
        waited = {"c_" + e: set() for e in self.ENGS}
        for e in self.ENGS:
            for waits, fn, incs in self.ops[e]:
                for n, v in waits.items():
                    if n in waited:
                        waited[n].add(v)
        rank = {n: {v: i + 1 for i, v in enumerate(sorted(vs))} for n, vs in waited.items()}
        pos = {e: 0 for e in self.ENGS}
        cnt = {e: 0 for e in self.ENGS}
        semv = {n: 0 for n in names}
        progressed = True
        while progressed:
            progressed = False
            for e in self.ENGS:
                while pos[e] < len(self.ops[e]):
                    waits, fn, incs = self.ops[e][pos[e]]
                    ok = True
                    for n, v in waits.items():
                        tv = rank[n][v] if n in rank else v
                        if semv[n] < tv:
                            ok = False
                            break
                    if not ok:
                        break
                    for n, v in incs:
                        if n in rank:
                            cnt[e] += 1
                            if cnt[e] in rank[n]:
                                semv[n] += 1
                        else:
                            semv[n] += v
                    pos[e] += 1
                    progressed = True
        assert all(pos[e] == len(self.ops[e]) for e in self.ENGS), "semaphore protocol deadlock"
        with contextlib.ExitStack() as st:
            sems = {n: st.enter_context(nc.semaphore(n)) for n in sorted(names)}
            block = st.enter_context(nc.Block())

            def replay(ename):
                def body(eng):
                    k = 0
                    for waits, fn, incs in self.ops[ename]:
                        for n, v in waits.items():
                            eng.wait_ge(sems[n], rank[n][v] if n in rank else v)
                        ins = fn(eng)
                        for n, v in incs:
                            if n in rank:
                                k += 1
                                if k in rank[n]:
                                    ins.then_inc(sems[n], 1)
                            else:
                                ins.then_inc(sems[n], v)
                    if ename == "sp":
                        for n, v in self.dma_sems.items():
                            eng.wait_ge(sems[n], v)
                return body

            block.tensor(replay("pe"))
            block.scalar(replay("act"))
            block.vector(replay("dve"))
            block.gpsimd(replay("pool"))
            block.sync(replay("sp"))


def host_consts():
    c = {}
    c["c_ident"] = np.eye(128, dtype=np.float32)
    zu = np.zeros((128, 8, 240), np.float32)
    for gl in range(8):
        for cc in range(16):
            zu[16 * gl + cc, gl, 112 + cc] = 1.0
    c["c_zu"] = zu.reshape(128, 8 * 240)
    sc = np.arange(128) // 16
    cm = (sc[None, :] >= sc[:, None]).astype(np.float32)
    c["c_cm"] = cm
    tri = (np.arange(128)[None, :] <= np.arange(128)[:, None]).astype(np.float32)
    c["c_tri"] = tri
    im = np.zeros((128, 2, 128), np.float32)
    for k in range(128):
        im[k, k // 64, k] = 1.0
    c["c_imask"] = im.reshape(128, 256)
    kvals = np.concatenate([np.arange(7, -1, -1), np.arange(-7, 1), np.arange(1, 9)]).astype(np.float32)
    kv = np.tile(kvals[None, None, :], (128, 8, 1))
    c["c_kv3"] = kv.reshape(128, 192)
    c["c_jv"] = np.tile(np.arange(1, J + 1, dtype=np.float32)[None, :], (128, 1))
    hm = np.zeros((128, 2, 128), np.float32)
    hm[:, 0, 0:64] = 2.0
    hm[:, 1, 64:128] = 2.0
    c["c_hmask"] = hm.reshape(128, 256)
    return c


CONST_SHAPES = {"c_ident": [128, 128], "c_zu": [128, 1920], "c_cm": [128, 128], "c_tri": [128, 128],
                "c_imask": [128, 256], "c_kv3": [128, 192], "c_jv": [128, J], "c_hmask": [128, 256]}

W_SHAPES = {
    "w_in": [DEPTH, D, 2560], "gm_w_s": [DEPTH, 4, 128, 128], "gm_b_s": [DEPTH, 4, 128],
    "gm_ln_g": [DEPTH, 4, 128], "gm_ln_b": [DEPTH, 4, 128], "ssm_lam_re": [DEPTH, 16, 64],
    "ssm_lam_im": [DEPTH, 16, 64], "ssm_log_step": [DEPTH, 16], "ssm_b_re": [DEPTH, 16, 64, 16],
    "ssm_b_im": [DEPTH, 16, 64, 16], "ssm_c_re": [DEPTH, 16, 16, 64], "ssm_c_im": [DEPTH, 16, 16, 64],
    "ssm_d": [DEPTH, 256], "glu_w": [DEPTH, 256, 256], "glu_b": [DEPTH, 256],
    "xa_w_k": [DEPTH, D, 256], "xa_w_v": [DEPTH, D, 256], "w_out": [DEPTH, D, D],
    "ln_g": [DEPTH, D], "ln_b": [DEPTH, D],
}


SBUF_FREE = [None]


def build_program(dbg=False, n_segs_limit=None):
    nc = bass.Bass("TRN2", target_bir_lowering=False)
    dr = {}
    dr["x"] = nc.dram_tensor("x", [NB, SEQ, D], F32, kind="ExternalInput").ap()
    dr["mem"] = nc.dram_tensor("mem", [NB, MEM, D], F32, kind="ExternalInput").ap()
    for k, shp in W_SHAPES.items():
        dr[k] = nc.dram_tensor(k, shp, F32, kind="ExternalInput").ap()
    for k, shp in CONST_SHAPES.items():
        dr[k] = nc.dram_tensor(k, shp, F32, kind="ExternalInput").ap()
    out = nc.dram_tensor("out", [NB, SEQ, D], F32, kind="ExternalOutput").ap()
    sck = dict(kind="ExternalOutput") if dbg else {}
    sc_ws = [nc.dram_tensor("sc_ws%d" % l, [128, 16 * 2 * 128], BF16, **sck).ap() for l in range(DEPTH)]
    sc_ml = [nc.dram_tensor("sc_ml%d" % l, [128, 16 * 128], BF16, **sck).ap() for l in range(DEPTH)]
    sc_ez = [nc.dram_tensor("sc_ez%d" % l, [128, 16 * 2 * 128], BF16, **sck).ap() for l in range(DEPTH)]
    sc_tb = [nc.dram_tensor("sc_tb%d" % l, [128, 3 * 8 * J + 8], F32, **sck).ap() for l in range(DEPTH)]
    if dbg:
        dbg_y = nc.dram_tensor("dbg_y", [128, 8 * BLK], F32, kind="ExternalOutput").ap()
        dbg_x = nc.dram_tensor("dbg_x", [128, 4 * D], F32, kind="ExternalOutput").ap()
    taps = {}

    def tap(P, name, ap, n, dt, reads):
        if not dbg:
            return
        t = nc.dram_tensor(name, [128, n], dt, kind="ExternalOutput").ap()
        P.dma(lambda e: e.dma_start(out=t, in_=ap), reads=reads, sem="d_dbg")

    P = Plan(nc)
    with contextlib.ExitStack() as st:
        def sb(name, shape, dt):
            return st.enter_context(nc.sbuf_tensor(name, shape, dt))

        def ps(name, shape, dt):
            return st.enter_context(nc.psum_tensor(name, shape, dt))

        psf = [ps("psf%d" % i, [128, 512], F32) for i in range(6)]
        psb = [ps("psb%d" % i, [128, 1024], BF16) for i in range(2)]
        rot = {"f": 0, "b": 0}

        def nf():
            i = rot["f"]; rot["f"] = (i + 1) % 6
            return psf[i], "psf%d" % i

        def nb_():
            i = rot["b"]; rot["b"] = (i + 1) % 2
            return psb[i], "psb%d" % i

        identf = sb("identf", [128, 128], F32)
        identb = sb("identb", [128, 128], BF16)
        zu = sb("zu", [128, 8, 240], BF16)
        cm = sb("cm", [128, 128], F32)
        imask = sb("imask", [128, 2, 128], BF16)
        hmask = sb("hmask", [128, 2, 128], BF16)
        onesb = sb("onesb", [128, 128], BF16)
        onesf = sb("onesf", [1, 128], F32)
        epst = sb("epst", [128, 1], F32)
        win = sb("win", [128, 8, 2560], BF16)
        wout = sb("wout", [128, 8, 1024], BF16)
        wk = sb("wk", [128, 8, 256], BF16)
        wv = sb("wv", [128, 8, 256], BF16)
        wsz = sb("wsz", [128, 16, 2, 128], BF16)
        mloc = sb("mloc", [128, 16, 128], BF16)
        ez = sb("ez", [128, 16, 2, 128], BF16)
        tabs = sb("tabs", [128, 3 * 8 * J + 8], F32)
        cosT = tabs[:, 0:8 * J]
        sinT = tabs[:, 8 * J:16 * J]
        r8 = tabs[:, 16 * J:16 * J + 8]
        decT = tabs[:, 16 * J + 8:24 * J + 8]
        lnG = sb("lnG", [128, D], F32)
        lnB = sb("lnB", [128, D], F32)
        wTg = [sb("wTg%d" % l, [128, 4, 128], BF16) for l in range(DEPTH)]
        bias2 = [sb("bias2_%d" % l, [128, 4, 128], F32) for l in range(DEPTH)]
        lng = [sb("lng%d" % l, [128, 4], F32) for l in range(DEPTH)]
        gluw = [sb("gluw%d" % l, [128, 2, 256], BF16) for l in range(DEPTH)]
        glub = [sb("glub%d" % l, [128, 2], F32) for l in range(DEPTH)]
        kTm = [sb("kTm%d" % l, [128, 4, 256], BF16) for l in range(DEPTH)]
        vm = [sb("vm%d" % l, [128, 2, 4, 128], BF16) for l in range(DEPTH)]
        cH = [[sb("cH%d_%d" % (l, r), [128, 8], F32) for r in range(2)] for l in range(DEPTH)]
        x_tok = sb("x_tok", [128, 8, D], F32)
        x_bf = sb("x_bf", [128, 4, D], BF16)
        xbT = sb("xbT", [128, 2, BLK], BF16)
        gbs = sb("gbs", [128, 2, BLK], BF16)
        yT = sb("yT", [128, 8, BLK], BF16)
        arX = sb("arX", [128, 6144], BF16)
        xT = arX[:, 0:4096].rearrange("p (k n) -> p k n", k=8)
        vn = arX[:, 4096:6144].rearrange("p (t n) -> p t n", t=4)
        hdi_t = sb("hdi_t", [128, 8 * J], F32)
        P.set_alias("s5t", ["x_bf0", "x_bf1", "x_bf2", "x_bf3"])
        mem_bf = x_bf[:, 0:2, :]
        memT = x_bf[:, 2:4, :].rearrange("p a n -> p (a n)").rearrange("p (k n) -> p k n", k=8)
        P.set_alias("memb", ["x_bf0", "x_bf1", "s5t"]); P.set_alias("memT", ["x_bf2", "x_bf3", "s5t"])
        arU = sb("arU", [128, 1024], BF16)
        ug = arU[:, 0:512]
        gs = arU[:, 512:1024]
        ugs = [(arU[:, 0:512], arU[:, 512:1024]), (arU[:, 0:512], arU[:, 512:1024])]
        Xg_t = sb("Xg_t", [128, 1024], BF16)
        Xg = Xg_t[:, :].rearrange("p (g j) -> p g j", g=16)
        arQ = sb("arQ", [128, 1024], BF16)
        qT = arQ[:, 0:512]
        gxs = arQ[:, 512:1024]
        Yg = arQ[:, :].rearrange("p (g j) -> p g j", g=16)
        P.set_alias("Yg", ["qT", "gxs"])
        arR = sb("arR", [128, 1056], BF16)
        recip = arR[:, 0:1024].bitcast(F32)
        Hb = arR[:, 0:2 * 8 * (J + 1)].rearrange("p (r g j) -> p r g j", r=2, g=8)
        P.set_alias("Hb", ["recip"])
        arV = sb("arV", [128, 1024], BF16)
        tmpA = arV[:, 0:512]
        ybT = arV[:, :].rearrange("p (c n) -> p c n", c=2)
        P.set_alias("ybT", ["tmpA"])
        sgt = sb("sgt", [128, BLK], BF16)
        arP = sb("arP", [128, 2048], BF16)
        pT = arP[:, :].rearrange("p (m h n) -> p m h n", m=2, h=2)
        acc0 = arP[:, :].bitcast(F32)
        acc1_t = sb("acc1_t", [128, D], F32)
        accs = [(acc0, "acc"), (acc1_t[:, :], "acc1")]
        P.set_alias("acc", ["pT"])
        st6 = sb("st6", [128, 16, 6], F32)
        mv = sb("mv", [128, 16, 2], F32)
        rstd = sb("rstd", [128, 16], F32)
        car = sb("car", [128, 2, 8], F32)
        lnst = sb("lnst", [128, 2, 16], F32)
        stc = sb("stc", [128, 2, 2, 6], F32)
        mvc = sb("mvc", [128, 2, 2], F32)
        rsc = sb("rsc", [128, 2, 2], F32)
        nw_v = sb("nw_v", [128, 18], F32)
        nw_t = sb("nw_t", [128, 18], F32)
        nw_i = sb("nw_i", [128, 18], I32)

        P.set_alias("setup", ["x_tok%d" % i for i in range(8)] + ["x_bf%d" % i for i in range(4)] + ["yT%d" % i for i in range(8)]
                    + ["x_bf", "xT", "vn", "s5t", "memb", "memT", "wout", "sin0", "sin1", "setupc", "RmZ"])
        sp_sem_ct = [0]
        SBUF_FREE[0] = nc.sbuf_bytes_remaining

        def sem_name(base):
            return "d_" + base

        P.dma(lambda e: e.dma_start(out=identf[:], in_=dr["c_ident"]), writes=["identf"], sem="d_c0")
        P.dma(lambda e: e.dma_start(out=cm[:], in_=dr["c_cm"]), writes=["cm"], sem="d_c1")
        P.dma(lambda e: e.dma_start(out=identb[:], in_=dr["c_ident"]), writes=["identb"], sem="d_c2", q="pool")
        P.dma(lambda e: e.dma_start(out=zu[:].rearrange("p g n -> p (g n)"), in_=dr["c_zu"]), writes=["zu"], sem="d_c3", q="pool")
        P.dma(lambda e: e.dma_start(out=imask[:].rearrange("p g n -> p (g n)"), in_=dr["c_imask"]), writes=["imask"], sem="d_c4", q="pool")
        P.dma(lambda e: e.dma_start(out=hmask[:].rearrange("p g n -> p (g n)"), in_=dr["c_hmask"]), writes=["hmask"], sem="d_c5", q="pool")
        P.op("dve", lambda e: e.memset(onesb[:], 1.0), writes=["onesb"])
        P.op("dve", lambda e: e.memset(onesf[:], 1.0), writes=["onesf"])
        P.op("dve", lambda e: e.memset(epst[:], EPS), writes=["epst"])
        for l in range(DEPTH):
            P.op("dve", lambda e, l=l: e.memset(kTm[l][:], 0.0), writes=["kTm%d" % l])
            P.op("pool", lambda e, l=l: e.memset(vm[l][:], 0.0), writes=["vm%d" % l])

        WIN_GROUPS = [("v", 512, 512), ("xb", 1536, 256), ("gb", 1792, 256)]
        for h in range(4):
            WIN_GROUPS += [("u%d" % h, h * 128, 128), ("g%d" % h, 1024 + h * 128, 128)]
        for c in range(2):
            WIN_GROUPS += [("q%d" % c, 2048 + c * 128, 128), ("gx%d" % c, 2304 + c * 128, 128)]
        WIN_G = {n: (c0, nc_) for n, c0, nc_ in WIN_GROUPS}

        def load_win_group(l, name):
            c0, ncol = WIN_G[name]
            P.dma(lambda e, l=l, c0=c0, ncol=ncol: e.dma_start(out=win[:, :, c0:c0 + ncol],
                                                               in_=dr["w_in"][l, :, c0:c0 + ncol].rearrange("(k p) n -> p k n", p=128)),
                  writes=["win_" + name], q="pool")

        def load_wkv(l):
            P.dma(lambda e, l=l: e.dma_start(out=wk[:], in_=dr["xa_w_k"][l].rearrange("(k p) n -> p k n", p=128)), writes=["wk"], q="pool")
            P.dma(lambda e, l=l: e.dma_start(out=wv[:], in_=dr["xa_w_v"][l].rearrange("(k p) n -> p k n", p=128)), writes=["wv"], q="pool")

        for name, _, _ in WIN_GROUPS:
            load_win_group(0, name)
        load_wkv(0)

        xt_flat = x_tok[:].rearrange("p t d -> p (t d)")
        wo_flat = wout[:].rearrange("p k n -> p (k n)").bitcast(F32)
        off = [0]
        off2 = [0]

        def tmp32(n):
            a = xt_flat[:, off[0]:off[0] + n]
            off[0] += n
            assert off[0] <= 8192, off[0]
            return a

        def tmpw(n):
            a = wo_flat[:, off2[0]:off2[0] + n]
            off2[0] += n
            assert off2[0] <= 4096
            return a

        NK = 24
        kv3 = tmp32(8 * NK)
        jv = tmp32(J)
        tri = tmp32(128)
        P.dma(lambda e: e.dma_start(out=kv3, in_=dr["c_kv3"]), writes=["setupc"])
        P.dma(lambda e: e.dma_start(out=jv, in_=dr["c_jv"]), writes=["setupc"])
        P.dma(lambda e: e.dma_start(out=tri, in_=dr["c_tri"]), writes=["setupc"])

        INP = []
        QS = ["sp", "act"]
        qi = [0]

        def qdma(fn, SK):
            P.dma(fn, writes=[SK], q=QS[qi[0] % 2])
            qi[0] += 1

        INP = [None] * DEPTH
        for l in reversed(range(DEPTH)):
            I_ = {}
            SK = "sin%d" % l
            I_["wraw"] = tmp32(512); I_["bs_bc"] = tmp32(512)
            I_["PA"] = tmp32(128); I_["LT"] = tmp32(128); I_["L16"] = tmp32(16)
            I_["Bre"] = tmp32(128); I_["Bim"] = tmp32(128); I_["CTr"] = tmp32(256); I_["CTi"] = tmp32(256)
            INP[l] = I_
            qdma(lambda e, l=l, a=I_["wraw"]: e.dma_start(out=a.rearrange("p (h s) -> p h s", h=4), in_=dr["gm_w_s"][l].rearrange("h t s -> t h s")), SK)
            qdma(lambda e, l=l, a=I_["bs_bc"]: e.dma_start(out=a, in_=dr["gm_b_s"][l].rearrange("(o h) d -> o (h d)", o=1).to_broadcast([128, 512])), SK)
            qdma(lambda e, l=l, a=I_["PA"]: e.dma_start(out=a[0:4, :], in_=dr["gm_ln_b"][l]), SK)
            qdma(lambda e, l=l, a=I_["PA"]: e.dma_start(out=a[4:8, :], in_=dr["gm_ln_g"][l]), SK)
            qdma(lambda e, l=l, a=I_["PA"]: e.dma_start(out=a[8:10, :], in_=dr["glu_b"][l].rearrange("(c p) -> c p", p=128)), SK)
            P.dma(lambda e, l=l: e.dma_start(out=gluw[l][:], in_=dr["glu_w"][l].rearrange("(k p) n -> p k n", p=128)), writes=["gluw%d" % l], q="pool")
            qdma(lambda e, l=l, a=I_["LT"]: e.dma_start(out=a[0:16, :].rearrange("g (o p) -> g o p", o=2),
                                                      in_=dr["ssm_lam_re"][l].unsqueeze(1).to_broadcast([16, 2, 64])), SK)
            qdma(lambda e, l=l, a=I_["LT"]: e.dma_start(out=a[16:32, :].rearrange("g (o p) -> g o p", o=2),
                                                      in_=dr["ssm_lam_im"][l].unsqueeze(1).to_broadcast([16, 2, 64])), SK)
            qdma(lambda e, l=l, a=I_["LT"]: e.dma_start(out=a[32:48, :].rearrange("g (s c) -> g s c", s=8),
                                                      in_=dr["ssm_d"][l].rearrange("(g o c) -> g o c", o=1, c=16).to_broadcast([16, 8, 16])), SK)
            qdma(lambda e, l=l, a=I_["L16"]: e.dma_start(out=a, in_=dr["ssm_log_step"][l].rearrange("(o g) -> o g", o=1).to_broadcast([128, 16])), SK)
            qdma(lambda e, l=l, a=I_["Bre"]: e.dma_start(out=a.rearrange("p (g c) -> p g c", g=8),
                                                       in_=dr["ssm_b_re"][l].rearrange("(gp gl) p c -> (gl p) gp c", gl=2)), SK)
            qdma(lambda e, l=l, a=I_["Bim"]: e.dma_start(out=a.rearrange("p (g c) -> p g c", g=8),
                                                       in_=dr["ssm_b_im"][l].rearrange("(gp gl) p c -> (gl p) gp c", gl=2)), SK)
            for nm, key in (("ssm_c_re", "CTr"), ("ssm_c_im", "CTi")):
                for t in range(2):
                    qdma(lambda e, l=l, t=t, nm=nm, a=I_[key]: e.dma_start(
                        out=a[:, t * 128:(t + 1) * 128].rearrange("r (o p) -> r o p", o=2),
                        in_=dr[nm][l].rearrange("(t gi) c p -> t (gi c) p", t=2)[t].unsqueeze(1).to_broadcast([128, 2, 64])), SK)
        base_off = off[0]

        def setup_layer(l):
            off[0] = base_off
            off2[0] = 0
            K = "setup"
            SK = "sin%d" % l
            RK = [K, SK, "setupc"]
            I_ = INP[l]
            Bre, Bim = I_["Bre"], I_["Bim"]
            lre = tmp32(8); lim = tmp32(8); lst = tmp32(8); dcol = tmp32(16); lnb_col = tmp32(4)
            pf, pfk = nf()
            P.op("pe", lambda e, pf=pf: e.transpose(pf[:, 0:10], I_["PA"][0:10, :], identf[0:10, 0:10]), reads=RK + ["identf"], writes=[pfk])
            P.op("dve", lambda e, pf=pf: e.tensor_copy(out=lnb_col, in_=pf[:, 0:4]), reads=[pfk], writes=[K])
            P.op("dve", lambda e, pf=pf: e.tensor_scalar_mul(out=lng[l][:], in0=pf[:, 4:8], scalar1=0.5), reads=[pfk], writes=["lng%d" % l])
            P.op("dve", lambda e, pf=pf: e.tensor_scalar_mul(out=glub[l][:], in0=pf[:, 8:10], scalar1=0.5), reads=[pfk], writes=["glub%d" % l])
            pf2, pfk2 = nf()
            P.op("pe", lambda e, pf2=pf2: e.transpose(pf2[:, 0:48], I_["LT"][0:48, :], identf[0:48, 0:48]), reads=RK + ["identf"], writes=[pfk2])
            for gl in range(2):
                rs_ = slice(gl * 64, (gl + 1) * 64)
                P.op("dve", lambda e, pf2=pf2, gl=gl, rs_=rs_: e.tensor_copy(out=lre[rs_, :], in_=pf2[rs_, gl:16:2]), reads=[pfk2], writes=[K])
                P.op("dve", lambda e, pf2=pf2, gl=gl, rs_=rs_: e.tensor_copy(out=lim[rs_, :], in_=pf2[rs_, 16 + gl:32:2]), reads=[pfk2], writes=[K])
                P.op("dve", lambda e, gl=gl, rs_=rs_: e.tensor_copy(out=lst[rs_, :], in_=I_["L16"][rs_, gl:16:2]), reads=RK, writes=[K])
            P.op("dve", lambda e, pf2=pf2: e.tensor_copy(out=dcol, in_=pf2[:, 32:48]), reads=[pfk2], writes=[K])
            wraw = I_["wraw"]
            wmb = x_bf[:, 0, 0:512]
            P.op("dve", lambda e, wraw=wraw, wmb=wmb: e.tensor_tensor(
                out=wmb.rearrange("p (h s) -> p h s", h=4), in0=wraw.rearrange("p (h s) -> p h s", h=4),
                in1=tri.unsqueeze(1).to_broadcast([128, 4, 128]), op=ALU.mult), reads=RK, writes=["x_bf"])
            pb, pbk = nb_()
            for h in range(4):
                P.op("pe", lambda e, h=h, pb=pb, wmb=wmb: e.transpose(pb[:, h * 128:(h + 1) * 128], wmb[:, h * 128:(h + 1) * 128], identb[:]),
                     reads=["x_bf", "identb"], writes=[pbk])
            P.op("dve", lambda e, l=l, pb=pb: e.tensor_copy(out=wTg[l][:].rearrange("p h t -> p (h t)"), in_=pb[:, 0:512]),
                 reads=[pbk], writes=["wTg%d" % l])
            pf, pfk = nf()
            P.op("pe", lambda e, l=l, pf=pf: e.matmul(pf[:, :], lhsT=onesb[:, :], rhs=wTg[l][:].rearrange("p h t -> p (h t)"),
                                                      start=True, stop=True), reads=["onesb", "wTg%d" % l], writes=[pfk])
            for h in range(4):
                P.op("dve", lambda e, l=l, h=h, pf=pf, bs_bc=I_["bs_bc"], lnb_col=lnb_col: e.scalar_tensor_tensor(
                    out=bias2[l][:, h, :], in0=pf[:, h * 128:(h + 1) * 128], scalar=lnb_col[:, h:h + 1], in1=bs_bc[:, h * 128:(h + 1) * 128],
                    op0=ALU.mult, op1=ALU.add), reads=[pfk] + RK, writes=["bias2_%d" % l])
            P.op("dve", lambda e, l=l: e.tensor_scalar_mul(out=bias2[l][:].rearrange("p h t -> p (h t)"), in0=bias2[l][:].rearrange("p h t -> p (h t)"), scalar1=0.5),
                 reads=["bias2_%d" % l], writes=["bias2_%d" % l])

            Cre = tmp32(128); Cim = tmp32(128)
            for key, dst in (("CTr", Cre), ("CTi", Cim)):
                for t in range(2):
                    pfc, pfck = nf()
                    P.op("pe", lambda e, pfc=pfc, key=key, t=t: e.transpose(pfc[:, 0:128], I_[key][:, t * 128:(t + 1) * 128], identf[:]),
                         reads=RK + ["identf"], writes=[pfck])
                    for gl in range(2):
                        rs_ = slice(gl * 64, (gl + 1) * 64)
                        P.op("dve", lambda e, pfc=pfc, gl=gl, rs_=rs_, t=t, dst=dst: e.tensor_copy(
                            out=dst[rs_, :].rearrange("p (g c) -> p g c", g=8)[:, 4 * t:4 * t + 4, :],
                            in_=pfc[rs_, 0:128].rearrange("p (gpl gl2 c) -> p gpl gl2 c", gl2=2, c=16)[:, :, gl, :]), reads=[pfck], writes=[K])
            dt = tmp32(8); ar = tmp32(8); th = tmp32(8)
            P.op("act", lambda e, dt=dt, lst=lst: e.activation(out=dt, in_=lst, func=AF.Exp), reads=RK, writes=[K])
            P.op("dve", lambda e, ar=ar, lre=lre, dt=dt: e.tensor_tensor(out=ar, in0=lre, in1=dt, op=ALU.mult), reads=RK, writes=[K])
            P.op("dve", lambda e, th=th, lim=lim, dt=dt: e.tensor_tensor(out=th, in0=lim, in1=dt, op=ALU.mult), reads=RK, writes=[K])
            NE = 8 * NK
            mag = tmp32(NE); tn = tmp32(NE); tq = tmp32(NE); fr = tmp32(NE); sn = tmp32(NE); cs = tmp32(NE)
            ti = tmp32(NE).bitcast(I32)
            k3 = lambda a: a.rearrange("p (g k) -> p g k", g=8)
            P.op("dve", lambda e: e.tensor_tensor(out=k3(mag), in0=k3(kv3), in1=ar.unsqueeze(2).to_broadcast([128, 8, NK]), op=ALU.mult), reads=RK, writes=[K])
            P.op("act", lambda e: e.activation(out=mag, in_=mag, func=AF.Exp), reads=[K], writes=[K])
            P.op("dve", lambda e: e.scalar_tensor_tensor(out=k3(tn), in0=k3(kv3), scalar=1.0 / TWO_PI, in1=th.unsqueeze(2).to_broadcast([128, 8, NK]),
                                                         op0=ALU.mult, op1=ALU.mult), reads=RK, writes=[K])

            def reduce_turns(src, dst, ti=ti, tq=tq):
                n_ = src.shape[1]
                P.op("dve", lambda e: e.tensor_copy(out=ti[:, 0:n_], in_=src), reads=[K], writes=[K])
                P.op("dve", lambda e: e.tensor_copy(out=tq[:, 0:n_], in_=ti[:, 0:n_]), reads=[K], writes=[K])
                P.op("dve", lambda e: e.tensor_tensor(out=dst, in0=src, in1=tq[:, 0:n_], op=ALU.subtract), reads=[K], writes=[K])

            reduce_turns(tn, fr)
            P.op("act", lambda e: e.activation(out=sn, in_=fr, func=AF.Sin, scale=TWO_PI), reads=[K], writes=[K])
            P.op("dve", lambda e: e.tensor_scalar_add(out=cs, in0=fr, scalar1=0.25), reads=[K], writes=[K])
            reduce_turns(cs, cs)
            P.op("act", lambda e: e.activation(out=cs, in_=cs, func=AF.Sin, scale=TWO_PI), reads=[K], writes=[K])
            pwr = tmp32(NE); pwi = tmp32(NE)
            P.op("dve", lambda e: e.tensor_tensor(out=pwr, in0=mag, in1=cs, op=ALU.mult), reads=[K], writes=[K])
            P.op("dve", lambda e: e.tensor_tensor(out=pwi, in0=mag, in1=sn, op=ALU.mult), reads=[K], writes=[K])
            pwr3 = k3(pwr); pwi3 = k3(pwi)
            xr = tmp32(8); den = tmp32(8); t8a = tmp32(8); t8b = tmp32(8); cr = tmp32(8); ci = tmp32(8)
            yi = pwi3[:, :, 16]
            P.op("dve", lambda e: e.tensor_scalar_add(out=xr, in0=pwr3[:, :, 16], scalar1=-1.0), reads=[K], writes=[K])
            P.op("dve", lambda e: e.tensor_tensor(out=den, in0=lre, in1=lre, op=ALU.mult), reads=RK, writes=[K])
            P.op("dve", lambda e: e.tensor_tensor(out=t8a, in0=lim, in1=lim, op=ALU.mult), reads=RK, writes=[K])
            P.op("dve", lambda e: e.tensor_tensor(out=den, in0=den, in1=t8a, op=ALU.add), reads=[K], writes=[K])
            P.op("dve", lambda e: e.reciprocal(out=den, in_=den), reads=[K], writes=[K])
            P.op("dve", lambda e: e.tensor_tensor(out=t8a, in0=xr, in1=lre, op=ALU.mult), reads=RK, writes=[K])
            P.op("dve", lambda e: e.tensor_tensor(out=t8b, in0=yi, in1=lim, op=ALU.mult), reads=RK, writes=[K])
            P.op("dve", lambda e: e.tensor_tensor(out=t8a, in0=t8a, in1=t8b, op=ALU.add), reads=[K], writes=[K])
            P.op("dve", lambda e: e.tensor_tensor(out=cr, in0=t8a, in1=den, op=ALU.mult), reads=[K], writes=[K])
            P.op("dve", lambda e: e.tensor_tensor(out=t8a, in0=yi, in1=lre, op=ALU.mult), reads=RK, writes=[K])
            P.op("dve", lambda e: e.tensor_tensor(out=t8b, in0=xr, in1=lim, op=ALU.mult), reads=RK, writes=[K])
            P.op("dve", lambda e: e.tensor_tensor(out=t8a, in0=t8a, in1=t8b, op=ALU.subtract), reads=[K], writes=[K])
            P.op("dve", lambda e: e.tensor_tensor(out=ci, in0=t8a, in1=den, op=ALU.mult), reads=[K], writes=[K])
            Bbr = tmp32(128); Bbi = tmp32(128)
            ta = tmpw(1024); tb = tmpw(1024)
            g3 = lambda a: a.rearrange("p (g c) -> p g c", g=8)
            g4 = lambda a: a.rearrange("p (g s c) -> p g s c", g=8, s=8)
            bc8 = lambda a: a.unsqueeze(2).to_broadcast([128, 8, 16])

            def cmul(out_r, out_i, ar_, ai_, br_, bi_, tv, neg_i=False):
                ta_, tb_ = tv(ta), tv(tb)
                P.op("dve", lambda e: e.tensor_tensor(out=ta_, in0=br_, in1=ar_, op=ALU.mult), reads=RK, writes=[K])
                P.op("dve", lambda e: e.tensor_tensor(out=tb_, in0=bi_, in1=ai_, op=ALU.mult), reads=RK, writes=[K])
                P.op("dve", lambda e: e.tensor_tensor(out=out_r, in0=ta_, in1=tb_, op=ALU.subtract), reads=[K], writes=[K])
                P.op("dve", lambda e: e.tensor_tensor(out=ta_, in0=bi_, in1=ar_, op=ALU.mult), reads=RK, writes=[K])
                P.op("dve", lambda e: e.tensor_tensor(out=tb_, in0=br_, in1=ai_, op=ALU.mult), reads=RK, writes=[K])
                if neg_i:
                    P.op("dve", lambda e: e.scalar_tensor_tensor(out=out_i, in0=ta_, scalar=-1.0, in1=tb_, op0=ALU.mult, op1=ALU.subtract),
                         reads=[K], writes=[K])
                else:
                    P.op("dve", lambda e: e.tensor_tensor(out=out_i, in0=ta_, in1=tb_, op=ALU.add), reads=[K], writes=[K])

            cmul(g3(Bbr), g3(Bbi), bc8(cr), bc8(ci), g3(Bre), g3(Bim), lambda a: g3(a[:, 0:128]))
            Ar = x_bf[:, 1, :].rearrange("p (g s c) -> p g s c", g=8, s=8)
            Ai = x_bf[:, 2, :].rearrange("p (g s c) -> p g s c", g=8, s=8)
            Rr = x_bf[:, 3, :].rearrange("p (g s c) -> p g s c", g=8, s=8)
            Rin = x_bf[:, 0, :].rearrange("p (g s c) -> p g s c", g=8, s=8)
            Etr = yT[:, 0:2, :].rearrange("p a n -> p (a n)").rearrange("p (g s c) -> p g s c", g=8, s=8)
            Etin = yT[:, 2:4, :].rearrange("p a n -> p (a n)").rearrange("p (g s c) -> p g s c", g=8, s=8)
            S4 = [128, 8, 8, 16]
            pwb = lambda p3, i0: p3[:, :, i0:i0 + 8].unsqueeze(3).to_broadcast(S4)
            vb = lambda a: g3(a).unsqueeze(2).to_broadcast(S4)
            cmul(Ar, Ai, pwb(pwr3, 0), pwb(pwi3, 0), vb(Bbr), vb(Bbi), g4)
            cmul(Rr, Rin, pwb(pwr3, 8), pwb(pwi3, 8), vb(Cre), vb(Cim), g4, neg_i=True)
            cmul(Etr, Etin, pwb(pwr3, 16), pwb(pwi3, 16), vb(Cre), vb(Cim), g4, neg_i=True)
            Rm = arX[:, 0:4096].rearrange("p (g r n) -> p g r n", g=16, r=2)
            P.op("pool", lambda e: e.memset(Rm, 0.0), reads=[K], writes=["RmZ"])
            P.op("pool", lambda e: e.memset(ez[:], 0.0), reads=[K], writes=["ez"])
            for gl in range(2):
                rs_ = slice(gl * 64, (gl + 1) * 64)
                for r, (srcR, srcE) in enumerate(((Rr, Etr), (Rin, Etin))):
                    P.op("dve", lambda e, rs_=rs_, gl=gl, r=r, srcR=srcR: e.tensor_copy(
                        out=Rm[rs_, gl:16:2, r, :], in_=srcR[rs_].rearrange("p g s c -> p g (s c)")), reads=[K, "RmZ"], writes=[K])
                    P.op("dve", lambda e, rs_=rs_, gl=gl, r=r, srcE=srcE: e.tensor_copy(
                        out=ez[rs_, gl:16:2, r, :], in_=srcE[rs_].rearrange("p g s c -> p g (s c)")), reads=[K], writes=["ez"])
            for gp in range(8):
                pf, pfk = nf()
                for gl in range(2):
                    for r, A_ in enumerate((Ar, Ai)):
                        P.op("pe", lambda e, pf=pf, gp=gp, gl=gl, r=r, A_=A_: e.matmul(
                            pf[:, (gl * 2 + r) * 128:(gl * 2 + r + 1) * 128], lhsT=A_[:, gp].rearrange("p s c -> p (s c)"), rhs=imask[:, gl, :],
                            start=True, stop=True), reads=[K, "imask"], writes=[pfk])
                if gp % 2 == 0:
                    P.op("act", lambda e, pf=pf, gp=gp: e.activation(out=wsz[:, 2 * gp:2 * gp + 2].rearrange("p g r n -> p (g r n)"), in_=pf[:, :], func=AF.Identity),
                         reads=[pfk], writes=["wsz"])
                else:
                    P.op("dve", lambda e, pf=pf, gp=gp: e.tensor_copy(out=wsz[:, 2 * gp:2 * gp + 2].rearrange("p g r n -> p (g r n)"), in_=pf[:, :]),
                         reads=[pfk], writes=["wsz"])
            mtmp = tmp32(512)
            for g4i in range(4):
                pf, pfk = nf()
                for gi in range(4):
                    g = g4i * 4 + gi
                    gp = g // 2
                    P.op("pe", lambda e, pf=pf, gi=gi, g=g, gp=gp: e.matmul(pf[:, gi * 128:(gi + 1) * 128], lhsT=Ar[:, gp].rearrange("p s c -> p (s c)"),
                                                                    rhs=Rm[:, g, 0, :], start=True, stop=False), reads=[K], writes=[pfk])
                    P.op("pe", lambda e, pf=pf, gi=gi, g=g, gp=gp: e.matmul(pf[:, gi * 128:(gi + 1) * 128], lhsT=Ai[:, gp].rearrange("p s c -> p (s c)"),
                                                                    rhs=Rm[:, g, 1, :], start=False, stop=True), reads=[K], writes=[pfk])
                P.op("dve", lambda e, pf=pf: e.tensor_tensor(out=mtmp.rearrange("p (g n) -> p g n", g=4), in0=pf[:, :].rearrange("p (g n) -> p g n", g=4),
                                                             in1=cm[:].unsqueeze(1).to_broadcast([128, 4, 128]), op=ALU.mult), reads=[pfk, "cm", K], writes=[K])
                for gi in range(4):
                    g = g4i * 4 + gi
                    P.op("dve", lambda e, gi=gi, g=g: e.scalar_tensor_tensor(out=mloc[:, g, :], in0=identf[:], scalar=dcol[:, g:g + 1],
                                                                             in1=mtmp[:, gi * 128:(gi + 1) * 128], op0=ALU.mult, op1=ALU.add),
                         reads=RK + ["identf"], writes=["mloc"])
            fr3 = k3(fr)
            tt_ = tmpw(8 * J); tf_ = tmpw(8 * J)
            tib = tmpw(8 * J).bitcast(I32)
            tqb = tmpw(8 * J)
            for gp in range(8):
                P.op("dve", lambda e, gp=gp: e.tensor_scalar(out=tt_[:, gp * J:(gp + 1) * J], in0=jv, scalar1=fr3[:, gp, 23:24], scalar2=None, op0=ALU.mult),
                     reads=RK, writes=[K])

            def reduce_big(src, dst):
                P.op("dve", lambda e: e.tensor_copy(out=tib, in_=src), reads=[K], writes=[K])
                P.op("dve", lambda e: e.tensor_copy(out=tqb, in_=tib), reads=[K], writes=[K])
                P.op("dve", lambda e: e.tensor_tensor(out=dst, in0=src, in1=tqb, op=ALU.subtract), reads=[K], writes=[K])

            reduce_big(tt_, tf_)
            P.op("act", lambda e: e.activation(out=sinT, in_=tf_, func=AF.Sin, scale=TWO_PI), reads=[K], writes=["tabs"])
            P.op("dve", lambda e: e.tensor_scalar_add(out=tt_, in0=tf_, scalar1=0.25), reads=[K], writes=[K])
            reduce_big(tt_, tf_)
            P.op("act", lambda e: e.activation(out=cosT, in_=tf_, func=AF.Sin, scale=TWO_PI), reads=[K], writes=["tabs"])
            P.op("dve", lambda e: e.tensor_copy(out=r8, in_=k3(mag)[:, :, 23]), reads=[K], writes=["tabs"])
            P.op("dve", lambda e: e.tensor_copy(out=decT.rearrange("p (g j) -> p g j", g=8), in_=k3(mag)[:, :, 23:24].to_broadcast([128, 8, J])), reads=[K], writes=["tabs"])
            P.op("dve", lambda e: e.memset(decT.rearrange("p (g j) -> p g j", g=8)[:, :, 0], 0.0), reads=["tabs"], writes=["tabs"])
            P.dma(lambda e, l=l: e.dma_start(out=sc_ws[l], in_=wsz[:].rearrange("p g r n -> p (g r n)")), reads=["wsz"], writes=["scws%d" % l])
            P.dma(lambda e, l=l: e.dma_start(out=sc_ml[l], in_=mloc[:].rearrange("p g n -> p (g n)")), reads=["mloc"], writes=["scml%d" % l])
            P.dma(lambda e, l=l: e.dma_start(out=sc_ez[l], in_=ez[:].rearrange("p g r n -> p (g r n)")), reads=["ez"], writes=["scez%d" % l])
            P.dma(lambda e, l=l: e.dma_start(out=sc_tb[l], in_=tabs[:]), reads=["tabs"], writes=["sctb%d" % l])

        for l in reversed(range(DEPTH)):
            setup_layer(l)

        segs = [(b, sg, l) for b in range(NB) for sg in range(NSEG) for l in range(DEPTH)]
        if n_segs_limit is not None:
            segs = segs[:n_segs_limit]

        def load_wout_ln(l):
            for kq in range(4):
                P.dma(lambda e, l=l, kq=kq: e.dma_start(out=wout[:, 2 * kq:2 * kq + 2, :],
                                                        in_=dr["w_out"][l, kq * 256:(kq + 1) * 256, :].rearrange("(k p) n -> p k n", p=128)),
                      writes=["wout"], q="pool")
            P.dma(lambda e, l=l: e.dma_start(out=lnG[:], in_=dr["ln_g"][l].rearrange("(o n) -> o n", o=1).to_broadcast([128, D])), writes=["lnG"])
            P.dma(lambda e, l=l: e.dma_start(out=lnB[:], in_=dr["ln_b"][l].rearrange("(o n) -> o n", o=1).to_broadcast([128, D])), writes=["lnB"])

        def load_derived(l):
            P.dma(lambda e, l=l: e.dma_start(out=wsz[:].rearrange("p g r n -> p (g r n)"), in_=sc_ws[l]), reads=["scws%d" % l], writes=["wsz"])
            P.dma(lambda e, l=l: e.dma_start(out=mloc[:].rearrange("p g n -> p (g n)"), in_=sc_ml[l]), reads=["scml%d" % l], writes=["mloc"])
            P.dma(lambda e, l=l: e.dma_start(out=ez[:].rearrange("p g r n -> p (g r n)"), in_=sc_ez[l]), reads=["scez%d" % l], writes=["ez"])
            P.dma(lambda e, l=l: e.dma_start(out=tabs[:], in_=sc_tb[l]), reads=["sctb%d" % l], writes=["tabs"])

        def rstd_newton(y, v, t, ti, n, reads, wkey):
            P.op("dve", lambda e: e.tensor_scalar_add(out=v, in0=v, scalar1=EPS), reads=reads, writes=[wkey])
            P.op("dve", lambda e: e.tensor_single_scalar(out=ti, in_=v.bitcast(I32), scalar=1, op=ALU.arith_shift_right), reads=[wkey], writes=[wkey])
            P.op("dve", lambda e: e.tensor_scalar(out=y.bitcast(I32), in0=ti, scalar1=-1.0, scalar2=1597463007.0, op0=ALU.mult, op1=ALU.add), reads=[wkey], writes=[wkey])
            for it in range(3):
                P.op("dve", lambda e: e.tensor_tensor(out=t, in0=y, in1=y, op=ALU.mult), reads=[wkey], writes=[wkey])
                P.op("dve", lambda e: e.tensor_tensor(out=t, in0=t, in1=v, op=ALU.mult), reads=[wkey], writes=[wkey])
                P.op("dve", lambda e: e.tensor_scalar(out=t, in0=t, scalar1=-0.5, scalar2=1.5, op0=ALU.mult, op1=ALU.add), reads=[wkey], writes=[wkey])
                P.op("dve", lambda e: e.tensor_tensor(out=y, in0=y, in1=t, op=ALU.mult), reads=[wkey], writes=[wkey])

        ev = [0]

        def evac(out_ap, in_ap, reads, writes, func=None, eng=None, **kw):
            if func is not None:
                P.op("act", lambda e: e.activation(out=out_ap, in_=in_ap, func=func, **kw), reads=reads, writes=writes)
                return
            if eng is None:
                eng = "act"
            if eng == "act":
                P.op("act", lambda e: e.activation(out=out_ap, in_=in_ap, func=AF.Identity), reads=reads, writes=writes)
            else:
                P.op(eng, lambda e: e.tensor_copy(out=out_ap, in_=in_ap), reads=reads, writes=writes)

        t1, t2, t3, hdr = [x_bf[:, i, :].bitcast(F32) for i in range(4)]
        hdi = hdi_t[:, :]
        KS = "s5t"
        j3 = lambda a: a.rearrange("p (g j) -> p g j", g=8)

        class Stages:
            pass

        def x_load(si, tiles):
            b, sg, l = segs[si]
            tok0 = sg * SEG
            for tt in tiles:
                P.dma(lambda e, b=b, tt=tt, tok0=tok0: e.dma_start(out=x_tok[:, tt, :], in_=dr["x"][b, tok0 + tt * 128: tok0 + (tt + 1) * 128, :]),
                      writes=["x_tok%d" % tt])

        def seg_prologue_x(si):
            b, sg, l = segs[si]
            if l == 0:
                x_load(si, range(4))
            if l == 0 and sg == 0:
                for r in range(2):
                    P.op("dve", lambda e, r=r: e.memset(cH[0][r][:], 0.0), writes=["cH0"])
                    P.op("dve", lambda e, r=r: e.memset(cH[1][r][:], 0.0), writes=["cH1"])

        def seg_prologue(si):
            b, sg, l = segs[si]
            if sg == 0:
                P.dma(lambda e, b=b: e.dma_start(out=mem_bf, in_=dr["mem"][b].rearrange("(t p) d -> p t d", p=128)), writes=["memb"], q="pool")
                for kt in range(8):
                    pb, pbk = nb_()
                    for mt in range(2):
                        P.op("pe", lambda e, pb=pb, kt=kt, mt=mt: e.transpose(pb[:, mt * 128:(mt + 1) * 128], mem_bf[:, mt, kt * 128:(kt + 1) * 128], identb[:]),
                             reads=["memb", "identb"], writes=[pbk])
                    evac(memT[:, kt, :], pb[:, 0:256], [pbk], ["memT"])
                for c2 in range(2):
                    pf, pfk = nf()
                    for kt in range(8):
                        P.op("pe", lambda e, pf=pf, kt=kt, c2=c2: e.matmul(pf[:, 0:256], lhsT=wk[:, kt, c2 * 128:(c2 + 1) * 128], rhs=memT[:, kt, :],
                                                                     start=(kt == 0), stop=(kt == 7)), reads=["wk", "memT"], writes=[pfk])
                    for hh in range(2):
                        rs_ = slice(hh * 64, (hh + 1) * 64)
                        evac(kTm[l][rs_, 2 * c2 + hh, :], pf[rs_, 0:256], [pfk], ["kTm%d" % l])
                for mt in range(2):
                    pf, pfk = nf()
                    for kt in range(8):
                        P.op("pe", lambda e, pf=pf, kt=kt, mt=mt: e.matmul(pf[:, 0:256], lhsT=memT[:, kt, mt * 128:(mt + 1) * 128], rhs=wv[:, kt, :],
                                                                     start=(kt == 0), stop=(kt == 7)), reads=["wv", "memT"], writes=[pfk])
                    for h in range(4):
                        evac(vm[l][:, mt, h, (h % 2) * 64:(h % 2) * 64 + 64], pf[:, h * 64:(h + 1) * 64], [pfk], ["vm%d" % l])
                for sj in range(si + 1, len(segs)):
                    if segs[sj][1] == 0:
                        load_wkv(segs[sj][2])
                        break

        def make_block(si, blk):
            b, sg, l = segs[si]
            tok0 = sg * SEG
            T0 = blk * 4
            last_blk = (blk == NBLK - 1)
            nxt_l = segs[si + 1][2] if si + 1 < len(segs) else None
            S = Stages()
            S.si, S.blk, S.l, S.last_blk = si, blk, l, last_blk

            def prefetch(name):
                if last_blk and nxt_l is not None:
                    load_win_group(nxt_l, name)

            def A0a():
                engs = ["act", "act", "act", "act"]
                for tt in range(4):
                    if engs[tt] == "act":
                        P.op("act", lambda e, tt=tt: e.activation(out=x_bf[:, tt, :], in_=x_tok[:, T0 + tt, :], func=AF.Identity),
                             reads=["x_tok%d" % (T0 + tt)], writes=["x_bf%d" % tt])
                    else:
                        P.op(engs[tt], lambda e, tt=tt: e.tensor_copy(out=x_bf[:, tt, :], in_=x_tok[:, T0 + tt, :]),
                             reads=["x_tok%d" % (T0 + tt)], writes=["x_bf%d" % tt])

            def A0():
                for kt in range(8):
                    pb, pbk = nb_()
                    for tt in range(4):
                        P.op("pe", lambda e, pb=pb, kt=kt, tt=tt: e.transpose(pb[:, tt * 128:(tt + 1) * 128], x_bf[:, tt, kt * 128:(kt + 1) * 128], identb[:]),
                             reads=["x_bf%d" % tt, "identb"], writes=[pbk])
                    evac(xT[:, kt, :], pb[:, 0:512], [pbk], ["xT"])

            def A1(tt):
                pf, pfk = nf()
                for kt in range(8):
                    P.op("pe", lambda e, pf=pf, kt=kt: e.matmul(pf[:, :], lhsT=xT[:, kt, tt * 128:(tt + 1) * 128], rhs=win[:, kt, 512:1024],
                                                          start=(kt == 0), stop=(kt == 7)), reads=["xT", "win_v"], writes=[pfk])
                evac(vn[:, tt, :], pf[:, :], [pfk], ["vn"], func=AF.Gelu)
                v3 = vn[:, tt, :].rearrange("p (h d) -> p h d", h=4)
                P.op("dve", lambda e: e.tensor_reduce(out=lnst[:, 0, tt * 4:(tt + 1) * 4], in_=v3, axis=mybir.AxisListType.X, op=ALU.add),
                     reads=["vn"], writes=["lnst"])
                P.op("dve", lambda e: e.tensor_tensor(out=tmpA, in0=vn[:, tt, :], in1=vn[:, tt, :], op=ALU.mult), reads=["vn"], writes=["tmpA"])
                P.op("dve", lambda e: e.tensor_reduce(out=lnst[:, 1, tt * 4:(tt + 1) * 4], in_=tmpA.rearrange("p (h d) -> p h d", h=4), axis=mybir.AxisListType.X, op=ALU.add),
                     reads=["tmpA"], writes=["lnst"])

            def A1_finish():
                prefetch("v")
                inv = 1.0 / 128.0
                P.op("dve", lambda e: e.tensor_scalar_mul(out=mv[:, :, 0], in0=lnst[:, 0, :], scalar1=inv), reads=["lnst"], writes=["mv"])
                P.op("dve", lambda e: e.tensor_tensor(out=mv[:, :, 1], in0=mv[:, :, 0], in1=mv[:, :, 0], op=ALU.mult), reads=["mv"], writes=["mv"])
                P.op("dve", lambda e: e.scalar_tensor_tensor(out=mv[:, :, 1], in0=lnst[:, 1, :], scalar=inv, in1=mv[:, :, 1], op0=ALU.mult, op1=ALU.subtract),
                     reads=["lnst", "mv"], writes=["mv"])
                P.op("act", lambda e: e.activation(out=rstd[:], in_=mv[:, :, 1], func=AF.Sqrt, bias=epst[:, 0:1], scale=1.0), reads=["mv", "epst"], writes=["rstd"])
                P.op("dve", lambda e: e.reciprocal(out=rstd[:], in_=rstd[:]), reads=["rstd"], writes=["rstd"])
                for tt in range(4):
                    for h in range(4):
                        i = tt * 4 + h
                        P.op("dve", lambda e, tt=tt, h=h, i=i: e.tensor_scalar(out=vn[:, tt, h * 128:(h + 1) * 128], in0=vn[:, tt, h * 128:(h + 1) * 128],
                                                                               scalar1=mv[:, i, 0:1], scalar2=rstd[:, i:i + 1], op0=ALU.subtract, op1=ALU.mult),
                             reads=["vn", "mv", "rstd"], writes=["vn"])

            def inproj(grp, out_ap, wkey, func=None, sub=0):
                col0 = WIN_G[grp][0] + sub
                pf, pfk = nf()
                for kt in range(8):
                    P.op("pe", lambda e, pf=pf, kt=kt, col0=col0: e.matmul(pf[:, :], lhsT=win[:, kt, col0:col0 + 128], rhs=xT[:, kt, :],
                                                                     start=(kt == 0), stop=(kt == 7)), reads=["xT", "win_" + grp], writes=[pfk])
                evac(out_ap, pf[:, :], [pfk], [wkey], func=func)

            def inproj_gate(grp, out_ap, wkey, sub=0):
                col0 = WIN_G[grp][0] + sub
                pf, pfk = nf()
                for kt in range(8):
                    P.op("pe", lambda e, pf=pf, kt=kt, col0=col0: e.matmul(pf[:, :], lhsT=win[:, kt, col0:col0 + 128], rhs=xT[:, kt, :],
                                                                     start=(kt == 0), stop=(kt == 7)), reads=["xT", "win_" + grp], writes=[pfk])
                P.op("act", lambda e, pf=pf: e.activation(out=out_ap, in_=pf[:, :], func=AF.Tanh, scale=0.5), reads=[pfk], writes=[wkey])
                P.op("dve", lambda e, pf=pf: e.scalar_tensor_tensor(out=out_ap, in0=out_ap, scalar=1.0, in1=pf[:, :], op0=ALU.add, op1=ALU.mult),
                     reads=[pfk, wkey], writes=[wkey])

            def Head(h):
                ug_, gs_ = ugs[h % 2]
                uk, gk = "ug", "gs"
                inproj("u%d" % h, ug_, uk, AF.Gelu)
                prefetch("u%d" % h)
                inproj_gate("g%d" % h, gs_, gk)
                prefetch("g%d" % h)
                P.op("dve", lambda e: e.tensor_tensor(out=ug_, in0=ug_, in1=gs_, op=ALU.mult), reads=[uk, gk], writes=[uk])
                pf3, pfk3 = nf()
                for tt in range(4):
                    P.op("pe", lambda e, pf3=pf3, tt=tt: e.matmul(pf3[:, tt * 128:(tt + 1) * 128], lhsT=vn[:, tt, h * 128:(h + 1) * 128], rhs=wTg[l][:, h, :],
                                                            start=True, stop=True), reads=["vn", "wTg%d" % l], writes=[pfk3])
                P.op("dve", lambda e, pf3=pf3: e.scalar_tensor_tensor(
                    out=tmpA.rearrange("p (a t) -> p a t", a=4), in0=pf3[:, :].rearrange("p (a t) -> p a t", a=4), scalar=lng[l][:, h:h + 1],
                    in1=bias2[l][:, h, :].unsqueeze(1).to_broadcast([128, 4, 128]), op0=ALU.mult, op1=ALU.add),
                    reads=[pfk3, "lng%d" % l, "bias2_%d" % l], writes=["tmpA"])
                P.op("dve", lambda e: e.tensor_tensor(out=yT[:, h, :], in0=tmpA, in1=ug_, op=ALU.mult), reads=["tmpA", uk], writes=["yT%d" % h])

            def A3():
                for c in range(2):
                    inproj("xb", xbT[:, c, :], "xbT", sub=c * 128)
                for c in range(2):
                    inproj_gate("gb", gbs[:, c, :], "gbs", sub=c * 128)
                prefetch("xb"); prefetch("gb")

            def Attn(c):
                inproj("q%d" % c, qT, "qT")
                prefetch("q%d" % c)
                inproj_gate("gx%d" % c, gxs, "gxs")
                prefetch("gx%d" % c)
                for hh in range(2):
                    h = 2 * c + hh
                    for mt in range(2):
                        pfs, pfsk = nf()
                        P.op("pe", lambda e, pfs=pfs, h=h, mt=mt: e.matmul(pfs[:, :], lhsT=kTm[l][:, h, mt * 128:(mt + 1) * 128], rhs=qT,
                                                                     start=True, stop=True), reads=["kTm%d" % l, "qT"], writes=[pfsk])
                        evac(pT[:, mt, hh, :], pfs[:, :], [pfsk], ["pT"], func=AF.Exp, scale=0.125)
                pfo, pfok = nf()
                pfd, pfdk = nf()
                n = 0
                for hh in range(2):
                    h = 2 * c + hh
                    for mt in range(2):
                        P.op("pe", lambda e, h=h, mt=mt, hh=hh, n=n: e.matmul(pfo[:, :], lhsT=vm[l][:, mt, h, :], rhs=pT[:, mt, hh, :],
                                                                        start=(n == 0), stop=(n == 3)), reads=["vm%d" % l, "pT"], writes=[pfok])
                        n += 1
                n = 0
                for hh in range(2):
                    for mt in range(2):
                        P.op("pe", lambda e, mt=mt, hh=hh, n=n: e.matmul(pfd[:, :], lhsT=hmask[:, hh, :], rhs=pT[:, mt, hh, :],
                                                                   start=(n == 0), stop=(n == 3)), reads=["hmask", "pT"], writes=[pfdk])
                        n += 1
                P.op("dve", lambda e: e.reciprocal(out=recip, in_=pfd[:, :]), reads=[pfdk], writes=["recip"])
                P.op("dve", lambda e: e.tensor_tensor(out=recip, in0=recip, in1=gxs, op=ALU.mult), reads=["recip", "gxs"], writes=["recip"])
                P.op("dve", lambda e: e.tensor_tensor(out=yT[:, 6 + c, :], in0=pfo[:, :], in1=recip, op=ALU.mult),
                     reads=[pfok, "recip"], writes=["yT%d" % (6 + c)])

            def B1():
                for c in range(2):
                    pf, pfk = nf()
                    for gl in range(8):
                        for s in range(8):
                            P.op("pe", lambda e, pf=pf, gl=gl, s=s, c=c: e.matmul(pf[:, gl * J:(gl + 1) * J], lhsT=zu[:, gl, 112 - 16 * s:240 - 16 * s],
                                                                            rhs=xbT[:, c, s:BLK:8], start=(s == 0), stop=(s == 7)),
                                 reads=["zu", "xbT"], writes=[pfk])
                    evac(Xg[:, c * 8:(c + 1) * 8, :].rearrange("p g j -> p (g j)"), pf[:, :], [pfk], ["Xg"])

            pS = []

            def B2():
                for r in range(2):
                    pf, pfk = nf()
                    pS.append((pf, pfk))
                    for gp in range(8):
                        for gl in range(2):
                            g = 2 * gp + gl
                            P.op("pe", lambda e, pf=pf, g=g, gp=gp, gl=gl, r=r: e.matmul(pf[:, gp * J:(gp + 1) * J], lhsT=wsz[:, g, r, :], rhs=Xg[:, g, :],
                                                                                   start=(gl == 0), stop=(gl == 1)), reads=["wsz", "Xg"], writes=[pfk])

            def B3():
                (pSr, pSrk), (pSi, pSik) = pS
                P.op("dve", lambda e: e.tensor_tensor(out=t1, in0=pSr[:, :], in1=cosT, op=ALU.mult), reads=[pSrk, "tabs"], writes=[KS])
                P.op("dve", lambda e: e.tensor_tensor(out=t3, in0=pSi[:, :], in1=sinT, op=ALU.mult), reads=[pSik, "tabs"], writes=[KS])
                P.op("dve", lambda e: e.tensor_tensor(out=t1, in0=t1, in1=t3, op=ALU.add), reads=[KS], writes=[KS])
                P.op("dve", lambda e: e.tensor_tensor(out=t2, in0=pSi[:, :], in1=cosT, op=ALU.mult), reads=[pSik, "tabs", KS], writes=[KS])
                P.op("dve", lambda e: e.tensor_tensor(out=t3, in0=pSr[:, :], in1=sinT, op=ALU.mult), reads=[pSrk, "tabs", KS], writes=[KS])
                P.op("dve", lambda e: e.tensor_tensor(out=t2, in0=t2, in1=t3, op=ALU.subtract), reads=[KS], writes=[KS])

            def B4():
                for r in range(2):
                    P.op("dve", lambda e, r=r: e.tensor_copy(out=Hb[:, r, :, 0], in_=cH[l][r][:]), reads=["cH%d" % l], writes=["Hb"])
                for r, (src, dst) in enumerate(((t1, hdr), (t2, hdi))):
                    P.op("dve", lambda e, r=r: e.tensor_tensor(out=car[:, r, :], in0=cH[l][r][:], in1=r8, op=ALU.mult), reads=["cH%d" % l, "tabs"], writes=["car"])
                    P.op("dve", lambda e, r=r, src=src: e.tensor_tensor(out=j3(src)[:, :, 0], in0=j3(src)[:, :, 0], in1=car[:, r, :], op=ALU.add), reads=[KS, "car"], writes=[KS])
                    P.op("dve", lambda e, src=src, dst=dst: e.tensor_tensor_scan(out=dst, data0=decT, data1=src, initial=0.0, op0=ALU.mult, op1=ALU.add),
                         reads=[KS, "tabs"], writes=[KS])

            def B5():
                P.op("dve", lambda e: e.tensor_tensor(out=t1, in0=hdr, in1=cosT, op=ALU.mult), reads=[KS, "tabs"], writes=[KS])
                P.op("dve", lambda e: e.tensor_tensor(out=t3, in0=hdi, in1=sinT, op=ALU.mult), reads=[KS, "tabs"], writes=[KS])
                P.op("dve", lambda e: e.tensor_tensor(out=Hb[:, 0, :, 1:J + 1], in0=j3(t1), in1=j3(t3), op=ALU.subtract), reads=[KS], writes=["Hb"])
                P.op("dve", lambda e: e.tensor_tensor(out=cH[l][0][:], in0=j3(t1)[:, :, J - 1], in1=j3(t3)[:, :, J - 1], op=ALU.subtract), reads=[KS], writes=["cH%d" % l])
                P.op("dve", lambda e: e.tensor_tensor(out=t1, in0=hdi, in1=cosT, op=ALU.mult), reads=[KS, "tabs"], writes=[KS])
                P.op("dve", lambda e: e.tensor_tensor(out=t3, in0=hdr, in1=sinT, op=ALU.mult), reads=[KS, "tabs"], writes=[KS])
                P.op("dve", lambda e: e.tensor_tensor(out=Hb[:, 1, :, 1:J + 1], in0=j3(t1), in1=j3(t3), op=ALU.add), reads=[KS], writes=["Hb"])
                P.op("dve", lambda e: e.tensor_tensor(out=cH[l][1][:], in0=j3(t1)[:, :, J - 1], in1=j3(t3)[:, :, J - 1], op=ALU.add), reads=[KS], writes=["cH%d" % l])

            def B6():
                for c in range(2):
                    pf, pfk = nf()
                    for gi in range(8):
                        g = c * 8 + gi
                        gp = g // 2
                        P.op("pe", lambda e, pf=pf, gi=gi, g=g: e.matmul(pf[:, gi * J:(gi + 1) * J], lhsT=mloc[:, g, :], rhs=Xg[:, g, :], start=True, stop=False),
                             reads=["mloc", "Xg"], writes=[pfk])
                        P.op("pe", lambda e, pf=pf, gi=gi, g=g, gp=gp: e.matmul(pf[:, gi * J:(gi + 1) * J], lhsT=ez[:, g, 0, :], rhs=Hb[:, 0, gp, 0:J], start=False, stop=False),
                             reads=["ez", "Hb"], writes=[pfk])
                        P.op("pe", lambda e, pf=pf, gi=gi, g=g, gp=gp: e.matmul(pf[:, gi * J:(gi + 1) * J], lhsT=ez[:, g, 1, :], rhs=Hb[:, 1, gp, 0:J], start=False, stop=True),
                             reads=["ez", "Hb"], writes=[pfk])
                    evac(Yg[:, c * 8:(c + 1) * 8, :].rearrange("p g j -> p (g j)"), pf[:, :], [pfk], ["Yg"], func=AF.Gelu)
                if last_blk and nxt_l is not None:
                    load_derived(nxt_l)

            def B7():
                for c in range(2):
                    pf, pfk = nf()
                    for t in range(8):
                        for gl in range(8):
                            P.op("pe", lambda e, pf=pf, t=t, gl=gl, c=c: e.matmul(pf[:, t * J:(t + 1) * J], lhsT=zu[:, t, 112 - 16 * gl:240 - 16 * gl],
                                                                            rhs=Yg[:, c * 8 + gl, :], start=(gl == 0), stop=(gl == 7)),
                                 reads=["zu", "Yg"], writes=[pfk])
                    evac(ybT[:, c, :].rearrange("p (j t) -> p t j", t=8), pf[:, :].rearrange("p (t j) -> p t j", t=8), [pfk], ["ybT"])

            def B8():
                for co in range(2):
                    pf, pfk = nf()
                    for kc in range(2):
                        P.op("pe", lambda e, pf=pf, kc=kc, co=co: e.matmul(pf[:, :], lhsT=gluw[l][:, kc, co * 128:(co + 1) * 128], rhs=ybT[:, kc, :],
                                                                     start=(kc == 0), stop=(kc == 1)), reads=["gluw%d" % l, "ybT"], writes=[pfk])
                    P.op("act", lambda e, pf=pf, co=co: e.activation(out=sgt[:], in_=pf[:, :], func=AF.Tanh, bias=glub[l][:, co:co + 1], scale=0.5),
                         reads=[pfk, "glub%d" % l], writes=["sg"])
                    P.op("dve", lambda e, co=co: e.scalar_tensor_tensor(out=sgt[:], in0=sgt[:], scalar=1.0, in1=gbs[:, co, :], op0=ALU.add, op1=ALU.mult),
                         reads=["sg", "gbs"], writes=["sg"])
                    P.op("dve", lambda e, co=co: e.scalar_tensor_tensor(out=yT[:, 4 + co, :], in0=ybT[:, co, :], scalar=0.25, in1=sgt[:], op0=ALU.mult, op1=ALU.mult),
                         reads=["sg", "ybT"], writes=["yT%d" % (4 + co)])

            def C(tt):
                T = T0 + tt
                acc, ak = accs[tt % 2]
                for nh in range(2):
                    pf, pfk = nf()
                    for kt in range(8):
                        P.op("pe", lambda e, pf=pf, kt=kt, nh=nh: e.matmul(pf[:, :], lhsT=yT[:, kt, tt * 128:(tt + 1) * 128], rhs=wout[:, kt, nh * 512:(nh + 1) * 512],
                                                                     start=(kt == 0), stop=(kt == 7)), reads=["yT%d" % kt, "wout"], writes=[pfk])
                    P.op("dve", lambda e, pf=pf, nh=nh: e.scalar_tensor_tensor(out=acc[:, nh * 512:(nh + 1) * 512], in0=x_tok[:, T, nh * 512:(nh + 1) * 512], scalar=ALPHA,
                                                                       in1=pf[:, :], op0=ALU.mult, op1=ALU.add),
                         reads=[pfk, "x_tok%d" % T], writes=[ak])
                    P.op("dve", lambda e, nh=nh: e.bn_stats(out=stc[:, tt % 2, nh, :], in_=acc[:, nh * 512:(nh + 1) * 512]), reads=[ak], writes=["stc%d" % (tt % 2)])
                P.op("dve", lambda e: e.bn_aggr(out=mvc[:, tt % 2, :], in_=stc[:, tt % 2].rearrange("p a s -> p (a s)")), reads=["stc%d" % (tt % 2)], writes=["mvc%d" % (tt % 2)])
                rs_ = rsc[:, tt % 2, :]
                rk = "rsc%d" % (tt % 2)
                P.op("act", lambda e: e.activation(out=rs_[:, 0:1], in_=mvc[:, tt % 2, 1:2], func=AF.Sqrt, bias=epst[:, 0:1], scale=1.0), reads=["mvc%d" % (tt % 2), "epst"], writes=[rk])
                P.op("dve", lambda e: e.reciprocal(out=rs_[:, 0:1], in_=rs_[:, 0:1]), reads=[rk], writes=[rk])
                P.op("dve", lambda e: e.scalar_tensor_tensor(out=rs_[:, 1:2], in0=mvc[:, tt % 2, 0:1], scalar=-1.0, in1=rs_[:, 0:1], op0=ALU.mult, op1=ALU.mult),
                     reads=["mvc%d" % (tt % 2), rk], writes=[rk])
                P.op("act", lambda e: e.activation(out=acc, in_=acc, func=AF.Identity, bias=rs_[:, 1:2], scale=rs_[:, 0:1]), reads=[ak, rk], writes=[ak])

            def Cb(tt):
                T = T0 + tt
                acc, ak = accs[tt % 2]
                P.op("dve", lambda e: e.tensor_tensor(out=acc, in0=acc, in1=lnG[:], op=ALU.mult), reads=[ak, "lnG"], writes=[ak])
                P.op("dve", lambda e: e.tensor_tensor(out=x_tok[:, T, :], in0=acc, in1=lnB[:], op=ALU.add), reads=[ak, "lnB"], writes=["x_tok%d" % T])
                if l == DEPTH - 1:
                    P.dma(lambda e: e.dma_start(out=out[b, tok0 + T * 128: tok0 + (T + 1) * 128, :], in_=x_tok[:, T, :]),
                          reads=["x_tok%d" % T])

            S.A0, S.A1, S.A1_finish, S.A3, S.Head, S.Attn = A0, A1, A1_finish, A3, Head, Attn
            S.A0a = A0a
            S.B1, S.B2, S.B3, S.B4, S.B5, S.B6, S.B7, S.B8, S.C = B1, B2, B3, B4, B5, B6, B7, B8, C
            S.Cb = Cb
            return S

        blocks = [(si, blk) for si in range(len(segs)) for blk in range(NBLK)]
        l0 = segs[0][2]
        seg_prologue_x(0)
        seg_prologue(0)
        load_wout_ln(l0)
        cur = make_block(*blocks[0])
        cur.A0a()
        cur.A0()
        prev = None
        for k in range(len(blocks)):
            si, blk = blocks[k]
            nxt_blk = None
            if k + 1 < len(blocks):
                nsi, nblk = blocks[k + 1]
                nxt_blk = make_block(nsi, nblk)
            for tt in range(4):
                cur.A1(tt)
            cur.A1_finish()
            cur.A3(); cur.B1()
            if nxt_blk is not None and nblk == 0:
                seg_prologue(nsi)
            if prev is not None:
                prev.C(0); prev.C(1); prev.Cb(0); prev.C(2); prev.Cb(1); prev.C(3); prev.Cb(2); prev.Cb(3)
            if prev is not None and prev.last_blk:
                load_wout_ln(segs[si][2])
                if segs[si][2] == 0:
                    x_load(si, range(4, 8))
            elif prev is None and segs[si][2] == 0:
                x_load(si, range(4, 8))
            cur.Head(0); cur.B2(); cur.B3(); cur.Head(1); cur.B4(); cur.Head(2); cur.B5()
            cur.Head(3); cur.B6(); cur.B7()
            cur.Attn(0)
            if nxt_blk is not None:
                if nblk == 0:
                    seg_prologue_x(nsi)
                nxt_blk.A0a()
            cur.B8(); cur.Attn(1)
            if nxt_blk is not None:
                nxt_blk.A0()
            prev, cur = cur, nxt_blk
        for tt in range(4):
            prev.C(tt)
            prev.Cb(tt)
        P.emit()
    nc._plan = P
    return nc


_CACHE = {}


def kernel(**inputs):
    x = np.ascontiguousarray(inputs["x"], dtype=np.float32)
    mem = np.ascontiguousarray(inputs["mem"], dtype=np.float32)
    consts = host_consts()
    if "nc" not in _CACHE:
        _CACHE["nc"] = build_program()
    nc = _CACHE["nc"]
    in_maps = []
    for i in range(N_CORES):
        m = {"x": x[NB * i:NB * (i + 1)], "mem": mem[NB * i:NB * (i + 1)]}
        for k in W_SHAPES:
            m[k] = np.ascontiguousarray(inputs[k], dtype=np.float32)
        m.update(consts)
        in_maps.append(m)
    res = run_bass_kernel_spmd(nc, in_maps, core_ids=list(range(N_CORES)))
    outp = np.concatenate([r["out"] for r in res.results], axis=0)
    return outp.astype(np.float32)
```
